# Optimizing a Trainium2 kernel written in Bass

```python
import jax, jax.numpy as jnp
from jax import lax
import numpy as np

D_MODEL = 1024
BATCH = 8
SEQ = 4096
DEPTH = 1

N_META = 16
BLOCK = 128
META_PAD = BLOCK - N_META
HEAD_DIM = 64
SB_HEADS = (D_MODEL // 2) // HEAD_DIM
RWKV_HEADS = (D_MODEL // 2) // HEAD_DIM
SB_WIDTH = SB_HEADS * HEAD_DIM
RWKV_WIDTH = RWKV_HEADS * HEAD_DIM
MIX_WIDTH = SB_WIDTH + RWKV_WIDTH
D_FF = 2816
W_LORA = 32
A_LORA = 32
G_LORA = 96
N_RWKV_COLS = 3 * RWKV_WIDTH + W_LORA + A_LORA + G_LORA
IN_COLS = 3 * SB_WIDTH + N_RWKV_COLS
RMS_EPS = 1e-6
LNX_EPS = 64e-5

kernel_name = "hymba_sb_rwkv7_macaron"


def rms_norm(x, g):
    xf = x.astype(jnp.float32)
    y = xf * lax.rsqrt(jnp.mean(xf * xf, axis=-1, keepdims=True) + RMS_EPS)
    return (y * g.astype(jnp.float32)).astype(x.dtype)


def swiglu(h, w_gate, w_up, w_down):
    return (jax.nn.silu(h @ w_gate) * (h @ w_up)) @ w_down


def stick_breaking_attention(q, k, v):
    B, L, H, Dh = q.shape
    pad = ((0, 0), (META_PAD, 0), (0, 0), (0, 0))
    q, k, v = [jnp.pad(t, pad).transpose(0, 2, 1, 3) for t in (q, k, v)]
    Lp = L + META_PAD
    nblk = Lp // BLOCK
    key_pos = jnp.arange(Lp)
    scale = Dh ** -0.5
    qb = q.reshape(B, H, nblk, BLOCK, Dh).transpose(2, 0, 1, 3, 4)

    def block(args):
        q_blk, i = args
        q_pos = i * BLOCK + jnp.arange(BLOCK)
        z = jnp.einsum('bhqd,bhkd->bhqk', q_blk, k).astype(jnp.float32) * scale
        valid = (key_pos[None, :] < q_pos[:, None]) & (key_pos[None, :] >= META_PAD)
        log_keep = jnp.where(valid, -jax.nn.softplus(z), 0.0)
        log_rest = lax.cumsum(log_keep, axis=3, reverse=True) - log_keep
        attn = jnp.where(valid, jnp.exp(jax.nn.log_sigmoid(z) + log_rest), 0.0)
        return jnp.einsum('bhqk,bhkd->bhqd', attn.astype(v.dtype), v)

    out = lax.map(block, (qb, jnp.arange(nblk)))
    out = out.transpose(1, 0, 3, 2, 4).reshape(B, Lp, H * Dh)
    return out[:, META_PAD:]


def rwkv7_time_mix(p, mu, w0, w_up, a0, a_up, g_up, k_k, k_a, r_k, lnx_w, lnx_b):
    B, L, _ = p.shape
    C, H, N = RWKV_WIDTH, RWKV_HEADS, HEAD_DIM
    p_prev = jnp.pad(p, ((0, 0), (1, 0), (0, 0)))[:, :-1]
    p = p + (p_prev - p) * mu
    r = p[..., :C]
    k = p[..., C:2 * C]
    v = p[..., 2 * C:3 * C]
    xw = p[..., 3 * C:3 * C + W_LORA]
    xa = p[..., 3 * C + W_LORA:3 * C + W_LORA + A_LORA]
    xg = p[..., 3 * C + W_LORA + A_LORA:]
    w = -jax.nn.softplus(-(w0 + jnp.tanh(xw) @ w_up)) - 0.5
    decay = jnp.exp(-jnp.exp(w.astype(jnp.float32)))
    a = jax.nn.sigmoid(a0 + xa @ a_up)
    g = jax.nn.sigmoid(xg) @ g_up
    kk = (k * k_k).astype(jnp.float32).reshape(B, L, H, N)
    kk = kk / jnp.maximum(jnp.sqrt(jnp.sum(kk * kk, axis=-1, keepdims=True)), 1e-12)
    k = k * (1.0 + (a - 1.0) * k_a)

    rh, kh, vh, ah, dh = [t.astype(jnp.float32).reshape(B, L, H, N) for t in (r, k, v, a, decay)]

    def step(S, inp):
        r_t, w_t, k_t, v_t, kk_t, a_t = inp
        sa = jnp.einsum('bhij,bhj->bhi', S, -kk_t)
        S = (S * w_t[:, :, None, :] + sa[..., None] * (kk_t * a_t)[:, :, None, :]
             + v_t[..., None] * k_t[:, :, None, :])
        return S, jnp.einsum('bhij,bhj->bhi', S, r_t)

    xs = tuple(t.transpose(1, 0, 2, 3) for t in (rh, dh, kh, vh, kk, ah))
    S0 = jnp.zeros((B, H, N, N), jnp.float32)
    _, ys = lax.scan(step, S0, xs)
    y = ys.transpose(1, 0, 2, 3)
    mean = jnp.mean(y, axis=-1, keepdims=True)
    var = jnp.mean(jnp.square(y - mean), axis=-1, keepdims=True)
    y = ((y - mean) * lax.rsqrt(var + LNX_EPS)).reshape(B, L, C)
    y = y * lnx_w.astype(jnp.float32) + lnx_b.astype(jnp.float32)
    bonus = jnp.sum(rh * kh * r_k.astype(jnp.float32), axis=-1, keepdims=True) * vh
    y = y + bonus.reshape(B, L, C)
    return (y * g.astype(jnp.float32)).astype(p.dtype)


def setup_inputs(seed: int = 0) -> dict:
    key = jax.random.key(seed)
    ks = jax.random.split(key, 32)
    nrm = lambda k, s: jax.random.normal(k, s, jnp.float32)
    gain = lambda k, s: 1.0 + 0.02 * nrm(k, s)
    Dd = DEPTH
    return {
        "x": nrm(ks[0], (BATCH, SEQ, D_MODEL)),
        "meta_tokens": nrm(ks[1], (N_META, D_MODEL)),
        "ffn1_norm": gain(ks[2], (Dd, D_MODEL)),
        "ffn1_w_gate": nrm(ks[3], (Dd, D_MODEL, D_FF)) * D_MODEL ** -0.5,
        "ffn1_w_up": nrm(ks[4], (Dd, D_MODEL, D_FF)) * D_MODEL ** -0.5,
        "ffn1_w_down": nrm(ks[5], (Dd, D_FF, D_MODEL)) * D_FF ** -0.5,
        "mix_norm": gain(ks[6], (Dd, D_MODEL)),
        "w_in": nrm(ks[7], (Dd, D_MODEL, IN_COLS)) * D_MODEL ** -0.5,
        "rwkv_mu": jax.random.uniform(ks[8], (Dd, N_RWKV_COLS), jnp.float32),
        "rwkv_w0": jax.random.uniform(ks[9], (Dd, RWKV_WIDTH), jnp.float32, -4.0, 1.0),
        "rwkv_w_up": nrm(ks[10], (Dd, W_LORA, RWKV_WIDTH)) * 0.5 * W_LORA ** -0.5,
        "rwkv_a0": 0.5 * nrm(ks[11], (Dd, RWKV_WIDTH)),
        "rwkv_a_up": nrm(ks[12], (Dd, A_LORA, RWKV_WIDTH)) * 0.5 * A_LORA ** -0.5,
        "rwkv_g_up": nrm(ks[13], (Dd, G_LORA, RWKV_WIDTH)) * G_LORA ** -0.5,
        "rwkv_k_k": 0.85 + 0.05 * nrm(ks[14], (Dd, RWKV_WIDTH)),
        "rwkv_k_a": 1.0 + 0.05 * nrm(ks[15], (Dd, RWKV_WIDTH)),
        "rwkv_r_k": 0.1 * nrm(ks[16], (Dd, RWKV_HEADS, HEAD_DIM)),
        "rwkv_lnx_w": gain(ks[17], (Dd, RWKV_WIDTH)),
        "rwkv_lnx_b": 0.02 * nrm(ks[18], (Dd, RWKV_WIDTH)),
        "w_out": nrm(ks[19], (Dd, MIX_WIDTH, D_MODEL)) * MIX_WIDTH ** -0.5,
        "ffn2_norm": gain(ks[20], (Dd, D_MODEL)),
        "ffn2_w_gate": nrm(ks[21], (Dd, D_MODEL, D_FF)) * D_MODEL ** -0.5,
        "ffn2_w_up": nrm(ks[22], (Dd, D_MODEL, D_FF)) * D_MODEL ** -0.5,
        "ffn2_w_down": nrm(ks[23], (Dd, D_FF, D_MODEL)) * D_FF ** -0.5,
        "final_norm": gain(ks[24], (D_MODEL,)),
    }


def reference(x, meta_tokens, ffn1_norm, ffn1_w_gate, ffn1_w_up, ffn1_w_down, mix_norm, w_in,
              rwkv_mu, rwkv_w0, rwkv_w_up, rwkv_a0, rwkv_a_up, rwkv_g_up, rwkv_k_k, rwkv_k_a,
              rwkv_r_k, rwkv_lnx_w, rwkv_lnx_b, w_out, ffn2_norm, ffn2_w_gate, ffn2_w_up,
              ffn2_w_down, final_norm):
    B = x.shape[0]
    meta = jnp.broadcast_to(meta_tokens[None].astype(x.dtype), (B, N_META, D_MODEL))
    h = jnp.concatenate([meta, x], axis=1)
    for l in range(DEPTH):
        n = rms_norm(h, ffn1_norm[l])
        h = h + 0.5 * swiglu(n, ffn1_w_gate[l], ffn1_w_up[l], ffn1_w_down[l])
        n = rms_norm(h, mix_norm[l])
        proj = n @ w_in[l]
        Bn, L, _ = proj.shape
        q = proj[..., :SB_WIDTH].reshape(Bn, L, SB_HEADS, HEAD_DIM)
        k = proj[..., SB_WIDTH:2 * SB_WIDTH].reshape(Bn, L, SB_HEADS, HEAD_DIM)
        v = proj[..., 2 * SB_WIDTH:3 * SB_WIDTH].reshape(Bn, L, SB_HEADS, HEAD_DIM)
        sb_out = stick_breaking_attention(q, k, v)
        rw_out = rwkv7_time_mix(proj[..., 3 * SB_WIDTH:], rwkv_mu[l], rwkv_w0[l], rwkv_w_up[l],
                                rwkv_a0[l], rwkv_a_up[l], rwkv_g_up[l], rwkv_k_k[l], rwkv_k_a[l],
                                rwkv_r_k[l], rwkv_lnx_w[l], rwkv_lnx_b[l])
        h = h + jnp.concatenate([sb_out, rw_out], axis=-1) @ w_out[l]
        n = rms_norm(h, ffn2_norm[l])
        h = h + 0.5 * swiglu(n, ffn2_w_gate[l], ffn2_w_up[l], ffn2_w_down[l])
    return rms_norm(h, final_norm)[:, N_META:]
```

```python
import numpy as np
from contextlib import ExitStack
import concourse.bass as bass
import concourse.mybir as mybir
from concourse.bass_utils import run_bass_kernel_spmd
from concourse.ap import AP

F32 = mybir.dt.float32
BF16 = mybir.dt.bfloat16
AF = mybir.ActivationFunctionType
ALU = mybir.AluOpType
AX = mybir.AxisListType

D = 1024
DFF = 2816
NM = DFF // 128
SEQ = 4096
NTILES_FULL = 33
PE_ELEMS = 3072
NSLOT = 4
INV_DT = F32
DECAY_C = float(np.exp(-0.5))
LNX_EPS = 64e-5
RMS_EPS = 1e-6

C_MU_R, C_MU_K, C_MU_V = 0, 4, 8
C_MU_XW, C_MU_XA, C_MU_XG = 12, 13, 14
C_W0, C_A0, C_KK, C_KA, C_RK, C_LW, C_LB = 15, 19, 23, 27, 31, 35, 39
C_G1, C_GM, C_G2 = 43, 51, 59
NCOLS = 67
K_ID, K_BD, K_M1, K_M2, K_MT, K_MP, K_NMT, K_NMP = 0, 128, 256, 384, 512, 640, 768, 896
NCONST = 1024


class Buf:
    def __init__(self, name):
        self.name = name
        self.w = None
        self.r = {}


class E:
    def __init__(self, name, eng, sem):
        self.name, self.eng, self.sem = name, eng, sem
        self.count = 0
        self.waited = {}
        self.prog = []


class DSem:
    def __init__(self, name, sem):
        self.name, self.sem = name, sem
        self.count = 0


class FW:
    def __init__(self, nc, sems):
        self.nc = nc
        self.pe = E("pe", nc.tensor, sems["pe"])
        self.act = E("act", nc.scalar, sems["act"])
        self.dve = E("dve", nc.vector, sems["dve"])
        self.pool = E("pool", nc.gpsimd, sems["pool"])
        self.sp = E("sp", nc.sync, sems["sp"])
        self.nops = 0

    def _deps(self, reads, writes):
        deps = []
        for b in reads:
            if b.w is not None:
                deps.append(b.w)
        for b in writes:
            if b.w is not None:
                deps.append(b.w)
            deps.extend(b.r.values())
        return deps

    def _wait(self, e, deps):
        best = {}
        for (src, c) in deps:
            if src is e:
                if e.name == "pe":
                    continue
            if e.waited.get(src.name, 0) >= c:
                continue
            if best.get(src.name, (None, 0))[1] < c:
                best[src.name] = (src, c)
        for nm, (src, c) in best.items():
            e.prog.append(lambda src=src, c=c, e=e: e.eng.wait_ge(src.sem, c))
            e.waited[nm] = c

    def _mark(self, me, reads, writes):
        for b in writes:
            b.w = me
            b.r = {}
        for b in reads:
            if b.w is not None and b.w == me:
                continue
            old = b.r.get(me[0].name)
            if old is None or old[1] < me[1]:
                b.r[me[0].name] = me

    def op(self, e, fn, reads=(), writes=()):
        self._wait(e, self._deps(reads, writes))
        e.count += 1
        e.prog.append(lambda fn=fn, e=e: fn().then_inc(e.sem, 1))
        self._mark((e, e.count), reads, writes)
        self.nops += 1

    def dma(self, q, dsem, out, in_, reads=(), writes=()):
        self._wait(q, self._deps(reads, writes))
        q.prog.append(lambda q=q, out=out, in_=in_, dsem=dsem: q.eng.dma_start(out=out, in_=in_).then_inc(dsem.sem, 16))
        dsem.count += 16
        self._mark((dsem, dsem.count), reads, writes)

    def handoff(self, srcs, dsts):
        for d in dsts:
            for s in srcs:
                if s.w is not None:
                    o = d.r.get(s.w[0].name)
                    if o is None or o[1] < s.w[1]:
                        d.r[s.w[0].name] = s.w
                for nm, v in s.r.items():
                    o = d.r.get(nm)
                    if o is None or o[1] < v[1]:
                        d.r[nm] = v

    def run(self, block):
        def mk(e):
            def body(eng):
                for f in e.prog:
                    f()
            return body
        block.tensor(mk(self.pe))
        block.scalar(mk(self.act))
        block.vector(mk(self.dve))
        block.gpsimd(mk(self.pool))
        block.sync(mk(self.sp))


def rev(ap_, n):
    pat = [list(x) for x in ap_.ap]
    step = pat[-1][0]
    pat[-1] = [-step, n]
    return AP(ap_.tensor, ap_.offset + (n - 1) * step, pat)


def piece_table():
    pieces = []
    for f in (1, 2):
        for m in range(NM):
            pieces.append([("g", f, m), ("u", f, m), ("d", f, m)])
    chunks = [("in", c) for c in range(27)]
    for i in range(0, 27, 3):
        pieces.append(chunks[i:i + 3])
    oc = [("o", c) for c in range(8)]
    for i in range(0, 8, 3):
        pieces.append(oc[i:i + 3])
    return pieces


PIECES = piece_table()
NPIECES = len(PIECES)
P_FFN = {1: 0, 2: NM}
P_IN = 2 * NM
P_OUT = 2 * NM + 9

IN_CHUNKS = []
for i in range(4):
    IN_CHUNKS.append((i * 128, 128))
for i in range(4):
    IN_CHUNKS.append((512 + i * 128, 128))
for i in range(4):
    IN_CHUNKS.append((1536 + i * 128, 128))
for i in range(4):
    IN_CHUNKS.append((2048 + i * 128, 128))
for i in range(4):
    IN_CHUNKS.append((2560 + i * 128, 128))
IN_CHUNKS.append((3072, 32))
IN_CHUNKS.append((3104, 32))
IN_CHUNKS.append((3136, 96))
for i in range(4):
    IN_CHUNKS.append((1024 + i * 128, 128))


def host_pieces(inp):
    wp = np.zeros((NPIECES, 128, PE_ELEMS), np.float32)

    def kc(wcols):
        c = wcols.shape[1]
        t = np.zeros((8, 128, 128), np.float32)
        t[:, :, :c] = wcols.reshape(8, 128, c)
        return t.transpose(1, 0, 2).reshape(128, 1024)

    for pi, piece in enumerate(PIECES):
        for si, sub in enumerate(piece):
            kind = sub[0]
            if kind == "g":
                blk = kc(inp[f"ffn{sub[1]}_w_gate"][0][:, sub[2] * 128:(sub[2] + 1) * 128])
            elif kind == "u":
                blk = kc(inp[f"ffn{sub[1]}_w_up"][0][:, sub[2] * 128:(sub[2] + 1) * 128])
            elif kind == "d":
                blk = inp[f"ffn{sub[1]}_w_down"][0][sub[2] * 128:(sub[2] + 1) * 128, :]
            elif kind == "in":
                s, w = IN_CHUNKS[sub[1]]
                blk = kc(inp["w_in"][0][:, s:s + w])
            else:
                blk = inp["w_out"][0][sub[1] * 128:(sub[1] + 1) * 128, :]
            wp[pi, :, si * 1024:(si + 1) * 1024] = blk
    return wp


def host_cols(inp):
    cols = np.zeros((128, NCOLS), np.float32)
    mu = inp["rwkv_mu"][0]

    def c4(v):
        return np.asarray(v).reshape(4, 128).T

    cols[:, C_MU_R:C_MU_R + 4] = c4(mu[0:512])
    cols[:, C_MU_K:C_MU_K + 4] = c4(mu[512:1024])
    cols[:, C_MU_V:C_MU_V + 4] = c4(mu[1024:1536])
    cols[0:32, C_MU_XW] = mu[1536:1568]
    cols[0:32, C_MU_XA] = mu[1568:1600]
    cols[0:96, C_MU_XG] = mu[1600:1696]
    cols[:, C_W0:C_W0 + 4] = c4(inp["rwkv_w0"][0])
    cols[:, C_A0:C_A0 + 4] = c4(inp["rwkv_a0"][0])
    cols[:, C_KK:C_KK + 4] = c4(inp["rwkv_k_k"][0])
    cols[:, C_KA:C_KA + 4] = c4(inp["rwkv_k_a"][0])
    cols[:, C_RK:C_RK + 4] = c4(inp["rwkv_r_k"][0].reshape(512))
    cols[:, C_LW:C_LW + 4] = c4(inp["rwkv_lnx_w"][0])
    cols[:, C_LB:C_LB + 4] = c4(inp["rwkv_lnx_b"][0])
    cols[:, C_G1:C_G1 + 8] = inp["ffn1_norm"][0].reshape(8, 128).T
    cols[:, C_GM:C_GM + 8] = inp["mix_norm"][0].reshape(8, 128).T
    cols[:, C_G2:C_G2 + 8] = inp["ffn2_norm"][0].reshape(8, 128).T
    return cols


def host_consts():
    k = np.zeros((128, NCONST), np.float32)
    i = np.arange(128)
    k[:, K_ID:K_ID + 128] = np.eye(128)
    k[:, K_BD:K_BD + 128] = (i[:, None] // 64 == i[None, :] // 64)
    k[:, K_M1:K_M1 + 128] = (i[:, None] < i[None, :])
    k[:, K_M2:K_M2 + 128] = (i[:, None] <= i[None, :])
    k[:, K_MT:K_MT + 128] = (i[None, :] < i[:, None])
    k[:, K_MP:K_MP + 128] = (i[None, :] >= 112) * np.ones((128, 1))
    k[:, K_NMT:K_NMT + 128] = 1.0 - k[:, K_MT:K_MT + 128]
    k[:, K_NMP:K_NMP + 128] = 1.0 - k[:, K_MP:K_MP + 128]
    return k


def build(ntiles=NTILES_FULL, debug=False):
    nc = bass.Bass("TRN2", target_bir_lowering=False)
    LP = ntiles * 128
    x_d = nc.dram_tensor("x", [SEQ, D], F32, kind="ExternalInput").ap()
    meta_d = nc.dram_tensor("meta", [16, D], F32, kind="ExternalInput").ap()
    wp_d = nc.dram_tensor("wp", [NPIECES, 128, PE_ELEMS], F32, kind="ExternalInput").ap()
    cols_d = nc.dram_tensor("cols", [128, NCOLS], F32, kind="ExternalInput").ap()
    consts_d = nc.dram_tensor("consts", [128, NCONST], F32, kind="ExternalInput").ap()
    gfin_d = nc.dram_tensor("gfin", [128, D], F32, kind="ExternalInput").ap()
    lora_d = nc.dram_tensor("lora", [96, 3, 512], F32, kind="ExternalInput").ap()
    out_d = nc.dram_tensor("out", [SEQ, D], F32, kind="ExternalOutput").ap()
    wbf_d = nc.dram_tensor("wbf", [NPIECES, 128, PE_ELEMS], BF16, kind="Internal").ap()
    if debug:
        dbg_d = nc.dram_tensor("dbg", [8, 128, NTILES_FULL * 128], BF16, kind="ExternalOutput").ap()

    with ExitStack() as st:
        def sb(name, shape, dt):
            return st.enter_context(nc.sbuf_tensor("s_" + name, shape, dt))

        def ps(name, shape, dt):
            return st.enter_context(nc.psum_tensor("p_" + name, shape, dt))

        sems = {k: st.enter_context(nc.semaphore("m_" + k)) for k in ["pe", "act", "dve", "pool", "sp"]}

        def dsem(name):
            return DSem(name, st.enter_context(nc.semaphore(name)))

        fw = FW(nc, sems)
        pe, act, dve, pool, sp = fw.pe, fw.act, fw.dve, fw.pool, fw.sp
        V, S, G, T = nc.vector, nc.scalar, nc.gpsimd, nc.tensor

        def acopy(out, in_, scale=1.0):
            return S.activation(out=out, in_=in_, func=AF.Copy, scale=scale)

        cols = sb("cols", [128, NCOLS], F32)
        consts = sb("consts", [128, NCONST], F32)
        gfin = sb("gfin", [128, D], F32)
        lora = sb("lora", [96, 3, 512], BF16)
        identb = sb("identb", [128, 128], BF16)
        bdones = sb("bdones", [128, 128], BF16)
        ones_f = sb("ones_f", [128, 128], F32)
        zeros_f = sb("zeros_f", [128, 512], BF16)
        KT = sb("KT", [128, 4, NTILES_FULL * 128], BF16)
        KTf = KT[:].rearrange("p a b -> p (a b)").bitcast(F32)
        VS = sb("VS", [128, NTILES_FULL, 512], BF16)
        VSf = VS[:].rearrange("p a b -> p (a b)").bitcast(F32)
        ring = [sb(f"ring{i}", [128, PE_ELEMS], BF16) for i in range(NSLOT)]
        stage = [VSf[:, i * PE_ELEMS:(i + 1) * PE_ELEMS] for i in range(2)] + [KTf[:, i * PE_ELEMS:(i + 1) * PE_ELEMS] for i in range(2)]

        class QV:
            def __init__(self, ap_):
                self.ap_ = ap_

            def __getitem__(self, idx):
                p_, q_, c_ = idx
                return self.ap_[p_, c_]
        B_cols, B_consts, B_gfin, B_lora = Buf("cols"), Buf("consts"), Buf("gfin"), Buf("lora")
        B_identb, B_bdones, B_ones, B_zeros = Buf("identb"), Buf("bdones"), Buf("ones"), Buf("zeros")
        B_KT, B_VS = Buf("KT"), Buf("VS")
        B_ring = [Buf(f"ring{i}") for i in range(NSLOT)]
        B_stage = [Buf(f"stage{i}") for i in range(4)]
        d_ring = [dsem(f"dring{i}") for i in range(NSLOT)]
        d_stage = [dsem(f"dstage{i}") for i in range(4)]
        d_cst = [dsem(f"dcst{i}") for i in range(4)]
        d_cvt = [dsem(f"dcvt{i}") for i in range(NSLOT)]
        d_x = [dsem("dx0"), dsem("dx1")]
        d_out = [dsem("dout0"), dsem("dout1")]
        d_dbg = dsem("ddbg")

        NT = 2
        TT = NT * 128
        hbuf = [sb("h0", [128, NT, D], F32)] * 2
        B_h = [[Buf(f"h_{j}") for j in range(NT)]] * 2
        nb = [sb("nb0", [128, D], BF16)] * 2
        B_nb = [Buf("nb0")] * 2
        nT = sb("nT", [128, 8, TT], BF16)
        B_nT = Buf("nT")
        actT = sb("actT", [128, NM, TT], BF16)
        B_actT = [Buf(f"actT{m}") for m in range(NM)]
        RWT = actT[:].rearrange("p m t -> p (m t)").bitcast(F32)
        sig_t = [sb("sig0", [128, TT], BF16)] * 2
        B_sig = [Buf("sig0")] * 2
        stat = sb("stat", [128, 64], F32)
        B_stat = Buf("stat")
        QT = sb("QT", [128, 4, TT], BF16)
        B_QT = Buf("QT")
        mixT = sb("mixT", [128, 8, TT], BF16)
        B_mixT = [Buf(f"mixT{c}") for c in range(8)]
        carry = sb("carry", [128, 15], F32)
        B_carry = Buf("carry")
        raw = [sb("raw0", [128, TT + 1], F32)] * 2
        B_raw = [Buf("raw0")] * 2
        tmpf = [sb(f"tmpf{i}", [128, TT], F32) for i in range(4)]
        B_tmpf = [Buf(f"tmpf{i}") for i in range(4)]
        MIX = sb("MIX", [128, 4, TT], F32)
        MIXR = sb("MIXR", [128, 4, TT], BF16)
        MIXV = sb("MIXV", [128, 4, TT], BF16)
        B_MIX = [Buf(f"MIX{i}") for i in range(12)]
        xw_t = sb("xw_t", [32, TT], BF16)
        xa_t = sb("xa_t", [32, TT], BF16)
        xg_t = sb("xg_t", [96, TT], BF16)
        B_xw, B_xa, B_xg = Buf("xw"), Buf("xa"), Buf("xg")
        A_T = QV(RWT[:, 0*256:1*256])
        B_A = [Buf("A")] * 4
        SG = QV(RWT[:, 1*256:2*256])
        B_SG = [Buf("SG")] * 4
        CS = QV(RWT[:, 2*256:3*256])
        B_CS = [Buf("CS")] * 4
        KKT = QV(RWT[:, 3*256:4*256])
        B_KKT = [Buf("KKT")] * 4
        KTL = QV(RWT[:, 4*256:5*256])
        B_KTL = [Buf("KTL")] * 4
        GG = RWT[:, 7*256:7*256+1024].rearrange("p (q t) -> p q t", q=4)
        GI = QV(RWT[:, 5*256:6*256])
        GP = QV(RWT[:, 6*256:7*256])
        B_GG = [Buf(f"GG{i}") for i in range(4)]
        B_GI = [Buf("GI")] * 4
        B_GP = [Buf("GP")] * 4
        UNI = sb("UNI", [128, 7680], F32)
        UNIb = UNI[:].bitcast(BF16)
        OPS = UNIb[:, 0:6144].rearrange("p (q j s t) -> p q j s t", q=4, j=NT, s=6)
        B_OPS = [[Buf(f"OPS{q}_{i}") for i in range(NT)] for q in range(4)]
        TOK = sb("TOK", [128, 2, 128], BF16)
        B_TOK = Buf("TOK")
        VTOK = sb("VTOK", [128, NT, 512], BF16)
        B_VTOK = [Buf(f"VTOK{i}") for i in range(NT)]
        BONT = sb("BONT", [128, 4, TT], BF16)
        B_BONT = [Buf(f"BONT{i}") for i in range(4)]
        GT = sb("GT", [128, 4, TT], BF16)
        B_GT = [Buf(f"GT{i}") for i in range(4)]
        AM = UNIb[:, 6144:10240].rearrange("p (a h b t) -> p a h b t", a=4, h=2, b=2)
        B_AM = [[Buf(f"AM{s_}_{h}") for h in range(2)] for s_ in range(4)]
        NCH = UNIb[:, 10240:14336].rearrange("p (s b h a t) -> p s b h a t", s=4, b=2, h=2, a=2)
        B_NCH = [[Buf(f"NCH{s_}_{b}") for b in range(2)] for s_ in range(4)]
        XCH = UNIb[:, 14336:15360].rearrange("p (s b t) -> p s b t", s=4, b=2)
        B_XCH = [[Buf(f"XCH{s_}_{b}") for b in range(2)] for s_ in range(4)]
        Pbf = sb("Pbf", [128, 4, 128], BF16)
        B_Pbf = [Buf(f"Pbf{i}") for i in range(4)]

        S32 = sb("S32", [128, 4, 128], F32)
        SBD = sb("SBD", [128, 4, 128], BF16)
        B_S32 = [Buf(f"S32_{q}") for q in range(4)]
        B_SBD = [Buf(f"SBD_{q}") for q in range(4)]
        YNB = sb("YNB", [128, 512], BF16)
        B_YNB = Buf("YNB")
        identf = consts[:, K_ID:K_ID + 128]
        bb_t = sb("bb", [128, 1024], F32)
        beta = [bb_t[:, 0:512], bb_t[:, 512:1024]]
        ybuf = [bb_t, bb_t]
        om = [sb(f"om{i}", [128, 512], F32)[:] for i in range(2)]
        RB = [sb(f"RB{i}", [128, 513], F32)[:] for i in range(2)]
        attn = [sb(f"attn{i}", [128, 512], BF16)[:] for i in range(2)]
        attnT = [sb(f"attnT{i}", [128, 4, 128], BF16)[:] for i in range(2)]
        NSETS = 6
        for k_ in range(NSETS - 2):
            base = k_ * 1538
            om.append(UNI[:, base:base + 512])
            RB.append(UNI[:, base + 512:base + 1025])
            attn.append(UNIb[:, 2 * (base + 1026):2 * (base + 1026) + 512])
            attnT.append(UNIb[:, 2 * (base + 1282):2 * (base + 1282) + 512].rearrange("p (b t) -> p b t", b=4))
        B_beta = [Buf(f"beta{i}") for i in range(5)]
        YN = beta[1]
        B_YN = B_beta[1]
        B_om = [Buf(f"om{i}") for i in range(6)]
        B_RB = [Buf(f"RB{i}") for i in range(6)]
        B_attn = [Buf(f"attn{i}") for i in range(6)]
        B_attnT = [Buf(f"attnT{i}") for i in range(6)]

        PS = [ps(f"ps{i}", [128, 512], F32) for i in range(8)]
        B_PS = [Buf(f"ps{i}") for i in range(8)]
        RG_ = {"mm1": B_PS[0], "mm3": B_PS[0], "mm2": B_PS[1], "state": B_PS[1], "Y": B_PS[2], "X0": B_PS[3], "X1": B_PS[3], "tok": B_PS[3],
               "ch0": B_PS[4], "ch1": B_PS[5], "sq0": B_PS[6], "sq1": B_PS[7]}

        def psb(i):
            return PS[i][:].bitcast(BF16)

        block = st.enter_context(nc.Block())

        fw.dma(sp, d_cst[0], cols[:], cols_d[:, :], writes=[B_cols])
        fw.dma(sp, d_cst[1], consts[:], consts_d[:, :], writes=[B_consts])
        fw.dma(sp, d_cst[2], gfin[:], gfin_d[:, :], writes=[B_gfin])
        fw.op(dve, lambda: V.tensor_copy(out=identb[:], in_=consts[:, K_ID:K_ID + 128]), reads=[B_consts], writes=[B_identb])
        fw.op(dve, lambda: V.tensor_copy(out=bdones[:], in_=consts[:, K_BD:K_BD + 128]), reads=[B_consts], writes=[B_bdones])
        for li in range(3):
            fw.dma(sp, d_cst[3], bb_t[0:96, 0:512], lora_d[:, li, :], writes=[B_beta[0]])
            fw.op(dve, lambda li=li: V.tensor_copy(out=lora[:, li, :], in_=bb_t[0:96, 0:512]), reads=[B_beta[0]], writes=[B_lora])
        fw.op(pool, lambda: G.memset(ones_f[:], 1.0), writes=[B_ones])
        fw.op(pool, lambda: G.memset(zeros_f[:], 0.0), writes=[B_zeros])
        fw.op(pool, lambda: G.memset(S32[:], 0.0), writes=B_S32)
        fw.op(pool, lambda: G.memset(SBD[:], 0.0), writes=B_SBD)
        fw.op(pool, lambda: G.memset(carry[:], 0.0), writes=[B_carry])

        cast_engs = [dve, act, pool]
        for pi, piece in enumerate(PIECES):
            sbuf_i = pi % 4
            fw.dma(sp, d_stage[sbuf_i], stage[sbuf_i], wp_d[pi, :, :], writes=[B_stage[sbuf_i]])
            slot = pi % NSLOT
            for si, sub in enumerate(piece):
                kind = sub[0]
                src = stage[sbuf_i][:, si * 1024:(si + 1) * 1024]
                dst = ring[slot][:, si * 1024:(si + 1) * 1024]
                gc = None
                if kind in ("g", "u"):
                    gc = C_G1 if sub[1] == 1 else C_G2
                elif kind == "in":
                    gc = C_GM
                if gc is not None:
                    fw.op(dve, lambda src=src, dst=dst, gc=gc: V.tensor_tensor(
                        out=dst.rearrange("p (k c) -> p k c", k=8), in0=src.rearrange("p (k c) -> p k c", k=8),
                        in1=cols[:, gc:gc + 8].unsqueeze(2).to_broadcast([128, 8, 128]), op=ALU.mult),
                        reads=[B_stage[sbuf_i], B_cols], writes=[B_ring[slot]])
                else:
                    e = act if (pi + si) % 2 == 0 else pool
                    if e is act:
                        fw.op(act, lambda src=src, dst=dst: acopy(out=dst, in_=src), reads=[B_stage[sbuf_i]], writes=[B_ring[slot]])
                    else:
                        fw.op(pool, lambda src=src, dst=dst: G.tensor_copy(out=dst, in_=src), reads=[B_stage[sbuf_i]], writes=[B_ring[slot]])
            ne = len(piece) * 1024
            fw.dma(sp, d_cvt[slot], wbf_d[pi, :, 0:ne], ring[slot][:, 0:ne], reads=[B_ring[slot]])
        for dc in d_cvt:
            sp.prog.append(lambda dc=dc, c=dc.count: sp.eng.wait_ge(dc.sem, c))
            sp.waited[dc.name] = dc.count
        fw.handoff(B_stage[0:2], [B_VS])
        fw.handoff(B_stage[2:4], [B_KT])

        stream_state = {"next": 0, "order": []}

        def prefetch(pi):
            slot = stream_state["next"] % NSLOT
            stream_state["next"] += 1
            ne = len(PIECES[pi]) * 1024
            fw.dma(sp, d_ring[slot], ring[slot][:, 0:ne], wbf_d[pi, :, 0:ne], writes=[B_ring[slot]])
            return slot

        class Stream:
            def __init__(self, order):
                self.order = order
                self.slots = {}
                self.issued = 0
                self.used = 0

            def ensure(self, upto):
                while self.issued < min(upto, len(self.order)):
                    self.slots[self.issued] = prefetch(self.order[self.issued])
                    self.issued += 1

            def get(self):
                i = self.used
                self.ensure(i + 1)
                slot = self.slots[i]
                self.used += 1
                return slot

            def after_use(self):
                self.ensure(self.used + NSLOT - 1)

        def load_h(sti, tiles, hb):
            for j, gi in enumerate(tiles):
                if gi == 0:
                    fw.op(pool, lambda hb=hb, j=j: G.memset(hbuf[hb][:, j, :], 0.0), writes=[B_h[hb][j]])
                    fw.dma(sp, d_x[j], hbuf[hb][112:128, j, :], meta_d[:, :], writes=[B_h[hb][j]])
                else:
                    fw.dma(sp, d_x[j], hbuf[hb][:, j, :], x_d[(gi - 1) * 128:gi * 128, :], writes=[B_h[hb][j]])

        def norm_T(hb, nt):
            for j in range(nt):
                k = j % 2
                sc = stat[:, 2 * j:2 * j + 1]
                sc2 = stat[:, 2 * j + 1:2 * j + 2]
                fw.op(act, lambda j=j, sc=sc: S.activation(out=beta[1][:].bitcast(BF16), in_=hbuf[hb][:, j, :], func=AF.Square, accum_out=sc),
                      reads=[B_h[hb][j]], writes=[B_beta[1], B_stat])
                fw.op(act, lambda sc=sc: S.activation(out=sc, in_=sc, func=AF.Sqrt, scale=1.0 / D, bias=RMS_EPS), reads=[B_stat], writes=[B_stat])
                fw.op(dve, lambda sc=sc, sc2=sc2: V.reciprocal(out=sc2, in_=sc), reads=[B_stat], writes=[B_stat])
                fw.op(act, lambda j=j, k=k, sc2=sc2: S.activation(out=nb[k][:], in_=hbuf[hb][:, j, :], func=AF.Copy, scale=sc2),
                      reads=[B_h[hb][j], B_stat], writes=[B_nb[k]])
                pb = 6 + k
                for c in range(8):
                    fw.op(pe, lambda c=c, k=k, pb=pb: T.transpose(out=psb(pb)[:, c * 128:(c + 1) * 128], in_=nb[k][:, c * 128:(c + 1) * 128],
                                                                   identity=identb[:]),
                          reads=[B_nb[k], B_identb], writes=[B_PS[pb]])
                fw.op(dve, lambda j=j, pb=pb: V.tensor_copy(out=nT[:, :, j * 128:(j + 1) * 128],
                                                            in_=psb(pb).rearrange("p (c t) -> p c t", c=8)),
                      reads=[B_PS[pb]], writes=[B_nT])

        def ffn(f, hb, nt, strm):
            W = nt * 128
            acc = [4, 5, 6, 7]
            pend = []

            def down(m, slot):
                for j in range(nt):
                    for half in range(2):
                        a = acc[2 * j + half]
                        fw.op(pe, lambda m=m, j=j, half=half, a=a, slot=slot: T.matmul(
                            PS[a][:, :], lhsT=actT[:, m, j * 128:(j + 1) * 128], rhs=ring[slot][:, 2048 + half * 512:2048 + (half + 1) * 512],
                            start=(m == 0), stop=(m == NM - 1)), reads=[B_actT[m], B_ring[slot]], writes=[B_PS[a]])

            for m in range(NM):
                slot = strm.get()
                gb, ub = (0, 1) if m % 2 == 0 else (2, 3)
                for k in range(8):
                    fw.op(pe, lambda k=k, slot=slot, gb=gb: T.matmul(PS[gb][:, 0:W], lhsT=ring[slot][:, k * 128:(k + 1) * 128], rhs=nT[:, k, 0:W],
                                                                     start=(k == 0), stop=(k == 7)),
                          reads=[B_ring[slot], B_nT], writes=[B_PS[gb]])
                for k in range(8):
                    fw.op(pe, lambda k=k, slot=slot, ub=ub: T.matmul(PS[ub][:, 0:W], lhsT=ring[slot][:, 1024 + k * 128:1024 + (k + 1) * 128],
                                                                     rhs=nT[:, k, 0:W], start=(k == 0), stop=(k == 7)),
                          reads=[B_ring[slot], B_nT], writes=[B_PS[ub]])
                sg = m % 2
                fw.op(act, lambda gb=gb, sg=sg: S.activation(out=sig_t[sg][:, 0:W], in_=PS[gb][:, 0:W], func=AF.Silu),
                      reads=[B_PS[gb]], writes=[B_sig[sg]])
                fw.op(dve, lambda m=m, ub=ub, sg=sg: V.tensor_tensor(out=actT[:, m, 0:W], in0=PS[ub][:, 0:W], in1=sig_t[sg][:, 0:W], op=ALU.mult),
                      reads=[B_PS[ub], B_sig[sg]], writes=[B_actT[m]])
                pend.append((m, slot))
                if len(pend) > 1:
                    down(*pend.pop(0))
                    strm.after_use()
            while pend:
                down(*pend.pop(0))
                strm.after_use()
            for j in range(nt):
                for half in range(2):
                    a = acc[2 * j + half]
                    fw.op(dve, lambda j=j, half=half, a=a: V.scalar_tensor_tensor(
                        out=hbuf[hb][:, j, half * 512:(half + 1) * 512], in0=PS[a][:, :], scalar=0.5,
                        in1=hbuf[hb][:, j, half * 512:(half + 1) * 512], op0=ALU.mult, op1=ALU.add),
                        reads=[B_PS[a], B_h[hb][j]], writes=[B_h[hb][j]])

        def col(c, n=128):
            return cols[0:n, c:c + 1]

        def proj(tiles, nt, strm):
            W = nt * 128
            t0 = tiles[0] * 128
            slot = None
            for ci in range(27):
                if ci % 3 == 0:
                    if slot is not None:
                        strm.after_use()
                    slot = strm.get()
                off = (ci % 3) * 1024
                pb = ci % 4
                if ci < 23:
                    for k in range(8):
                        fw.op(pe, lambda k=k, slot=slot, off=off, pb=pb: T.matmul(
                            PS[pb][:, 0:W], lhsT=ring[slot][:, off + k * 128:off + (k + 1) * 128], rhs=nT[:, k, 0:W], start=(k == 0), stop=(k == 7)),
                            reads=[B_ring[slot], B_nT], writes=[B_PS[pb]])
                    if ci < 4:
                        fw.op(act, lambda ci=ci, pb=pb: acopy(out=QT[:, ci, 0:W], in_=PS[pb][:, 0:W], scale=0.125), reads=[B_PS[pb]], writes=[B_QT])
                    elif ci < 8:
                        fw.op(act, lambda ci=ci, pb=pb: acopy(out=KT[:, ci - 4, t0:t0 + W], in_=PS[pb][:, 0:W]), reads=[B_PS[pb]], writes=[B_KT])
                    else:
                        ri = ci - 8
                        rb = ri % 2
                        if ri < 4:
                            npart, mu_c, dst, dbuf = 128, C_MU_R + ri, MIXR[:, ri, 0:W], B_MIX[ri]
                        elif ri < 8:
                            npart, mu_c, dst, dbuf = 128, C_MU_R + ri, MIX[:, ri - 4, 0:W], B_MIX[ri]
                        elif ri < 12:
                            npart, mu_c, dst, dbuf = 128, C_MU_R + ri, MIXV[:, ri - 8, 0:W], B_MIX[ri]
                        elif ri == 12:
                            npart, mu_c, dst, dbuf = 32, C_MU_XW, xw_t[:, 0:W], B_xw
                        elif ri == 13:
                            npart, mu_c, dst, dbuf = 32, C_MU_XA, xa_t[:, 0:W], B_xa
                        else:
                            npart, mu_c, dst, dbuf = 96, C_MU_XG, xg_t[:, 0:W], B_xg
                        P_ = slice(0, npart)
                        fw.op(act, lambda rb=rb, pb=pb, P_=P_: acopy(out=raw[rb][P_, 1:W + 1], in_=PS[pb][P_, 0:W]), reads=[B_PS[pb]], writes=[B_raw[rb]])
                        fw.op(pool, lambda rb=rb, ri=ri, P_=P_: G.tensor_copy(out=raw[rb][P_, 0:1], in_=carry[P_, ri:ri + 1]),
                              reads=[B_carry], writes=[B_raw[rb]])
                        fw.op(pool, lambda rb=rb, ri=ri, P_=P_: G.tensor_copy(out=carry[P_, ri:ri + 1], in_=raw[rb][P_, W:W + 1]),
                              reads=[B_raw[rb]], writes=[B_carry])
                        tb = ri % 4
                        fw.op(dve, lambda rb=rb, tb=tb, P_=P_: V.tensor_tensor(out=tmpf[tb][P_, 0:W], in0=raw[rb][P_, 0:W], in1=raw[rb][P_, 1:W + 1],
                                                                               op=ALU.subtract), reads=[B_raw[rb]], writes=[B_tmpf[tb]])
                        fw.op(dve, lambda rb=rb, tb=tb, P_=P_, mu_c=mu_c, dst=dst, npart=npart: V.scalar_tensor_tensor(
                            out=dst, in0=tmpf[tb][P_, 0:W], scalar=col(mu_c, npart), in1=raw[rb][P_, 1:W + 1], op0=ALU.mult, op1=ALU.add),
                            reads=[B_tmpf[tb], B_raw[rb], B_cols], writes=[dbuf])
                else:
                    vi = ci - 23
                    for j in range(nt):
                        pbv = (ci + j) % 4
                        for k in range(8):
                            fw.op(pe, lambda k=k, j=j, slot=slot, off=off, pbv=pbv: T.matmul(
                                PS[pbv][:, 0:128], lhsT=nT[:, k, j * 128:(j + 1) * 128], rhs=ring[slot][:, off + k * 128:off + (k + 1) * 128],
                                start=(k == 0), stop=(k == 7)), reads=[B_ring[slot], B_nT], writes=[B_PS[pbv]])
                        gi = tiles[j]
                        fw.op(act, lambda gi=gi, vi=vi, pbv=pbv: acopy(out=VS[:, gi, vi * 128:(vi + 1) * 128], in_=PS[pbv][:, 0:128]),
                              reads=[B_PS[pbv]], writes=[B_VS])
            strm.after_use()

        def attention(tiles, nt):
            jobs = []
            for j, gi in enumerate(tiles):
                for q in range(4):
                    for hp in range(2):
                        nchunks = (gi + 1 + 3) // 4
                        for c in range(nchunks):
                            jobs.append((j, gi, q, hp, c, nchunks))

            SKEW = NSETS - 2
            uni_rw = [b_ for l_ in B_OPS for b_ in l_] + [b_ for l_ in B_AM for b_ in l_] + [b_ for l_ in B_NCH for b_ in l_] + [b_ for l_ in B_XCH for b_ in l_]
            uni_at = []
            for k_ in range(2, NSETS):
                uni_at += [B_om[k_], B_RB[k_], B_attn[k_], B_attnT[k_]]
            fw.handoff(uni_rw, uni_at)

            def stage_a(idx):
                j, gi, q, hp, c, nchunks = jobs[idx]
                bb = idx % NSETS
                zb = [0, 1, 6, 7][idx % 4]
                R_ = slice(hp * 64, (hp + 1) * 64)
                hi = gi - 4 * c
                lo = max(0, hi - 3)
                w = (hi - lo + 1) * 128
                fw.op(pe, lambda: T.matmul(PS[zb][:, 0:w], lhsT=QT[R_, q, j * 128:(j + 1) * 128], rhs=KT[R_, q, lo * 128:lo * 128 + w], start=True, stop=True),
                      reads=[B_QT, B_KT], writes=[B_PS[zb]])
                fw.op(act, lambda: S.activation(out=om[bb][:, 0:w], in_=PS[zb][:, 0:w], func=AF.Sigmoid, scale=-1.0), reads=[B_PS[zb]], writes=[B_om[bb]])
                masks = []
                if c == 0:
                    masks.append((w - 128, K_NMT))
                if lo == 0:
                    masks.append((0, K_NMP))
                for (o_, mk) in masks:
                    fw.op(dve, lambda o_=o_, mk=mk: V.tensor_tensor(out=om[bb][:, o_:o_ + 128], in0=om[bb][:, o_:o_ + 128],
                                                                    in1=consts[:, mk:mk + 128], op=ALU.max),
                          reads=[B_om[bb], B_consts], writes=[B_om[bb]])
                if c == 0:
                    init = 1.0
                    rd = [B_om[bb], B_zeros]
                else:
                    pr = (idx - 1) % NSETS
                    init = RB[pr][:, 0:1]
                    rd = [B_om[bb], B_zeros, B_RB[pr]]
                fw.op(dve, lambda: V.tensor_tensor_scan(out=rev(RB[bb][:, 0:w], w), data0=rev(om[bb][:, 0:w], w), data1=rev(zeros_f[:, 0:w], w),
                                                        initial=init, op0=ALU.mult, op1=ALU.add), reads=rd, writes=[B_RB[bb]])
                fw.op(pool, lambda: G.tensor_tensor(out=attn[bb][:, 0:w - 1], in0=RB[bb][:, 1:w], in1=RB[bb][:, 0:w - 1], op=ALU.subtract),
                      reads=[B_RB[bb]], writes=[B_attn[bb]])

            def stage_b(idx):
                j, gi, q, hp, c, nchunks = jobs[idx]
                bb = idx % NSETS
                tb = 2 + (idx % 2)
                ob = 4 + (q % 2)
                R_ = slice(hp * 64, (hp + 1) * 64)
                hi = gi - 4 * c
                lo = max(0, hi - 3)
                nb_ = hi - lo + 1
                w = nb_ * 128
                pr = (idx - 1) % NSETS
                if c == 0:
                    fw.op(act, lambda: S.activation(out=attn[bb][:, w - 1:w], in_=RB[bb][:, w - 1:w], func=AF.Identity, scale=-1.0, bias=1.0),
                          reads=[B_RB[bb]], writes=[B_attn[bb]])
                else:
                    fw.op(act, lambda: S.activation(out=attn[bb][:, w - 1:w], in_=RB[bb][:, w - 1:w], func=AF.Identity, scale=-1.0, bias=RB[pr][:, 0:1]),
                          reads=[B_RB[bb], B_RB[pr]], writes=[B_attn[bb]])
                for b_ in range(nb_):
                    fw.op(pe, lambda b_=b_: T.transpose(out=psb(tb)[:, b_ * 128:(b_ + 1) * 128], in_=attn[bb][:, b_ * 128:(b_ + 1) * 128], identity=identb[:]),
                          reads=[B_attn[bb], B_identb], writes=[B_PS[tb]])
                fw.op(act, lambda: acopy(out=attnT[bb].rearrange("p b t -> p (b t)")[:, 0:w], in_=psb(tb)[:, 0:w]),
                      reads=[B_PS[tb]], writes=[B_attnT[bb]])

            def stage_b2(idx):
                j, gi, q, hp, c, nchunks = jobs[idx]
                bb = idx % NSETS
                ob = 4 + (q % 2)
                R_ = slice(hp * 64, (hp + 1) * 64)
                hi = gi - 4 * c
                lo = max(0, hi - 3)
                nb_ = hi - lo + 1
                for b_ in range(nb_):
                    first = (c == 0) and (b_ == 0)
                    last = (c == nchunks - 1) and (b_ == nb_ - 1)
                    fw.op(pe, lambda b_=b_, first=first, last=last: T.matmul(
                        PS[ob][R_, 0:128], lhsT=VS[:, lo + b_, q * 128 + hp * 64:q * 128 + (hp + 1) * 64], rhs=attnT[bb][:, b_, :],
                        start=first, stop=last), reads=[B_VS, B_attnT[bb]], writes=[B_PS[ob]])
                if hp == 1 and c == nchunks - 1:
                    fw.op(act, lambda: acopy(out=mixT[:, q, j * 128:(j + 1) * 128], in_=PS[ob][:, 0:128]), reads=[B_PS[ob]], writes=[B_mixT[q]])

            for i in range(len(jobs) + SKEW + 1):
                if i < len(jobs):
                    stage_a(i)
                if SKEW <= i < len(jobs) + SKEW:
                    stage_b(i - SKEW)
                if i >= SKEW + 1:
                    stage_b2(i - SKEW - 1)
            fw.handoff(uni_at, uni_rw)


        def rwkv(tiles, nt, full):
            W = nt * 128
            lw_up, la_up, lg_up = lora[0:32, 0, :], lora[0:32, 1, :], lora[0:96, 2, :]
            fw.op(act, lambda: S.activation(out=xw_t[:, 0:W], in_=xw_t[:, 0:W], func=AF.Tanh), reads=[B_xw], writes=[B_xw])
            fw.op(act, lambda: S.activation(out=xg_t[:, 0:W], in_=xg_t[:, 0:W], func=AF.Sigmoid), reads=[B_xg], writes=[B_xg])
            def rr_emit(chains):
                while any(chains):
                    for c_ in chains:
                        if c_:
                            fw.op(*c_.pop(0))

            for q in range(4):
                Q_ = slice(q * 128, (q + 1) * 128)
                kmix = MIX[:, q, 0:W]
                rmix = MIXR[:, q, 0:W]
                c1, c2, c3, c4 = [], [], [], []
                c1.append((pe, lambda Q_=Q_: T.matmul(PS[0][:, 0:W], lhsT=lw_up[:, Q_], rhs=xw_t[:, 0:W], start=True, stop=True),
                           [B_lora, B_xw], [B_PS[0]]))
                c1.append((act, lambda q=q: S.activation(out=SG[:, q, 0:W], in_=PS[0][:, 0:W], func=AF.Sigmoid, bias=col(C_W0 + q)),
                           [B_PS[0], B_cols], [B_SG[q]]))
                for j in range(nt):
                    J_ = slice(j * 128, (j + 1) * 128)
                    c1.append((dve, lambda q=q, J_=J_: V.tensor_tensor_scan(out=CS[:, q, J_], data0=ones_f[:, 0:128], data1=SG[:, q, J_], initial=0.0,
                                                                             op0=ALU.mult, op1=ALU.add), [B_SG[q], B_ones], [B_CS[q]]))
                c1.append((act, lambda q=q: S.activation(out=GG[:, q, 0:W], in_=CS[:, q, 0:W], func=AF.Exp, scale=-DECAY_C), [B_CS[q]], [B_GG[q]]))
                c1.append((act, lambda q=q: S.activation(out=GI[:, q, 0:W], in_=CS[:, q, 0:W], func=AF.Exp, scale=DECAY_C), [B_CS[q]], [B_GI[q]]))
                c1.append((pool, lambda q=q: G.tensor_tensor(out=tmpf[0][:, 0:W], in0=CS[:, q, 0:W], in1=SG[:, q, 0:W], op=ALU.subtract),
                           [B_CS[q], B_SG[q]], [B_tmpf[0]]))
                c1.append((act, lambda q=q: S.activation(out=GP[:, q, 0:W], in_=tmpf[0][:, 0:W], func=AF.Exp, scale=-DECAY_C), [B_tmpf[0]], [B_GP[q]]))
                c2.append((pe, lambda Q_=Q_: T.matmul(PS[1][:, 0:W], lhsT=la_up[:, Q_], rhs=xa_t[:, 0:W], start=True, stop=True),
                           [B_lora, B_xa], [B_PS[1]]))
                c2.append((act, lambda q=q: S.activation(out=A_T[:, q, 0:W], in_=PS[1][:, 0:W], func=AF.Sigmoid, bias=col(C_A0 + q)),
                           [B_PS[1], B_cols], [B_A[q]]))
                c2.append((dve, lambda q=q: V.tensor_scalar(out=tmpf[3][:, 0:W], in0=A_T[:, q, 0:W], scalar1=-1.0, scalar2=col(C_KA + q),
                                                            op0=ALU.add, op1=ALU.mult), [B_A[q], B_cols], [B_tmpf[3]]))
                c2.append((dve, lambda q=q, kmix=kmix: V.scalar_tensor_tensor(out=KTL[:, q, 0:W], in0=tmpf[3][:, 0:W], scalar=1.0, in1=kmix,
                                                                              op0=ALU.add, op1=ALU.mult), [B_tmpf[3], B_MIX[4 + q]], [B_KTL[q]]))
                c2.append((dve, lambda q=q, rmix=rmix: V.scalar_tensor_tensor(out=nT[:, 1, 0:W], in0=rmix, scalar=col(C_RK + q), in1=KTL[:, q, 0:W],
                                                                              op0=ALU.mult, op1=ALU.mult), [B_MIX[q], B_KTL[q], B_cols], [B_nT]))
                c2.append((pe, lambda: T.matmul(PS[3][:, 256:256 + W], lhsT=bdones[:], rhs=nT[:, 1, 0:W], start=True, stop=True),
                           [B_bdones, B_nT], [B_PS[3]]))
                c2.append((dve, lambda q=q: V.tensor_tensor(out=BONT[:, q, 0:W], in0=PS[3][:, 256:256 + W], in1=MIXV[:, q, 0:W], op=ALU.mult),
                           [B_PS[3], B_MIX[8 + q]], [B_BONT[q]]))
                c3.append((pe, lambda Q_=Q_: T.matmul(PS[2][:, 0:W], lhsT=lg_up[:, Q_], rhs=xg_t[:, 0:W], start=True, stop=True),
                           [B_lora, B_xg], [B_PS[2]]))
                c3.append((act, lambda q=q: acopy(out=GT[:, q, 0:W], in_=PS[2][:, 0:W]), [B_PS[2]], [B_GT[q]]))
                c4.append((dve, lambda q=q, kmix=kmix: V.tensor_scalar(out=KKT[:, q, 0:W], in0=kmix, scalar1=col(C_KK + q), scalar2=None, op0=ALU.mult),
                           [B_MIX[4 + q], B_cols], [B_KKT[q]]))
                c4.append((pool, lambda q=q: G.tensor_tensor(out=nT[:, 0, 0:W], in0=KKT[:, q, 0:W], in1=KKT[:, q, 0:W], op=ALU.mult),
                           [B_KKT[q]], [B_nT]))
                c4.append((pe, lambda: T.matmul(PS[3][:, 0:W], lhsT=bdones[:], rhs=nT[:, 0, 0:W], start=True, stop=True),
                           [B_bdones, B_nT], [B_PS[3]]))
                c4.append((act, lambda: S.activation(out=tmpf[1][:, 0:W], in_=PS[3][:, 0:W], func=AF.Sqrt), [B_PS[3]], [B_tmpf[1]]))
                c4.append((dve, lambda: V.tensor_scalar(out=tmpf[1][:, 0:W], in0=tmpf[1][:, 0:W], scalar1=1e-12, scalar2=None, op0=ALU.max),
                           [B_tmpf[1]], [B_tmpf[1]]))
                c4.append((dve, lambda: V.reciprocal(out=tmpf[2][:, 0:W], in_=tmpf[1][:, 0:W]), [B_tmpf[1]], [B_tmpf[2]]))
                c4.append((dve, lambda q=q: V.tensor_tensor(out=KKT[:, q, 0:W], in0=KKT[:, q, 0:W], in1=tmpf[2][:, 0:W], op=ALU.mult),
                           [B_KKT[q], B_tmpf[2]], [B_KKT[q]]))
                rr_emit([c1, c4, c2, c3])
                oc = []
                for j in range(nt):
                    J_ = slice(j * 128, (j + 1) * 128)
                    ob_ = B_OPS[q][j]
                    tq = tmpf[1] if j == 0 else tmpf[2]
                    bq = B_tmpf[1] if j == 0 else B_tmpf[2]
                    c_ = []
                    c_.append((dve, lambda q=q, j=j, J_=J_: V.scalar_tensor_tensor(out=OPS[:, q, j, 0, :], in0=KKT[:, q, J_], scalar=-1.0, in1=GP[:, q, J_],
                                                                                    op0=ALU.mult, op1=ALU.mult), [B_KKT[q], B_GP[q]], [ob_]))
                    c_.append((pool, lambda q=q, j=j, J_=J_: G.tensor_tensor(out=OPS[:, q, j, 1, :], in0=MIXR[:, q, J_], in1=GG[:, q, J_], op=ALU.mult),
                               [B_MIX[q], B_GG[q]], [ob_]))
                    c_.append((pool, lambda q=q, j=j, J_=J_: G.tensor_tensor(out=OPS[:, q, j, 2, :], in0=KTL[:, q, J_], in1=GI[:, q, J_], op=ALU.mult),
                               [B_KTL[q], B_GI[q]], [ob_]))
                    c_.append((pool, lambda q=q, J_=J_, tq=tq: G.tensor_tensor(out=tq[:, 0:128], in0=KKT[:, q, J_], in1=A_T[:, q, J_], op=ALU.mult),
                               [B_KKT[q], B_A[q]], [bq]))
                    c_.append((pool, lambda q=q, j=j, J_=J_, tq=tq: G.tensor_tensor(out=OPS[:, q, j, 3, :], in0=tq[:, 0:128], in1=GI[:, q, J_], op=ALU.mult),
                               [bq, B_GI[q]], [ob_]))
                    gl = GG[:, q, j * 128 + 127:j * 128 + 128]
                    c_.append((dve, lambda q=q, j=j, gl=gl: V.tensor_scalar(out=OPS[:, q, j, 4:6, :], in0=OPS[:, q, j, 2:4, :], scalar1=gl, scalar2=None,
                                                                            op0=ALU.mult), [ob_, B_GG[q]], [ob_]))
                    oc.append(c_)
                rr_emit(oc)
            for j in range(nt):
                for q in range(4):
                    fw.op(pe, lambda q=q, j=j: T.transpose(out=psb(0)[:, q * 128:(q + 1) * 128], in_=MIXV[:, q, j * 128:(j + 1) * 128], identity=identb[:]),
                          reads=[B_MIX[8 + q], B_identb], writes=[B_PS[0]])
                fw.op(act, lambda j=j: acopy(out=VTOK[:, j, :], in_=psb(0)[:, 0:512]), reads=[B_PS[0]], writes=[B_VTOK[j]])
            for j, gi in enumerate(tiles):
                for q in range(4):
                    sl = q
                    ob_ = B_OPS[q][j]
                    for hp in range(2):
                        R_ = slice(hp * 64, (hp + 1) * 64)
                        fw.op(pe, lambda R_=R_, q=q, j=j: T.matmul(PS[0][:, 0:256], lhsT=OPS[R_, q, j, 2, :], rhs=OPS[R_, q, j, 0:2, :].rearrange("p a t -> p (a t)"),
                                                                   start=True, stop=True), reads=[ob_], writes=[B_PS[0]])
                        fw.op(pe, lambda R_=R_, q=q, j=j: T.matmul(PS[1][:, 0:256], lhsT=OPS[R_, q, j, 3, :], rhs=OPS[R_, q, j, 0:2, :].rearrange("p a t -> p (a t)"),
                                                                   start=True, stop=True), reads=[ob_], writes=[B_PS[1]])
                        fw.op(pe, lambda R_=R_, q=q, j=j: T.matmul(PS[0][:, 256:384], lhsT=OPS[R_, q, j, 0, :], rhs=OPS[R_, q, j, 3, :],
                                                                   start=True, stop=True), reads=[ob_], writes=[B_PS[0]])
                        fw.op(dve, lambda hp=hp, sl=sl: V.tensor_tensor(out=AM[:, sl, hp, 0, :], in0=PS[0][:, 0:256], in1=consts[:, K_M1:K_M1 + 256], op=ALU.mult),
                              reads=[B_PS[0], B_consts], writes=[B_AM[sl][hp]])
                        fw.op(dve, lambda hp=hp, sl=sl: V.tensor_tensor(out=AM[:, sl, hp, 1, :], in0=PS[1][:, 0:256], in1=consts[:, K_M1:K_M1 + 256], op=ALU.mult),
                              reads=[B_PS[1], B_consts], writes=[B_AM[sl][hp]])
                        fw.op(act, lambda hp=hp, sl=sl: acopy(out=NCH[:, sl, 0, hp, 0, :], in_=AM[:, sl, hp, 1, 0:128]),
                              reads=[B_AM[sl][hp]], writes=[B_NCH[sl][0]])
                        fw.op(dve, lambda hp=hp, sl=sl: V.tensor_tensor(out=NCH[:, sl, 0, hp, 1, :], in0=PS[0][:, 256:384], in1=consts[:, K_MT:K_MT + 128], op=ALU.mult),
                              reads=[B_PS[0], B_consts], writes=[B_NCH[sl][0]])
                    XR = slice(sl * 128, (sl + 1) * 128)
                    fw.op(pe, lambda q=q, j=j, XR=XR: T.matmul(PS[3][:, XR], lhsT=OPS[:, q, j, 0, :], rhs=SBD[:, q, :], start=True, stop=False),
                          reads=[ob_, B_SBD[q]], writes=[B_PS[3]])
                    for hp in range(2):
                        HC = slice(sl * 128 + hp * 64, sl * 128 + (hp + 1) * 64)
                        fw.op(pe, lambda hp=hp, HC=HC, q=q, j=j, sl=sl: T.matmul(PS[3][:, HC], lhsT=AM[:, sl, hp, 0, 0:128], rhs=VTOK[:, j, q * 128 + hp * 64:q * 128 + (hp + 1) * 64],
                                                                                 start=False, stop=(hp == 1)), reads=[B_AM[sl][hp], B_VTOK[j]], writes=[B_PS[3]])
                    fw.op(act, lambda sl=sl, XR=XR: acopy(out=XCH[:, sl, 0, :], in_=PS[3][:, XR]), reads=[B_PS[3]], writes=[B_XCH[sl][0]])
                for lvl in range(7):
                    cb, nb2 = lvl % 2, (lvl + 1) % 2
                    for sl in range(4):
                        cbk = sl // 2
                        for hp in range(2):
                            H_ = slice(hp * 64, (hp + 1) * 64)
                            CO = slice((sl % 2) * 128 + hp * 64, (sl % 2) * 128 + (hp + 1) * 64)
                            fw.op(pe, lambda H_=H_, CO=CO, cb=cb, cbk=cbk, sl=sl: T.matmul(PS[cbk][:, CO], lhsT=identb[:], rhs=XCH[:, sl, cb, H_], start=True, stop=False),
                                  reads=[B_identb, B_XCH[sl][cb]], writes=[B_PS[cbk]])
                            fw.op(pe, lambda H_=H_, CO=CO, cb=cb, cbk=cbk, sl=sl, hp=hp: T.matmul(PS[cbk][:, CO], lhsT=NCH[:, sl, cb, hp, 0, :], rhs=XCH[:, sl, cb, H_],
                                                                                                   start=False, stop=True),
                                  reads=[B_NCH[sl][cb], B_XCH[sl][cb]], writes=[B_PS[cbk]])
                        if sl % 2 == 1:
                            for s2 in (sl - 1, sl):
                                CX = slice((s2 % 2) * 128, (s2 % 2) * 128 + 128)
                                if lvl < 6:
                                    fw.op(act, lambda s2=s2, nb2=nb2, cbk=cbk, CX=CX: acopy(out=XCH[:, s2, nb2, :], in_=PS[cbk][:, CX]),
                                          reads=[B_PS[cbk]], writes=[B_XCH[s2][nb2]])
                                else:
                                    fw.op(act, lambda s2=s2, cbk=cbk, CX=CX: acopy(out=Pbf[:, s2, :], in_=PS[cbk][:, CX]), reads=[B_PS[cbk]], writes=[B_Pbf[s2]])
                    if lvl < 6:
                        for sl in range(4):
                            sbk = 4 + sl
                            for hp in range(2):
                                o_ = hp * 256
                                fw.op(pe, lambda cb=cb, hp=hp, sbk=sbk, sl=sl, o_=o_: T.matmul(PS[sbk][:, o_:o_ + 128], lhsT=NCH[:, sl, cb, hp, 1, :], rhs=NCH[:, sl, cb, hp, 0, :],
                                                                                               start=True, stop=True), reads=[B_NCH[sl][cb]], writes=[B_PS[sbk]])
                                fw.op(pe, lambda cb=cb, hp=hp, sbk=sbk, sl=sl, o_=o_: T.matmul(PS[sbk][:, o_ + 128:o_ + 256], lhsT=NCH[:, sl, cb, hp, 0, :], rhs=NCH[:, sl, cb, hp, 1, :],
                                                                                               start=True, stop=True), reads=[B_NCH[sl][cb]], writes=[B_PS[sbk]])
                            if sl % 2 == 0:
                                fw.op(dve, lambda nb2=nb2, sbk=sbk, sl=sl: V.tensor_copy(out=NCH[:, sl, nb2, :, :, :].rearrange("p h a t -> p (h a t)"), in_=PS[sbk][:, 0:512]),
                                      reads=[B_PS[sbk]], writes=[B_NCH[sl][nb2]])
                            else:
                                fw.op(act, lambda nb2=nb2, sbk=sbk, sl=sl: acopy(out=NCH[:, sl, nb2, :, :, :].rearrange("p h a t -> p (h a t)"), in_=PS[sbk][:, 0:512]),
                                      reads=[B_PS[sbk]], writes=[B_NCH[sl][nb2]])
                for q in range(4):
                    sl = q
                    ob_ = B_OPS[q][j]
                    if full:
                        YC = slice(q * 128, (q + 1) * 128)
                        fw.op(pe, lambda q=q, j=j, YC=YC: T.matmul(PS[2][:, YC], lhsT=OPS[:, q, j, 1, :], rhs=SBD[:, q, :], start=True, stop=False),
                              reads=[ob_, B_SBD[q]], writes=[B_PS[2]])
                        for hp in range(2):
                            HC = slice(q * 128 + hp * 64, q * 128 + (hp + 1) * 64)
                            H_ = slice(hp * 64, (hp + 1) * 64)
                            fw.op(pe, lambda hp=hp, HC=HC, H_=H_, sl=sl: T.matmul(PS[2][:, HC], lhsT=AM[:, sl, hp, 1, 128:256], rhs=Pbf[:, sl, H_], start=False, stop=False),
                                  reads=[B_AM[sl][hp], B_Pbf[sl]], writes=[B_PS[2]])
                            fw.op(pe, lambda hp=hp, HC=HC, j=j, sl=sl: T.matmul(PS[2][:, HC], lhsT=AM[:, sl, hp, 0, 128:256], rhs=VTOK[:, j, HC], start=False, stop=(hp == 1)),
                                  reads=[B_AM[sl][hp], B_VTOK[j]], writes=[B_PS[2]])
                    tkb = 3
                    for s_ in range(2):
                        fw.op(pe, lambda s_=s_, q=q, j=j: T.transpose(out=psb(tkb)[:, s_ * 128:(s_ + 1) * 128], in_=OPS[:, q, j, 4 + s_, :], identity=identb[:]),
                              reads=[ob_, B_identb], writes=[B_PS[tkb]])
                    fw.op(dve, lambda: V.tensor_copy(out=TOK[:].rearrange("p a t -> p (a t)"), in_=psb(tkb)[:, 0:256]), reads=[B_PS[tkb]], writes=[B_TOK])
                    stb = 0 + (q % 2)
                    fw.op(pe, lambda sl=sl, stb=stb: T.matmul(PS[stb][:, 0:128], lhsT=TOK[:, 1, :], rhs=Pbf[:, sl, :], start=True, stop=False),
                          reads=[B_TOK, B_Pbf[sl]], writes=[B_PS[stb]])
                    fw.op(pe, lambda q=q, j=j, stb=stb: T.matmul(PS[stb][:, 0:128], lhsT=TOK[:, 0, :], rhs=VTOK[:, j, q * 128:(q + 1) * 128], start=False, stop=True),
                          reads=[B_TOK, B_VTOK[j]], writes=[B_PS[stb]])
                    gl = GG[:, q, j * 128 + 127:j * 128 + 128]
                    for hp in range(2):
                        H_ = slice(hp * 64, (hp + 1) * 64)
                        fw.op(dve, lambda q=q, H_=H_, gl=gl, stb=stb: V.scalar_tensor_tensor(out=S32[H_, q, H_], in0=S32[H_, q, H_], scalar=gl[H_, :],
                                                                                             in1=PS[stb][H_, H_.start:H_.stop], op0=ALU.mult, op1=ALU.add),
                              reads=[B_S32[q], B_PS[stb], B_GG[q]], writes=[B_S32[q]])
                    fw.op(act, lambda q=q: acopy(out=SBD[:, q, :], in_=S32[:, q, :]), reads=[B_S32[q]], writes=[B_SBD[q]])
                if not full:
                    continue
                Y3 = PS[2][:, :].rearrange("p (h i) -> p h i", h=8)
                sm, sv, sr = stat[:, 16:24], stat[:, 24:32], stat[:, 32:40]
                fw.op(dve, lambda: V.tensor_reduce(out=sm, in_=Y3, op=ALU.add, axis=AX.X), reads=[RG_["Y"]], writes=[B_stat])
                fw.op(dve, lambda: V.tensor_scalar(out=sm, in0=sm, scalar1=1.0 / 64, scalar2=None, op0=ALU.mult), reads=[B_stat], writes=[B_stat])
                fw.op(dve, lambda: V.tensor_tensor(out=YN[:].rearrange("p (h i) -> p h i", h=8), in0=Y3, in1=sm.unsqueeze(2).to_broadcast([128, 8, 64]),
                                                   op=ALU.subtract), reads=[RG_["Y"], B_stat], writes=[B_YN])
                fw.op(pool, lambda: G.tensor_tensor(out=beta[0][:, 0:512], in0=YN[:], in1=YN[:], op=ALU.mult), reads=[B_YN], writes=[B_beta[0]])
                fw.op(dve, lambda: V.tensor_reduce(out=sv, in_=beta[0][:, 0:512].rearrange("p (h i) -> p h i", h=8), op=ALU.add, axis=AX.X),
                      reads=[B_beta[0]], writes=[B_stat])
                fw.op(act, lambda: S.activation(out=sv, in_=sv, func=AF.Sqrt, scale=1.0 / 64, bias=LNX_EPS), reads=[B_stat], writes=[B_stat])
                fw.op(dve, lambda: V.reciprocal(out=sr, in_=sv), reads=[B_stat], writes=[B_stat])
                fw.op(dve, lambda: V.tensor_tensor(out=YNB[:].rearrange("p (h i) -> p h i", h=8), in0=YN[:].rearrange("p (h i) -> p h i", h=8),
                                                   in1=sr.unsqueeze(2).to_broadcast([128, 8, 64]), op=ALU.mult), reads=[B_YN, B_stat], writes=[B_YNB])
                for q in range(4):
                    fw.op(pe, lambda q=q: T.transpose(out=psb(0)[:, q * 128:(q + 1) * 128], in_=YNB[:, q * 128:(q + 1) * 128], identity=identb[:]),
                          reads=[B_YNB, B_identb], writes=[B_PS[0]])
                J_ = slice(j * 128, (j + 1) * 128)
                for q in range(4):
                    fw.op(dve, lambda q=q: V.tensor_scalar(out=tmpf[q][:, 0:128], in0=psb(0)[:, q * 128:(q + 1) * 128], scalar1=col(C_LW + q), scalar2=col(C_LB + q),
                                                           op0=ALU.mult, op1=ALU.add), reads=[B_PS[0], B_cols], writes=[B_tmpf[q]])
                for q in range(4):
                    fw.op(pool, lambda q=q, J_=J_: G.tensor_tensor(out=tmpf[q][:, 128:256], in0=tmpf[q][:, 0:128], in1=BONT[:, q, J_], op=ALU.add),
                          reads=[B_tmpf[q], B_BONT[q]], writes=[B_tmpf[q]])
                for q in range(4):
                    fw.op(pool, lambda q=q, J_=J_: G.tensor_tensor(out=mixT[:, 4 + q, J_], in0=tmpf[q][:, 128:256], in1=GT[:, q, J_], op=ALU.mult),
                          reads=[B_tmpf[q], B_GT[q]], writes=[B_mixT[4 + q]])

        def wout(hb, nt, strm):
            acc = [4, 5, 6, 7]
            slot = None
            for c in range(8):
                if c % 3 == 0:
                    if slot is not None:
                        strm.after_use()
                    slot = strm.get()
                off = (c % 3) * 1024
                for j in range(nt):
                    for half in range(2):
                        a = acc[2 * j + half]
                        fw.op(pe, lambda c=c, j=j, half=half, a=a, slot=slot, off=off: T.matmul(
                            PS[a][:, :], lhsT=mixT[:, c, j * 128:(j + 1) * 128], rhs=ring[slot][:, off + half * 512:off + (half + 1) * 512],
                            start=(c == 0), stop=(c == 7)), reads=[B_mixT[c], B_ring[slot]], writes=[B_PS[a]])
            strm.after_use()
            for j in range(nt):
                for half in range(2):
                    a = acc[2 * j + half]
                    fw.op(dve, lambda j=j, half=half, a=a: V.tensor_tensor(
                        out=hbuf[hb][:, j, half * 512:(half + 1) * 512], in0=PS[a][:, :], in1=hbuf[hb][:, j, half * 512:(half + 1) * 512], op=ALU.add),
                        reads=[B_PS[a], B_h[hb][j]], writes=[B_h[hb][j]])

        def final(hb, tiles, nt):
            for j, gi in enumerate(tiles):
                k = j % 2
                sc = stat[:, 40 + 2 * j:41 + 2 * j]
                sc2 = stat[:, 41 + 2 * j:42 + 2 * j]
                fw.op(act, lambda j=j, sc=sc, k=k: S.activation(out=ybuf[k][:], in_=hbuf[hb][:, j, :], func=AF.Square, accum_out=sc),
                      reads=[B_h[hb][j]], writes=[B_beta[0], B_beta[1], B_stat])
                fw.op(act, lambda sc=sc: S.activation(out=sc, in_=sc, func=AF.Sqrt, scale=1.0 / D, bias=RMS_EPS), reads=[B_stat], writes=[B_stat])
                fw.op(dve, lambda sc=sc, sc2=sc2: V.reciprocal(out=sc2, in_=sc), reads=[B_stat], writes=[B_stat])
                fw.op(dve, lambda j=j, k=k, sc2=sc2: V.scalar_tensor_tensor(out=ybuf[k][:], in0=hbuf[hb][:, j, :], scalar=sc2, in1=gfin[:],
                                                                            op0=ALU.mult, op1=ALU.mult), reads=[B_h[hb][j], B_stat, B_gfin], writes=[B_beta[0], B_beta[1]])
                fw.dma(sp, d_out[k], out_d[(gi - 1) * 128:gi * 128, :], ybuf[k][:], reads=[B_beta[0], B_beta[1]])

        sts = [[0]] + [[i, i + 1] for i in range(1, ntiles, 2) if i + 1 < ntiles]
        if ntiles % 2 == 0:
            sts.append([ntiles - 1])
        load_h(0, sts[0], 0)
        for sti, tiles in enumerate(sts):
            hb = sti % 2
            nt = len(tiles)
            full = sti > 0
            order = list(range(P_FFN[1], P_FFN[1] + NM)) + list(range(P_IN, P_IN + 9))
            if full:
                order += list(range(P_OUT, P_OUT + 3)) + list(range(P_FFN[2], P_FFN[2] + NM))
            strm = Stream(order)
            strm.ensure(NSLOT - 1)
            norm_T(hb, nt)
            ffn(1, hb, nt, strm)
            norm_T(hb, nt)
            proj(tiles, nt, strm)
            if full:
                attention(tiles, nt)
            rwt_bufs = [B_A[0], B_SG[0], B_CS[0], B_KKT[0], B_KTL[0], B_GI[0], B_GP[0]] + B_GG
            fw.handoff(B_actT, rwt_bufs)
            rwkv(tiles, nt, full)
            fw.handoff(rwt_bufs, B_actT)
            if full and debug:
                for c in range(8):
                    fw.dma(sp, d_dbg, dbg_d[c, :, tiles[0] * 128:tiles[0] * 128 + nt * 128], mixT[:, c, 0:nt * 128], reads=[B_mixT[c]])
            if full:
                wout(hb, nt, strm)
                norm_T(hb, nt)
                ffn(2, hb, nt, strm)
                final(hb, tiles, nt)
            if sti + 1 < len(sts):
                load_h(sti + 1, sts[sti + 1], 0)
        if debug:
            sp.prog.append(lambda: sp.eng.wait_ge(d_dbg.sem, d_dbg.count))
        for k in range(2):
            sp.prog.append(lambda k=k: sp.eng.wait_ge(d_out[k].sem, d_out[k].count))
        fw.run(block)
    return nc


_NC_CACHE = {}


def kernel(**inp):
    inp = {k: np.asarray(v) for k, v in inp.items()}
    x = inp["x"].astype(np.float32, copy=False)
    nb_ = x.shape[0]
    if "nc" not in _NC_CACHE:
        _NC_CACHE["nc"] = build()
    nc = _NC_CACHE["nc"]
    wp = host_pieces(inp)
    cols = host_cols(inp)
    consts = host_consts()
    gfin = np.ascontiguousarray(np.broadcast_to(inp["final_norm"].reshape(1, D), (128, D))).astype(np.float32)
    lora = np.zeros((96, 3, 512), np.float32)
    lora[0:32, 0] = inp["rwkv_w_up"][0]
    lora[0:32, 1] = inp["rwkv_a_up"][0]
    lora[0:96, 2] = inp["rwkv_g_up"][0]
    meta = np.ascontiguousarray(inp["meta_tokens"]).astype(np.float32)
    in_maps = [{"x": np.ascontiguousarray(x[b]), "meta": meta, "wp": wp, "cols": cols, "consts": consts, "gfin": gfin, "lora": lora}
               for b in range(nb_)]
    res = run_bass_kernel_spmd(nc, in_maps, core_ids=list(range(nb_)))
    return np.stack([np.asarray(r["out"]) for r in res.results], axis=0).astype(np.float32)
```

```python
import numpy as np
from contextlib import ExitStack
import concourse.bass as bass
import concourse.mybir as mybir
from concourse.bass_utils import run_bass_kernel_spmd
from concourse.ap import AP

F32 = mybir.dt.float32
BF16 = mybir.dt.bfloat16
AF = mybir.ActivationFunctionType
ALU = mybir.AluOpType
AX = mybir.AxisListType

D = 1024
DFF = 2816
NM = DFF // 128
SEQ = 4096
NTILES_FULL = 33
PE_ELEMS = 3072
NSLOT = 4
INV_DT = F32
DECAY_C = float(np.exp(-0.5))
LNX_EPS = 64e-5
RMS_EPS = 1e-6

C_MU_R, C_MU_K, C_MU_V = 0, 4, 8
C_MU_XW, C_MU_XA, C_MU_XG = 12, 13, 14
C_W0, C_A0, C_KK, C_KA, C_RK, C_LW, C_LB = 15, 19, 23, 27, 31, 35, 39
C_G1, C_GM, C_G2 = 43, 51, 59
NCOLS = 67
K_ID, K_BD, K_M1, K_M2, K_MT, K_MP, K_NMT, K_NMP = 0, 128, 256, 384, 512, 640, 768, 896
NCONST = 1024


class Buf:
    def __init__(self, name):
        self.name = name
        self.w = None
        self.r = {}


class E:
    def __init__(self, name, eng, sem):
        self.name, self.eng, self.sem = name, eng, sem
        self.count = 0
        self.waited = {}
        self.prog = []


class DSem:
    def __init__(self, name, sem):
        self.name, self.sem = name, sem
        self.count = 0


class FW:
    def __init__(self, nc, sems):
        self.nc = nc
        self.pe = E("pe", nc.tensor, sems["pe"])
        self.act = E("act", nc.scalar, sems["act"])
        self.dve = E("dve", nc.vector, sems["dve"])
        self.pool = E("pool", nc.gpsimd, sems["pool"])
        self.sp = E("sp", nc.sync, sems["sp"])
        self.nops = 0

    def _deps(self, reads, writes):
        deps = []
        for b in reads:
            if b.w is not None:
                deps.append(b.w)
        for b in writes:
            if b.w is not None:
                deps.append(b.w)
            deps.extend(b.r.values())
        return deps

    def _wait(self, e, deps):
        best = {}
        for (src, c) in deps:
            if src is e:
                if e.name == "pe":
                    continue
            if e.waited.get(src.name, 0) >= c:
                continue
            if best.get(src.name, (None, 0))[1] < c:
                best[src.name] = (src, c)
        for nm, (src, c) in best.items():
            e.prog.append(lambda src=src, c=c, e=e: e.eng.wait_ge(src.sem, c))
            e.waited[nm] = c

    def _mark(self, me, reads, writes):
        for b in writes:
            b.w = me
            b.r = {}
        for b in reads:
            if b.w is not None and b.w == me:
                continue
            old = b.r.get(me[0].name)
            if old is None or old[1] < me[1]:
                b.r[me[0].name] = me

    def op(self, e, fn, reads=(), writes=()):
        self._wait(e, self._deps(reads, writes))
        e.count += 1
        e.prog.append(lambda fn=fn, e=e: fn().then_inc(e.sem, 1))
        self._mark((e, e.count), reads, writes)
        self.nops += 1

    def dma(self, q, dsem, out, in_, reads=(), writes=()):
        self._wait(q, self._deps(reads, writes))
        q.prog.append(lambda q=q, out=out, in_=in_, dsem=dsem: q.eng.dma_start(out=out, in_=in_).then_inc(dsem.sem, 16))
        dsem.count += 16
        self._mark((dsem, dsem.count), reads, writes)

    def handoff(self, srcs, dsts):
        for d in dsts:
            for s in srcs:
                if s.w is not None:
                    o = d.r.get(s.w[0].name)
                    if o is None or o[1] < s.w[1]:
                        d.r[s.w[0].name] = s.w
                for nm, v in s.r.items():
                    o = d.r.get(nm)
                    if o is None or o[1] < v[1]:
                        d.r[nm] = v

    def run(self, block):
        def mk(e):
            def body(eng):
                for f in e.prog:
                    f()
            return body
        block.tensor(mk(self.pe))
        block.scalar(mk(self.act))
        block.vector(mk(self.dve))
        block.gpsimd(mk(self.pool))
        block.sync(mk(self.sp))


def rev(ap_, n):
    pat = [list(x) for x in ap_.ap]
    step = pat[-1][0]
    pat[-1] = [-step, n]
    return AP(ap_.tensor, ap_.offset + (n - 1) * step, pat)


def piece_table():
    pieces = []
    for f in (1, 2):
        for m in range(NM):
            pieces.append([("g", f, m), ("u", f, m), ("d", f, m)])
    chunks = [("in", c) for c in range(27)]
    for i in range(0, 27, 3):
        pieces.append(chunks[i:i + 3])
    oc = [("o", c) for c in range(8)]
    for i in range(0, 8, 3):
        pieces.append(oc[i:i + 3])
    return pieces


PIECES = piece_table()
NPIECES = len(PIECES)
P_FFN = {1: 0, 2: NM}
P_IN = 2 * NM
P_OUT = 2 * NM + 9

IN_CHUNKS = []
for i in range(4):
    IN_CHUNKS.append((i * 128, 128))
for i in range(4):
    IN_CHUNKS.append((512 + i * 128, 128))
for i in range(4):
    IN_CHUNKS.append((1536 + i * 128, 128))
for i in range(4):
    IN_CHUNKS.append((2048 + i * 128, 128))
for i in range(4):
    IN_CHUNKS.append((2560 + i * 128, 128))
IN_CHUNKS.append((3072, 32))
IN_CHUNKS.append((3104, 32))
IN_CHUNKS.append((3136, 96))
for i in range(4):
    IN_CHUNKS.append((1024 + i * 128, 128))


def host_pieces(inp):
    wp = np.zeros((NPIECES, 128, PE_ELEMS), np.float32)

    def kc(wcols):
        c = wcols.shape[1]
        t = np.zeros((8, 128, 128), np.float32)
        t[:, :, :c] = wcols.reshape(8, 128, c)
        return t.transpose(1, 0, 2).reshape(128, 1024)

    for pi, piece in enumerate(PIECES):
        for si, sub in enumerate(piece):
            kind = sub[0]
            if kind == "g":
                blk = kc(inp[f"ffn{sub[1]}_w_gate"][0][:, sub[2] * 128:(sub[2] + 1) * 128])
            elif kind == "u":
                blk = kc(inp[f"ffn{sub[1]}_w_up"][0][:, sub[2] * 128:(sub[2] + 1) * 128])
            elif kind == "d":
                blk = inp[f"ffn{sub[1]}_w_down"][0][sub[2] * 128:(sub[2] + 1) * 128, :]
            elif kind == "in":
                s, w = IN_CHUNKS[sub[1]]
                blk = kc(inp["w_in"][0][:, s:s + w])
            else:
                blk = inp["w_out"][0][sub[1] * 128:(sub[1] + 1) * 128, :]
            wp[pi, :, si * 1024:(si + 1) * 1024] = blk
    return wp


def host_cols(inp):
    cols = np.zeros((128, NCOLS), np.float32)
    mu = inp["rwkv_mu"][0]

    def c4(v):
        return np.asarray(v).reshape(4, 128).T

    cols[:, C_MU_R:C_MU_R + 4] = c4(mu[0:512])
    cols[:, C_MU_K:C_MU_K + 4] = c4(mu[512:1024])
    cols[:, C_MU_V:C_MU_V + 4] = c4(mu[1024:1536])
    cols[0:32, C_MU_XW] = mu[1536:1568]
    cols[0:32, C_MU_XA] = mu[1568:1600]
    cols[0:96, C_MU_XG] = mu[1600:1696]
    cols[:, C_W0:C_W0 + 4] = c4(inp["rwkv_w0"][0])
    cols[:, C_A0:C_A0 + 4] = c4(inp["rwkv_a0"][0])
    cols[:, C_KK:C_KK + 4] = c4(inp["rwkv_k_k"][0])
    cols[:, C_KA:C_KA + 4] = c4(inp["rwkv_k_a"][0])
    cols[:, C_RK:C_RK + 4] = c4(inp["rwkv_r_k"][0].reshape(512))
    cols[:, C_LW:C_LW + 4] = c4(inp["rwkv_lnx_w"][0])
    cols[:, C_LB:C_LB + 4] = c4(inp["rwkv_lnx_b"][0])
    cols[:, C_G1:C_G1 + 8] = inp["ffn1_norm"][0].reshape(8, 128).T
    cols[:, C_GM:C_GM + 8] = inp["mix_norm"][0].reshape(8, 128).T
    cols[:, C_G2:C_G2 + 8] = inp["ffn2_norm"][0].reshape(8, 128).T
    return cols


def host_consts():
    k = np.zeros((128, NCONST), np.float32)
    i = np.arange(128)
    k[:, K_ID:K_ID + 128] = np.eye(128)
    k[:, K_BD:K_BD + 128] = (i[:, None] // 64 == i[None, :] // 64)
    k[:, K_M1:K_M1 + 128] = (i[:, None] < i[None, :])
    k[:, K_M2:K_M2 + 128] = (i[:, None] <= i[None, :])
    k[:, K_MT:K_MT + 128] = (i[None, :] < i[:, None])
    k[:, K_MP:K_MP + 128] = (i[None, :] >= 112) * np.ones((128, 1))
    k[:, K_NMT:K_NMT + 128] = 1.0 - k[:, K_MT:K_MT + 128]
    k[:, K_NMP:K_NMP + 128] = 1.0 - k[:, K_MP:K_MP + 128]
    return k


def build(ntiles=NTILES_FULL, debug=False):
    nc = bass.Bass("TRN2", target_bir_lowering=False)
    LP = ntiles * 128
    x_d = nc.dram_tensor("x", [SEQ, D], F32, kind="ExternalInput").ap()
    meta_d = nc.dram_tensor("meta", [16, D], F32, kind="ExternalInput").ap()
    wp_d = nc.dram_tensor("wp", [NPIECES, 128, PE_ELEMS], F32, kind="ExternalInput").ap()
    cols_d = nc.dram_tensor("cols", [128, NCOLS], F32, kind="ExternalInput").ap()
    consts_d = nc.dram_tensor("consts", [128, NCONST], F32, kind="ExternalInput").ap()
    gfin_d = nc.dram_tensor("gfin", [128, D], F32, kind="ExternalInput").ap()
    lora_d = nc.dram_tensor("lora", [96, 3, 512], F32, kind="ExternalInput").ap()
    out_d = nc.dram_tensor("out", [SEQ, D], F32, kind="ExternalOutput").ap()
    wbf_d = nc.dram_tensor("wbf", [NPIECES, 128, PE_ELEMS], BF16, kind="Internal").ap()
    if debug:
        dbg_d = nc.dram_tensor("dbg", [8, 128, NTILES_FULL * 128], BF16, kind="ExternalOutput").ap()

    with ExitStack() as st:
        def sb(name, shape, dt):
            return st.enter_context(nc.sbuf_tensor("s_" + name, shape, dt))

        def ps(name, shape, dt):
            return st.enter_context(nc.psum_tensor("p_" + name, shape, dt))

        sems = {k: st.enter_context(nc.semaphore("m_" + k)) for k in ["pe", "act", "dve", "pool", "sp"]}

        def dsem(name):
            return DSem(name, st.enter_context(nc.semaphore(name)))

        fw = FW(nc, sems)
        pe, act, dve, pool, sp = fw.pe, fw.act, fw.dve, fw.pool, fw.sp
        V, S, G, T = nc.vector, nc.scalar, nc.gpsimd, nc.tensor

        def acopy(out, in_, scale=1.0):
            return S.activation(out=out, in_=in_, func=AF.Copy, scale=scale)

        cols = sb("cols", [128, NCOLS], F32)
        consts = sb("consts", [128, NCONST], F32)
        gfin = sb("gfin", [128, D], F32)
        lora = sb("lora", [96, 3, 512], BF16)
        identb = sb("identb", [128, 128], BF16)
        bdones = sb("bdones", [128, 128], BF16)
        ones_f = sb("ones_f", [128, 128], F32)
        zeros_f = sb("zeros_f", [128, 512], BF16)
        KT = sb("KT", [128, 4, NTILES_FULL * 128], BF16)
        KTf = KT[:].rearrange("p a b -> p (a b)").bitcast(F32)
        VS = sb("VS", [128, NTILES_FULL, 512], BF16)
        VSf = VS[:].rearrange("p a b -> p (a b)").bitcast(F32)
        ring = [sb(f"ring{i}", [128, PE_ELEMS], BF16) for i in range(NSLOT)]
        stage = [VSf[:, i * PE_ELEMS:(i + 1) * PE_ELEMS] for i in range(2)] + [KTf[:, i * PE_ELEMS:(i + 1) * PE_ELEMS] for i in range(2)]

        class QV:
            def __init__(self, ap_):
                self.ap_ = ap_

            def __getitem__(self, idx):
                p_, q_, c_ = idx
                return self.ap_[p_, c_]
        B_cols, B_consts, B_gfin, B_lora = Buf("cols"), Buf("consts"), Buf("gfin"), Buf("lora")
        B_identb, B_bdones, B_ones, B_zeros = Buf("identb"), Buf("bdones"), Buf("ones"), Buf("zeros")
        B_KT, B_VS = Buf("KT"), Buf("VS")
        B_ring = [Buf(f"ring{i}") for i in range(NSLOT)]
        B_stage = [Buf(f"stage{i}") for i in range(4)]
        d_ring = [dsem(f"dring{i}") for i in range(NSLOT)]
        d_stage = [dsem(f"dstage{i}") for i in range(4)]
        d_cst = [dsem(f"dcst{i}") for i in range(4)]
        d_cvt = [dsem(f"dcvt{i}") for i in range(NSLOT)]
        d_x = [dsem("dx0"), dsem("dx1")]
        d_out = [dsem("dout0"), dsem("dout1")]
        d_dbg = dsem("ddbg")

        NT = 2
        TT = NT * 128
        hbuf = [sb("h0", [128, NT, D], F32)] * 2
        B_h = [[Buf(f"h_{j}") for j in range(NT)]] * 2
        nb = [sb("nb0", [128, D], BF16)] * 2
        B_nb = [Buf("nb0")] * 2
        nT = sb("nT", [128, 8, TT], BF16)
        B_nT = Buf("nT")
        actT = sb("actT", [128, NM, TT], BF16)
        B_actT = [Buf(f"actT{m}") for m in range(NM)]
        RWT = actT[:].rearrange("p m t -> p (m t)").bitcast(F32)
        sig_t = [sb("sig0", [128, TT], BF16)] * 2
        B_sig = [Buf("sig0")] * 2
        stat = sb("stat", [128, 64], F32)
        B_stat = Buf("stat")
        QT = sb("QT", [128, 4, TT], BF16)
        B_QT = Buf("QT")
        mixT = sb("mixT", [128, 8, TT], BF16)
        B_mixT = [Buf(f"mixT{c}") for c in range(8)]
        carry = sb("carry", [128, 15], F32)
        B_carry = Buf("carry")
        raw = [sb("raw0", [128, TT + 1], F32)] * 2
        B_raw = [Buf("raw0")] * 2
        tmpf = [sb(f"tmpf{i}", [128, TT], F32) for i in range(4)]
        B_tmpf = [Buf(f"tmpf{i}") for i in range(4)]
        MIX = sb("MIX", [128, 4, TT], F32)
        MIXR = sb("MIXR", [128, 4, TT], BF16)
        MIXV = sb("MIXV", [128, 4, TT], BF16)
        B_MIX = [Buf(f"MIX{i}") for i in range(12)]
        xw_t = sb("xw_t", [32, TT], BF16)
        xa_t = sb("xa_t", [32, TT], BF16)
        xg_t = sb("xg_t", [96, TT], BF16)
        B_xw, B_xa, B_xg = Buf("xw"), Buf("xa"), Buf("xg")
        A_T = QV(RWT[:, 0*256:1*256])
        B_A = [Buf("A")] * 4
        SG = QV(RWT[:, 1*256:2*256])
        B_SG = [Buf("SG")] * 4
        CS = QV(RWT[:, 2*256:3*256])
        B_CS = [Buf("CS")] * 4
        KKT = QV(RWT[:, 3*256:4*256])
        B_KKT = [Buf("KKT")] * 4
        KTL = QV(RWT[:, 4*256:5*256])
        B_KTL = [Buf("KTL")] * 4
        GG = RWT[:, 7*256:7*256+1024].rearrange("p (q t) -> p q t", q=4)
        GI = QV(RWT[:, 5*256:6*256])
        GP = QV(RWT[:, 6*256:7*256])
        B_GG = [Buf(f"GG{i}") for i in range(4)]
        B_GI = [Buf("GI")] * 4
        B_GP = [Buf("GP")] * 4
        UNI = sb("UNI", [128, 7680], F32)
        UNIb = UNI[:].bitcast(BF16)
        OPS = UNIb[:, 0:6144].rearrange("p (q j s t) -> p q j s t", q=4, j=NT, s=6)
        B_OPS = [[Buf(f"OPS{q}_{i}") for i in range(NT)] for q in range(4)]
        TOK = sb("TOK", [128, 2, 128], BF16)
        B_TOK = Buf("TOK")
        VTOK = sb("VTOK", [128, NT, 512], BF16)
        B_VTOK = [Buf(f"VTOK{i}") for i in range(NT)]
        BONT = sb("BONT", [128, 4, TT], BF16)
        B_BONT = [Buf(f"BONT{i}") for i in range(4)]
        GT = sb("GT", [128, 4, TT], BF16)
        B_GT = [Buf(f"GT{i}") for i in range(4)]
        AM = UNIb[:, 6144:10240].rearrange("p (a h b t) -> p a h b t", a=4, h=2, b=2)
        B_AM = [[Buf(f"AM{s_}_{h}") for h in range(2)] for s_ in range(4)]
        NCH = UNIb[:, 10240:14336].rearrange("p (s b h a t) -> p s b h a t", s=4, b=2, h=2, a=2)
        B_NCH = [[Buf(f"NCH{s_}_{b}") for b in range(2)] for s_ in range(4)]
        XCH = UNIb[:, 14336:15360].rearrange("p (s b t) -> p s b t", s=4, b=2)
        B_XCH = [[Buf(f"XCH{s_}_{b}") for b in range(2)] for s_ in range(4)]
        Pbf = sb("Pbf", [128, 4, 128], BF16)
        B_Pbf = [Buf(f"Pbf{i}") for i in range(4)]

        S32 = sb("S32", [128, 4, 128], F32)
        SBD = sb("SBD", [128, 4, 128], BF16)
        B_S32 = [Buf(f"S32_{q}") for q in range(4)]
        B_SBD = [Buf(f"SBD_{q}") for q in range(4)]
        YNB = sb("YNB", [128, 512], BF16)
        B_YNB = Buf("YNB")
        identf = consts[:, K_ID:K_ID + 128]
        bb_t = sb("bb", [128, 1024], F32)
        beta = [bb_t[:, 0:512], bb_t[:, 512:1024]]
        ybuf = [bb_t, bb_t]
        om = [sb(f"om{i}", [128, 512], F32)[:] for i in range(2)]
        RB = [sb(f"RB{i}", [128, 513], F32)[:] for i in range(2)]
        attn = [sb(f"attn{i}", [128, 512], BF16)[:] for i in range(2)]
        attnT = [sb(f"attnT{i}", [128, 4, 128], BF16)[:] for i in range(2)]
        NSETS = 6
        for k_ in range(NSETS - 2):
            base = k_ * 1538
            om.append(UNI[:, base:base + 512])
            RB.append(UNI[:, base + 512:base + 1025])
            attn.append(UNIb[:, 2 * (base + 1026):2 * (base + 1026) + 512])
            attnT.append(UNIb[:, 2 * (base + 1282):2 * (base + 1282) + 512].rearrange("p (b t) -> p b t", b=4))
        B_beta = [Buf(f"beta{i}") for i in range(5)]
        YN = beta[1]
        B_YN = B_beta[1]
        B_om = [Buf(f"om{i}") for i in range(6)]
        B_RB = [Buf(f"RB{i}") for i in range(6)]
        B_attn = [Buf(f"attn{i}") for i in range(6)]
        B_attnT = [Buf(f"attnT{i}") for i in range(6)]

        PS = [ps(f"ps{i}", [128, 512], F32) for i in range(8)]
        B_PS = [Buf(f"ps{i}") for i in range(8)]
        RG_ = {"mm1": B_PS[0], "mm3": B_PS[0], "mm2": B_PS[1], "state": B_PS[1], "Y": B_PS[2], "X0": B_PS[3], "X1": B_PS[3], "tok": B_PS[3],
               "ch0": B_PS[4], "ch1": B_PS[5], "sq0": B_PS[6], "sq1": B_PS[7]}

        def psb(i):
            return PS[i][:].bitcast(BF16)

        block = st.enter_context(nc.Block())

        fw.dma(sp, d_cst[0], cols[:], cols_d[:, :], writes=[B_cols])
        fw.dma(sp, d_cst[1], consts[:], consts_d[:, :], writes=[B_consts])
        fw.dma(sp, d_cst[2], gfin[:], gfin_d[:, :], writes=[B_gfin])
        fw.op(dve, lambda: V.tensor_copy(out=identb[:], in_=consts[:, K_ID:K_ID + 128]), reads=[B_consts], writes=[B_identb])
        fw.op(dve, lambda: V.tensor_copy(out=bdones[:], in_=consts[:, K_BD:K_BD + 128]), reads=[B_consts], writes=[B_bdones])
        for li in range(3):
            fw.dma(sp, d_cst[3], bb_t[0:96, 0:512], lora_d[:, li, :], writes=[B_beta[0]])
            fw.op(dve, lambda li=li: V.tensor_copy(out=lora[:, li, :], in_=bb_t[0:96, 0:512]), reads=[B_beta[0]], writes=[B_lora])
        fw.op(pool, lambda: G.memset(ones_f[:], 1.0), writes=[B_ones])
        fw.op(pool, lambda: G.memset(zeros_f[:], 0.0), writes=[B_zeros])
        fw.op(pool, lambda: G.memset(S32[:], 0.0), writes=B_S32)
        fw.op(pool, lambda: G.memset(SBD[:], 0.0), writes=B_SBD)
        fw.op(pool, lambda: G.memset(carry[:], 0.0), writes=[B_carry])

        cast_engs = [dve, act, pool]
        for pi, piece in enumerate(PIECES):
            sbuf_i = pi % 4
            fw.dma(sp, d_stage[sbuf_i], stage[sbuf_i], wp_d[pi, :, :], writes=[B_stage[sbuf_i]])
            slot = pi % NSLOT
            for si, sub in enumerate(piece):
                kind = sub[0]
                src = stage[sbuf_i][:, si * 1024:(si + 1) * 1024]
                dst = ring[slot][:, si * 1024:(si + 1) * 1024]
                gc = None
                if kind in ("g", "u"):
                    gc = C_G1 if sub[1] == 1 else C_G2
                elif kind == "in":
                    gc = C_GM
                if gc is not None:
                    fw.op(dve, lambda src=src, dst=dst, gc=gc: V.tensor_tensor(
                        out=dst.rearrange("p (k c) -> p k c", k=8), in0=src.rearrange("p (k c) -> p k c", k=8),
                        in1=cols[:, gc:gc + 8].unsqueeze(2).to_broadcast([128, 8, 128]), op=ALU.mult),
                        reads=[B_stage[sbuf_i], B_cols], writes=[B_ring[slot]])
                else:
                    e = act if (pi + si) % 2 == 0 else pool
                    if e is act:
                        fw.op(act, lambda src=src, dst=dst: acopy(out=dst, in_=src), reads=[B_stage[sbuf_i]], writes=[B_ring[slot]])
                    else:
                        fw.op(pool, lambda src=src, dst=dst: G.tensor_copy(out=dst, in_=src), reads=[B_stage[sbuf_i]], writes=[B_ring[slot]])
            ne = len(piece) * 1024
            fw.dma(sp, d_cvt[slot], wbf_d[pi, :, 0:ne], ring[slot][:, 0:ne], reads=[B_ring[slot]])
        for dc in d_cvt:
            sp.prog.append(lambda dc=dc, c=dc.count: sp.eng.wait_ge(dc.sem, c))
            sp.waited[dc.name] = dc.count
        fw.handoff(B_stage[0:2], [B_VS])
        fw.handoff(B_stage[2:4], [B_KT])

        stream_state = {"next": 0, "order": []}

        def prefetch(pi):
            slot = stream_state["next"] % NSLOT
            stream_state["next"] += 1
            ne = len(PIECES[pi]) * 1024
            fw.dma(sp, d_ring[slot], ring[slot][:, 0:ne], wbf_d[pi, :, 0:ne], writes=[B_ring[slot]])
            return slot

        class Stream:
            def __init__(self, order):
                self.order = order
                self.slots = {}
                self.issued = 0
                self.used = 0

            def ensure(self, upto):
                while self.issued < min(upto, len(self.order)):
                    self.slots[self.issued] = prefetch(self.order[self.issued])
                    self.issued += 1

            def get(self):
                i = self.used
                self.ensure(i + 1)
                slot = self.slots[i]
                self.used += 1
                return slot

            def after_use(self):
                self.ensure(self.used + NSLOT - 1)

        def load_h(sti, tiles, hb):
            for j, gi in enumerate(tiles):
                if gi == 0:
                    fw.op(pool, lambda hb=hb, j=j: G.memset(hbuf[hb][:, j, :], 0.0), writes=[B_h[hb][j]])
                    fw.dma(sp, d_x[j], hbuf[hb][112:128, j, :], meta_d[:, :], writes=[B_h[hb][j]])
                else:
                    fw.dma(sp, d_x[j], hbuf[hb][:, j, :], x_d[(gi - 1) * 128:gi * 128, :], writes=[B_h[hb][j]])

        def norm_T(hb, nt):
            for j in range(nt):
                k = j % 2
                sc = stat[:, 2 * j:2 * j + 1]
                sc2 = stat[:, 2 * j + 1:2 * j + 2]
                fw.op(act, lambda j=j, sc=sc: S.activation(out=beta[1][:].bitcast(BF16), in_=hbuf[hb][:, j, :], func=AF.Square, accum_out=sc),
                      reads=[B_h[hb][j]], writes=[B_beta[1], B_stat])
                fw.op(act, lambda sc=sc: S.activation(out=sc, in_=sc, func=AF.Sqrt, scale=1.0 / D, bias=RMS_EPS), reads=[B_stat], writes=[B_stat])
                fw.op(dve, lambda sc=sc, sc2=sc2: V.reciprocal(out=sc2, in_=sc), reads=[B_stat], writes=[B_stat])
                fw.op(act, lambda j=j, k=k, sc2=sc2: S.activation(out=nb[k][:], in_=hbuf[hb][:, j, :], func=AF.Copy, scale=sc2),
                      reads=[B_h[hb][j], B_stat], writes=[B_nb[k]])
                pb = 6 + k
                for c in range(8):
                    fw.op(pe, lambda c=c, k=k, pb=pb: T.transpose(out=psb(pb)[:, c * 128:(c + 1) * 128], in_=nb[k][:, c * 128:(c + 1) * 128],
                                                                   identity=identb[:]),
                          reads=[B_nb[k], B_identb], writes=[B_PS[pb]])
                fw.op(dve, lambda j=j, pb=pb: V.tensor_copy(out=nT[:, :, j * 128:(j + 1) * 128],
                                                            in_=psb(pb).rearrange("p (c t) -> p c t", c=8)),
                      reads=[B_PS[pb]], writes=[B_nT])

        def ffn(f, hb, nt, strm):
            W = nt * 128
            acc = [4, 5, 6, 7]
            pend = []

            def down(m, slot):
                for j in range(nt):
                    for half in range(2):
                        a = acc[2 * j + half]
                        fw.op(pe, lambda m=m, j=j, half=half, a=a, slot=slot: T.matmul(
                            PS[a][:, :], lhsT=actT[:, m, j * 128:(j + 1) * 128], rhs=ring[slot][:, 2048 + half * 512:2048 + (half + 1) * 512],
                            start=(m == 0), stop=(m == NM - 1)), reads=[B_actT[m], B_ring[slot]], writes=[B_PS[a]])

            for m in range(NM):
                slot = strm.get()
                gb, ub = (0, 1) if m % 2 == 0 else (2, 3)
                for k in range(8):
                    fw.op(pe, lambda k=k, slot=slot, gb=gb: T.matmul(PS[gb][:, 0:W], lhsT=ring[slot][:, k * 128:(k + 1) * 128], rhs=nT[:, k, 0:W],
                                                                     start=(k == 0), stop=(k == 7)),
                          reads=[B_ring[slot], B_nT], writes=[B_PS[gb]])
                for k in range(8):
                    fw.op(pe, lambda k=k, slot=slot, ub=ub: T.matmul(PS[ub][:, 0:W], lhsT=ring[slot][:, 1024 + k * 128:1024 + (k + 1) * 128],
                                                                     rhs=nT[:, k, 0:W], start=(k == 0), stop=(k == 7)),
                          reads=[B_ring[slot], B_nT], writes=[B_PS[ub]])
                sg = m % 2
                fw.op(act, lambda gb=gb, sg=sg: S.activation(out=sig_t[sg][:, 0:W], in_=PS[gb][:, 0:W], func=AF.Silu),
                      reads=[B_PS[gb]], writes=[B_sig[sg]])
                fw.op(dve, lambda m=m, ub=ub, sg=sg: V.tensor_tensor(out=actT[:, m, 0:W], in0=PS[ub][:, 0:W], in1=sig_t[sg][:, 0:W], op=ALU.mult),
                      reads=[B_PS[ub], B_sig[sg]], writes=[B_actT[m]])
                pend.append((m, slot))
                if len(pend) > 1:
                    down(*pend.pop(0))
                    strm.after_use()
            while pend:
                down(*pend.pop(0))
                strm.after_use()
            for j in range(nt):
                for half in range(2):
                    a = acc[2 * j + half]
                    fw.op(dve, lambda j=j, half=half, a=a: V.scalar_tensor_tensor(
                        out=hbuf[hb][:, j, half * 512:(half + 1) * 512], in0=PS[a][:, :], scalar=0.5,
                        in1=hbuf[hb][:, j, half * 512:(half + 1) * 512], op0=ALU.mult, op1=ALU.add),
                        reads=[B_PS[a], B_h[hb][j]], writes=[B_h[hb][j]])

        def col(c, n=128):
            return cols[0:n, c:c + 1]

        def proj(tiles, nt, strm):
            W = nt * 128
            t0 = tiles[0] * 128
            slot = None
            for ci in range(27):
                if ci % 3 == 0:
                    if slot is not None:
                        strm.after_use()
                    slot = strm.get()
                off = (ci % 3) * 1024
                pb = ci % 4
                if ci < 23:
                    for k in range(8):
                        fw.op(pe, lambda k=k, slot=slot, off=off, pb=pb: T.matmul(
                            PS[pb][:, 0:W], lhsT=ring[slot][:, off + k * 128:off + (k + 1) * 128], rhs=nT[:, k, 0:W], start=(k == 0), stop=(k == 7)),
                            reads=[B_ring[slot], B_nT], writes=[B_PS[pb]])
                    if ci < 4:
                        fw.op(act, lambda ci=ci, pb=pb: acopy(out=QT[:, ci, 0:W], in_=PS[pb][:, 0:W], scale=0.125), reads=[B_PS[pb]], writes=[B_QT])
                    elif ci < 8:
                        fw.op(act, lambda ci=ci, pb=pb: acopy(out=KT[:, ci - 4, t0:t0 + W], in_=PS[pb][:, 0:W]), reads=[B_PS[pb]], writes=[B_KT])
                    else:
                        ri = ci - 8
                        rb = ri % 2
                        if ri < 4:
                            npart, mu_c, dst, dbuf = 128, C_MU_R + ri, MIXR[:, ri, 0:W], B_MIX[ri]
                        elif ri < 8:
                            npart, mu_c, dst, dbuf = 128, C_MU_R + ri, MIX[:, ri - 4, 0:W], B_MIX[ri]
                        elif ri < 12:
                            npart, mu_c, dst, dbuf = 128, C_MU_R + ri, MIXV[:, ri - 8, 0:W], B_MIX[ri]
                        elif ri == 12:
                            npart, mu_c, dst, dbuf = 32, C_MU_XW, xw_t[:, 0:W], B_xw
                        elif ri == 13:
                            npart, mu_c, dst, dbuf = 32, C_MU_XA, xa_t[:, 0:W], B_xa
                        else:
                            npart, mu_c, dst, dbuf = 96, C_MU_XG, xg_t[:, 0:W], B_xg
                        P_ = slice(0, npart)
                        fw.op(act, lambda rb=rb, pb=pb, P_=P_: acopy(out=raw[rb][P_, 1:W + 1], in_=PS[pb][P_, 0:W]), reads=[B_PS[pb]], writes=[B_raw[rb]])
                        fw.op(pool, lambda rb=rb, ri=ri, P_=P_: G.tensor_copy(out=raw[rb][P_, 0:1], in_=carry[P_, ri:ri + 1]),
                              reads=[B_carry], writes=[B_raw[rb]])
                        fw.op(pool, lambda rb=rb, ri=ri, P_=P_: G.tensor_copy(out=carry[P_, ri:ri + 1], in_=raw[rb][P_, W:W + 1]),
                              reads=[B_raw[rb]], writes=[B_carry])
                        tb = ri % 4
                        fw.op(dve, lambda rb=rb, tb=tb, P_=P_: V.tensor_tensor(out=tmpf[tb][P_, 0:W], in0=raw[rb][P_, 0:W], in1=raw[rb][P_, 1:W + 1],
                                                                               op=ALU.subtract), reads=[B_raw[rb]], writes=[B_tmpf[tb]])
                        fw.op(dve, lambda rb=rb, tb=tb, P_=P_, mu_c=mu_c, dst=dst, npart=npart: V.scalar_tensor_tensor(
                            out=dst, in0=tmpf[tb][P_, 0:W], scalar=col(mu_c, npart), in1=raw[rb][P_, 1:W + 1], op0=ALU.mult, op1=ALU.add),
                            reads=[B_tmpf[tb], B_raw[rb], B_cols], writes=[dbuf])
                else:
                    vi = ci - 23
                    for j in range(nt):
                        pbv = (ci + j) % 4
                        for k in range(8):
                            fw.op(pe, lambda k=k, j=j, slot=slot, off=off, pbv=pbv: T.matmul(
                                PS[pbv][:, 0:128], lhsT=nT[:, k, j * 128:(j + 1) * 128], rhs=ring[slot][:, off + k * 128:off + (k + 1) * 128],
                                start=(k == 0), stop=(k == 7)), reads=[B_ring[slot], B_nT], writes=[B_PS[pbv]])
                        gi = tiles[j]
                        fw.op(act, lambda gi=gi, vi=vi, pbv=pbv: acopy(out=VS[:, gi, vi * 128:(vi + 1) * 128], in_=PS[pbv][:, 0:128]),
                              reads=[B_PS[pbv]], writes=[B_VS])
            strm.after_use()

        def attention(tiles, nt):
            jobs = []
            for j, gi in enumerate(tiles):
                for q in range(4):
                    for hp in range(2):
                        nchunks = (gi + 1 + 3) // 4
                        for c in range(nchunks):
                            jobs.append((j, gi, q, hp, c, nchunks))

            SKEW = NSETS - 2
            uni_rw = [b_ for l_ in B_OPS for b_ in l_] + [b_ for l_ in B_AM for b_ in l_] + [b_ for l_ in B_NCH for b_ in l_] + [b_ for l_ in B_XCH for b_ in l_]
            uni_at = []
            for k_ in range(2, NSETS):
                uni_at += [B_om[k_], B_RB[k_], B_attn[k_], B_attnT[k_]]
            fw.handoff(uni_rw, uni_at)

            def stage_a(idx):
                j, gi, q, hp, c, nchunks = jobs[idx]
                bb = idx % NSETS
                zb = [0, 1, 6, 7][idx % 4]
                R_ = slice(hp * 64, (hp + 1) * 64)
                hi = gi - 4 * c
                lo = max(0, hi - 3)
                w = (hi - lo + 1) * 128
                fw.op(pe, lambda: T.matmul(PS[zb][:, 0:w], lhsT=QT[R_, q, j * 128:(j + 1) * 128], rhs=KT[R_, q, lo * 128:lo * 128 + w], start=True, stop=True),
                      reads=[B_QT, B_KT], writes=[B_PS[zb]])
                fw.op(act, lambda: S.activation(out=om[bb][:, 0:w], in_=PS[zb][:, 0:w], func=AF.Sigmoid, scale=-1.0), reads=[B_PS[zb]], writes=[B_om[bb]])
                masks = []
                if c == 0:
                    masks.append((w - 128, K_NMT))
                if lo == 0:
                    masks.append((0, K_NMP))
                for (o_, mk) in masks:
                    fw.op(dve, lambda o_=o_, mk=mk: V.tensor_tensor(out=om[bb][:, o_:o_ + 128], in0=om[bb][:, o_:o_ + 128],
                                                                    in1=consts[:, mk:mk + 128], op=ALU.max),
                          reads=[B_om[bb], B_consts], writes=[B_om[bb]])
                if c == 0:
                    init = 1.0
                    rd = [B_om[bb], B_zeros]
                else:
                    pr = (idx - 1) % NSETS
                    init = RB[pr][:, 0:1]
                    rd = [B_om[bb], B_zeros, B_RB[pr]]
                fw.op(dve, lambda: V.tensor_tensor_scan(out=rev(RB[bb][:, 0:w], w), data0=rev(om[bb][:, 0:w], w), data1=rev(zeros_f[:, 0:w], w),
                                                        initial=init, op0=ALU.mult, op1=ALU.add), reads=rd, writes=[B_RB[bb]])
                fw.op(pool, lambda: G.tensor_tensor(out=attn[bb][:, 0:w - 1], in0=RB[bb][:, 1:w], in1=RB[bb][:, 0:w - 1], op=ALU.subtract),
                      reads=[B_RB[bb]], writes=[B_attn[bb]])
                if c == 0:
                    fw.op(pool, lambda: G.tensor_scalar(out=attn[bb][:, w - 1:w], in0=RB[bb][:, w - 1:w], scalar1=-1.0, scalar2=1.0, op0=ALU.mult, op1=ALU.add),
                          reads=[B_RB[bb]], writes=[B_attn[bb]])
                else:
                    fw.op(pool, lambda: G.tensor_tensor(out=attn[bb][:, w - 1:w], in0=RB[pr][:, 0:1], in1=RB[bb][:, w - 1:w], op=ALU.subtract),
                          reads=[B_RB[bb], B_RB[pr]], writes=[B_attn[bb]])

            def stage_b(idx):
                j, gi, q, hp, c, nchunks = jobs[idx]
                bb = idx % NSETS
                tb = 2 + (idx % 2)
                ob = 4 + (q % 2)
                R_ = slice(hp * 64, (hp + 1) * 64)
                hi = gi - 4 * c
                lo = max(0, hi - 3)
                nb_ = hi - lo + 1
                w = nb_ * 128
                for b_ in range(nb_):
                    fw.op(pe, lambda b_=b_: T.transpose(out=psb(tb)[:, b_ * 128:(b_ + 1) * 128], in_=attn[bb][:, b_ * 128:(b_ + 1) * 128], identity=identb[:]),
                          reads=[B_attn[bb], B_identb], writes=[B_PS[tb]])
                fw.op(act, lambda: acopy(out=attnT[bb].rearrange("p b t -> p (b t)")[:, 0:w], in_=psb(tb)[:, 0:w]),
                      reads=[B_PS[tb]], writes=[B_attnT[bb]])

            def stage_b2(idx):
                j, gi, q, hp, c, nchunks = jobs[idx]
                bb = idx % NSETS
                ob = 4 + (q % 2)
                R_ = slice(hp * 64, (hp + 1) * 64)
                hi = gi - 4 * c
                lo = max(0, hi - 3)
                nb_ = hi - lo + 1
                for b_ in range(nb_):
                    first = (c == 0) and (b_ == 0)
                    last = (c == nchunks - 1) and (b_ == nb_ - 1)
                    fw.op(pe, lambda b_=b_, first=first, last=last: T.matmul(
                        PS[ob][R_, 0:128], lhsT=VS[:, lo + b_, q * 128 + hp * 64:q * 128 + (hp + 1) * 64], rhs=attnT[bb][:, b_, :],
                        start=first, stop=last), reads=[B_VS, B_attnT[bb]], writes=[B_PS[ob]])
                if hp == 1 and c == nchunks - 1:
                    fw.op(act, lambda: acopy(out=mixT[:, q, j * 128:(j + 1) * 128], in_=PS[ob][:, 0:128]), reads=[B_PS[ob]], writes=[B_mixT[q]])

            for i in range(len(jobs) + SKEW + 1):
                if i < len(jobs):
                    stage_a(i)
                if SKEW <= i < len(jobs) + SKEW:
                    stage_b(i - SKEW)
                if i >= SKEW + 1:
                    stage_b2(i - SKEW - 1)
            fw.handoff(uni_at, uni_rw)


        def rwkv(tiles, nt, full):
            W = nt * 128
            lw_up, la_up, lg_up = lora[0:32, 0, :], lora[0:32, 1, :], lora[0:96, 2, :]
            fw.op(act, lambda: S.activation(out=xw_t[:, 0:W], in_=xw_t[:, 0:W], func=AF.Tanh), reads=[B_xw], writes=[B_xw])
            fw.op(act, lambda: S.activation(out=xg_t[:, 0:W], in_=xg_t[:, 0:W], func=AF.Sigmoid), reads=[B_xg], writes=[B_xg])
            def rr_emit(chains):
                while any(chains):
                    for c_ in chains:
                        if c_:
                            fw.op(*c_.pop(0))

            for q in range(4):
                Q_ = slice(q * 128, (q + 1) * 128)
                kmix = MIX[:, q, 0:W]
                rmix = MIXR[:, q, 0:W]
                c1, c2, c3, c4 = [], [], [], []
                c1.append((pe, lambda Q_=Q_: T.matmul(PS[0][:, 0:W], lhsT=lw_up[:, Q_], rhs=xw_t[:, 0:W], start=True, stop=True),
                           [B_lora, B_xw], [B_PS[0]]))
                c1.append((act, lambda q=q: S.activation(out=SG[:, q, 0:W], in_=PS[0][:, 0:W], func=AF.Sigmoid, bias=col(C_W0 + q)),
                           [B_PS[0], B_cols], [B_SG[q]]))
                for j in range(nt):
                    J_ = slice(j * 128, (j + 1) * 128)
                    c1.append((dve, lambda q=q, J_=J_: V.tensor_tensor_scan(out=CS[:, q, J_], data0=ones_f[:, 0:128], data1=SG[:, q, J_], initial=0.0,
                                                                             op0=ALU.mult, op1=ALU.add), [B_SG[q], B_ones], [B_CS[q]]))
                c1.append((act, lambda q=q: S.activation(out=GG[:, q, 0:W], in_=CS[:, q, 0:W], func=AF.Exp, scale=-DECAY_C), [B_CS[q]], [B_GG[q]]))
                c1.append((act, lambda q=q: S.activation(out=GI[:, q, 0:W], in_=CS[:, q, 0:W], func=AF.Exp, scale=DECAY_C), [B_CS[q]], [B_GI[q]]))
                c1.append((pool, lambda q=q: G.tensor_tensor(out=tmpf[0][:, 0:W], in0=CS[:, q, 0:W], in1=SG[:, q, 0:W], op=ALU.subtract),
                           [B_CS[q], B_SG[q]], [B_tmpf[0]]))
                c1.append((act, lambda q=q: S.activation(out=GP[:, q, 0:W], in_=tmpf[0][:, 0:W], func=AF.Exp, scale=-DECAY_C), [B_tmpf[0]], [B_GP[q]]))
                c2.append((pe, lambda Q_=Q_: T.matmul(PS[1][:, 0:W], lhsT=la_up[:, Q_], rhs=xa_t[:, 0:W], start=True, stop=True),
                           [B_lora, B_xa], [B_PS[1]]))
                c2.append((act, lambda q=q: S.activation(out=A_T[:, q, 0:W], in_=PS[1][:, 0:W], func=AF.Sigmoid, bias=col(C_A0 + q)),
                           [B_PS[1], B_cols], [B_A[q]]))
                c2.append((dve, lambda q=q: V.tensor_scalar(out=tmpf[3][:, 0:W], in0=A_T[:, q, 0:W], scalar1=-1.0, scalar2=col(C_KA + q),
                                                            op0=ALU.add, op1=ALU.mult), [B_A[q], B_cols], [B_tmpf[3]]))
                c2.append((dve, lambda q=q, kmix=kmix: V.scalar_tensor_tensor(out=KTL[:, q, 0:W], in0=tmpf[3][:, 0:W], scalar=1.0, in1=kmix,
                                                                              op0=ALU.add, op1=ALU.mult), [B_tmpf[3], B_MIX[4 + q]], [B_KTL[q]]))
                c2.append((dve, lambda q=q, rmix=rmix: V.scalar_tensor_tensor(out=nT[:, 1, 0:W], in0=rmix, scalar=col(C_RK + q), in1=KTL[:, q, 0:W],
                                                                              op0=ALU.mult, op1=ALU.mult), [B_MIX[q], B_KTL[q], B_cols], [B_nT]))
                c2.append((pe, lambda: T.matmul(PS[3][:, 256:256 + W], lhsT=bdones[:], rhs=nT[:, 1, 0:W], start=True, stop=True),
                           [B_bdones, B_nT], [B_PS[3]]))
                c2.append((dve, lambda q=q: V.tensor_tensor(out=BONT[:, q, 0:W], in0=PS[3][:, 256:256 + W], in1=MIXV[:, q, 0:W], op=ALU.mult),
                           [B_PS[3], B_MIX[8 + q]], [B_BONT[q]]))
                c3.append((pe, lambda Q_=Q_: T.matmul(PS[2][:, 0:W], lhsT=lg_up[:, Q_], rhs=xg_t[:, 0:W], start=True, stop=True),
                           [B_lora, B_xg], [B_PS[2]]))
                c3.append((act, lambda q=q: acopy(out=GT[:, q, 0:W], in_=PS[2][:, 0:W]), [B_PS[2]], [B_GT[q]]))
                c4.append((dve, lambda q=q, kmix=kmix: V.tensor_scalar(out=KKT[:, q, 0:W], in0=kmix, scalar1=col(C_KK + q), scalar2=None, op0=ALU.mult),
                           [B_MIX[4 + q], B_cols], [B_KKT[q]]))
                c4.append((pool, lambda q=q: G.tensor_tensor(out=nT[:, 0, 0:W], in0=KKT[:, q, 0:W], in1=KKT[:, q, 0:W], op=ALU.mult),
                           [B_KKT[q]], [B_nT]))
                c4.append((pe, lambda: T.matmul(PS[3][:, 0:W], lhsT=bdones[:], rhs=nT[:, 0, 0:W], start=True, stop=True),
                           [B_bdones, B_nT], [B_PS[3]]))
                c4.append((act, lambda: S.activation(out=tmpf[1][:, 0:W], in_=PS[3][:, 0:W], func=AF.Sqrt), [B_PS[3]], [B_tmpf[1]]))
                c4.append((dve, lambda: V.tensor_scalar(out=tmpf[1][:, 0:W], in0=tmpf[1][:, 0:W], scalar1=1e-12, scalar2=None, op0=ALU.max),
                           [B_tmpf[1]], [B_tmpf[1]]))
                c4.append((dve, lambda: V.reciprocal(out=tmpf[2][:, 0:W], in_=tmpf[1][:, 0:W]), [B_tmpf[1]], [B_tmpf[2]]))
                c4.append((dve, lambda q=q: V.tensor_tensor(out=KKT[:, q, 0:W], in0=KKT[:, q, 0:W], in1=tmpf[2][:, 0:W], op=ALU.mult),
                           [B_KKT[q], B_tmpf[2]], [B_KKT[q]]))
                rr_emit([c1, c4, c2, c3])
                oc = []
                for j in range(nt):
                    J_ = slice(j * 128, (j + 1) * 128)
                    ob_ = B_OPS[q][j]
                    tq = tmpf[1] if j == 0 else tmpf[2]
                    bq = B_tmpf[1] if j == 0 else B_tmpf[2]
                    c_ = []
                    c_.append((dve, lambda q=q, j=j, J_=J_: V.scalar_tensor_tensor(out=OPS[:, q, j, 0, :], in0=KKT[:, q, J_], scalar=-1.0, in1=GP[:, q, J_],
                                                                                    op0=ALU.mult, op1=ALU.mult), [B_KKT[q], B_GP[q]], [ob_]))
                    c_.append((pool, lambda q=q, j=j, J_=J_: G.tensor_tensor(out=OPS[:, q, j, 1, :], in0=MIXR[:, q, J_], in1=GG[:, q, J_], op=ALU.mult),
                               [B_MIX[q], B_GG[q]], [ob_]))
                    c_.append((pool, lambda q=q, j=j, J_=J_: G.tensor_tensor(out=OPS[:, q, j, 2, :], in0=KTL[:, q, J_], in1=GI[:, q, J_], op=ALU.mult),
                               [B_KTL[q], B_GI[q]], [ob_]))
                    c_.append((pool, lambda q=q, J_=J_, tq=tq: G.tensor_tensor(out=tq[:, 0:128], in0=KKT[:, q, J_], in1=A_T[:, q, J_], op=ALU.mult),
                               [B_KKT[q], B_A[q]], [bq]))
                    c_.append((pool, lambda q=q, j=j, J_=J_, tq=tq: G.tensor_tensor(out=OPS[:, q, j, 3, :], in0=tq[:, 0:128], in1=GI[:, q, J_], op=ALU.mult),
                               [bq, B_GI[q]], [ob_]))
                    gl = GG[:, q, j * 128 + 127:j * 128 + 128]
                    c_.append((dve, lambda q=q, j=j, gl=gl: V.tensor_scalar(out=OPS[:, q, j, 4:6, :], in0=OPS[:, q, j, 2:4, :], scalar1=gl, scalar2=None,
                                                                            op0=ALU.mult), [ob_, B_GG[q]], [ob_]))
                    oc.append(c_)
                rr_emit(oc)
            for j in range(nt):
                for q in range(4):
                    fw.op(pe, lambda q=q, j=j: T.transpose(out=psb(0)[:, q * 128:(q + 1) * 128], in_=MIXV[:, q, j * 128:(j + 1) * 128], identity=identb[:]),
                          reads=[B_MIX[8 + q], B_identb], writes=[B_PS[0]])
                fw.op(act, lambda j=j: acopy(out=VTOK[:, j, :], in_=psb(0)[:, 0:512]), reads=[B_PS[0]], writes=[B_VTOK[j]])
            for j, gi in enumerate(tiles):
                for q in range(4):
                    sl = q
                    ob_ = B_OPS[q][j]
                    for hp in range(2):
                        R_ = slice(hp * 64, (hp + 1) * 64)
                        fw.op(pe, lambda R_=R_, q=q, j=j: T.matmul(PS[0][:, 0:256], lhsT=OPS[R_, q, j, 2, :], rhs=OPS[R_, q, j, 0:2, :].rearrange("p a t -> p (a t)"),
                                                                   start=True, stop=True), reads=[ob_], writes=[B_PS[0]])
                        fw.op(pe, lambda R_=R_, q=q, j=j: T.matmul(PS[1][:, 0:256], lhsT=OPS[R_, q, j, 3, :], rhs=OPS[R_, q, j, 0:2, :].rearrange("p a t -> p (a t)"),
                                                                   start=True, stop=True), reads=[ob_], writes=[B_PS[1]])
                        fw.op(pe, lambda R_=R_, q=q, j=j: T.matmul(PS[0][:, 256:384], lhsT=OPS[R_, q, j, 0, :], rhs=OPS[R_, q, j, 3, :],
                                                                   start=True, stop=True), reads=[ob_], writes=[B_PS[0]])
                        fw.op(dve, lambda hp=hp, sl=sl: V.tensor_tensor(out=AM[:, sl, hp, 0, :], in0=PS[0][:, 0:256], in1=consts[:, K_M1:K_M1 + 256], op=ALU.mult),
                              reads=[B_PS[0], B_consts], writes=[B_AM[sl][hp]])
                        fw.op(dve, lambda hp=hp, sl=sl: V.tensor_tensor(out=AM[:, sl, hp, 1, :], in0=PS[1][:, 0:256], in1=consts[:, K_M1:K_M1 + 256], op=ALU.mult),
                              reads=[B_PS[1], B_consts], writes=[B_AM[sl][hp]])
                        fw.op(act, lambda hp=hp, sl=sl: acopy(out=NCH[:, sl, 0, hp, 0, :], in_=AM[:, sl, hp, 1, 0:128]),
                              reads=[B_AM[sl][hp]], writes=[B_NCH[sl][0]])
                        fw.op(dve, lambda hp=hp, sl=sl: V.tensor_tensor(out=NCH[:, sl, 0, hp, 1, :], in0=PS[0][:, 256:384], in1=consts[:, K_MT:K_MT + 128], op=ALU.mult),
                              reads=[B_PS[0], B_consts], writes=[B_NCH[sl][0]])
                    XR = slice(sl * 128, (sl + 1) * 128)
                    fw.op(pe, lambda q=q, j=j, XR=XR: T.matmul(PS[3][:, XR], lhsT=OPS[:, q, j, 0, :], rhs=SBD[:, q, :], start=True, stop=False),
                          reads=[ob_, B_SBD[q]], writes=[B_PS[3]])
                    for hp in range(2):
                        HC = slice(sl * 128 + hp * 64, sl * 128 + (hp + 1) * 64)
                        fw.op(pe, lambda hp=hp, HC=HC, q=q, j=j, sl=sl: T.matmul(PS[3][:, HC], lhsT=AM[:, sl, hp, 0, 0:128], rhs=VTOK[:, j, q * 128 + hp * 64:q * 128 + (hp + 1) * 64],
                                                                                 start=False, stop=(hp == 1)), reads=[B_AM[sl][hp], B_VTOK[j]], writes=[B_PS[3]])
                    fw.op(act, lambda sl=sl, XR=XR: acopy(out=XCH[:, sl, 0, :], in_=PS[3][:, XR]), reads=[B_PS[3]], writes=[B_XCH[sl][0]])
                for lvl in range(7):
                    cb, nb2 = lvl % 2, (lvl + 1) % 2
                    for sl in range(4):
                        cbk = sl // 2
                        for hp in range(2):
                            H_ = slice(hp * 64, (hp + 1) * 64)
                            CO = slice((sl % 2) * 128 + hp * 64, (sl % 2) * 128 + (hp + 1) * 64)
                            fw.op(pe, lambda H_=H_, CO=CO, cb=cb, cbk=cbk, sl=sl: T.matmul(PS[cbk][:, CO], lhsT=identb[:], rhs=XCH[:, sl, cb, H_], start=True, stop=False),
                                  reads=[B_identb, B_XCH[sl][cb]], writes=[B_PS[cbk]])
                            fw.op(pe, lambda H_=H_, CO=CO, cb=cb, cbk=cbk, sl=sl, hp=hp: T.matmul(PS[cbk][:, CO], lhsT=NCH[:, sl, cb, hp, 0, :], rhs=XCH[:, sl, cb, H_],
                                                                                                   start=False, stop=True),
                                  reads=[B_NCH[sl][cb], B_XCH[sl][cb]], writes=[B_PS[cbk]])
                        if sl % 2 == 1:
                            for s2 in (sl - 1, sl):
                                CX = slice((s2 % 2) * 128, (s2 % 2) * 128 + 128)
                                if lvl < 6:
                                    fw.op(act, lambda s2=s2, nb2=nb2, cbk=cbk, CX=CX: acopy(out=XCH[:, s2, nb2, :], in_=PS[cbk][:, CX]),
                                          reads=[B_PS[cbk]], writes=[B_XCH[s2][nb2]])
                                else:
                                    fw.op(act, lambda s2=s2, cbk=cbk, CX=CX: acopy(out=Pbf[:, s2, :], in_=PS[cbk][:, CX]), reads=[B_PS[cbk]], writes=[B_Pbf[s2]])
                    if lvl < 6:
                        for sl in range(4):
                            sbk = 4 + sl
                            for hp in range(2):
                                o_ = hp * 256
                                fw.op(pe, lambda cb=cb, hp=hp, sbk=sbk, sl=sl, o_=o_: T.matmul(PS[sbk][:, o_:o_ + 128], lhsT=NCH[:, sl, cb, hp, 1, :], rhs=NCH[:, sl, cb, hp, 0, :],
                                                                                               start=True, stop=True), reads=[B_NCH[sl][cb]], writes=[B_PS[sbk]])
                                fw.op(pe, lambda cb=cb, hp=hp, sbk=sbk, sl=sl, o_=o_: T.matmul(PS[sbk][:, o_ + 128:o_ + 256], lhsT=NCH[:, sl, cb, hp, 0, :], rhs=NCH[:, sl, cb, hp, 1, :],
                                                                                               start=True, stop=True), reads=[B_NCH[sl][cb]], writes=[B_PS[sbk]])
                            if sl % 2 == 0:
                                fw.op(dve, lambda nb2=nb2, sbk=sbk, sl=sl: V.tensor_copy(out=NCH[:, sl, nb2, :, :, :].rearrange("p h a t -> p (h a t)"), in_=PS[sbk][:, 0:512]),
                                      reads=[B_PS[sbk]], writes=[B_NCH[sl][nb2]])
                            else:
                                fw.op(act, lambda nb2=nb2, sbk=sbk, sl=sl: acopy(out=NCH[:, sl, nb2, :, :, :].rearrange("p h a t -> p (h a t)"), in_=PS[sbk][:, 0:512]),
                                      reads=[B_PS[sbk]], writes=[B_NCH[sl][nb2]])
                for q in range(4):
                    sl = q
                    ob_ = B_OPS[q][j]
                    if full:
                        YC = slice(q * 128, (q + 1) * 128)
                        fw.op(pe, lambda q=q, j=j, YC=YC: T.matmul(PS[2][:, YC], lhsT=OPS[:, q, j, 1, :], rhs=SBD[:, q, :], start=True, stop=False),
                              reads=[ob_, B_SBD[q]], writes=[B_PS[2]])
                        for hp in range(2):
                            HC = slice(q * 128 + hp * 64, q * 128 + (hp + 1) * 64)
                            H_ = slice(hp * 64, (hp + 1) * 64)
                            fw.op(pe, lambda hp=hp, HC=HC, H_=H_, sl=sl: T.matmul(PS[2][:, HC], lhsT=AM[:, sl, hp, 1, 128:256], rhs=Pbf[:, sl, H_], start=False, stop=False),
                                  reads=[B_AM[sl][hp], B_Pbf[sl]], writes=[B_PS[2]])
                            fw.op(pe, lambda hp=hp, HC=HC, j=j, sl=sl: T.matmul(PS[2][:, HC], lhsT=AM[:, sl, hp, 0, 128:256], rhs=VTOK[:, j, HC], start=False, stop=(hp == 1)),
                                  reads=[B_AM[sl][hp], B_VTOK[j]], writes=[B_PS[2]])
                    tkb = 3
                    for s_ in range(2):
                        fw.op(pe, lambda s_=s_, q=q, j=j: T.transpose(out=psb(tkb)[:, s_ * 128:(s_ + 1) * 128], in_=OPS[:, q, j, 4 + s_, :], identity=identb[:]),
                              reads=[ob_, B_identb], writes=[B_PS[tkb]])
                    fw.op(dve, lambda: V.tensor_copy(out=TOK[:].rearrange("p a t -> p (a t)"), in_=psb(tkb)[:, 0:256]), reads=[B_PS[tkb]], writes=[B_TOK])
                    stb = 0 + (q % 2)
                    fw.op(pe, lambda sl=sl, stb=stb: T.matmul(PS[stb][:, 0:128], lhsT=TOK[:, 1, :], rhs=Pbf[:, sl, :], start=True, stop=False),
                          reads=[B_TOK, B_Pbf[sl]], writes=[B_PS[stb]])
                    fw.op(pe, lambda q=q, j=j, stb=stb: T.matmul(PS[stb][:, 0:128], lhsT=TOK[:, 0, :], rhs=VTOK[:, j, q * 128:(q + 1) * 128], start=False, stop=True),
                          reads=[B_TOK, B_VTOK[j]], writes=[B_PS[stb]])
                    gl = GG[:, q, j * 128 + 127:j * 128 + 128]
                    for hp in range(2):
                        H_ = slice(hp * 64, (hp + 1) * 64)
                        fw.op(dve, lambda q=q, H_=H_, gl=gl, stb=stb: V.scalar_tensor_tensor(out=S32[H_, q, H_], in0=S32[H_, q, H_], scalar=gl[H_, :],
                                                                                             in1=PS[stb][H_, H_.start:H_.stop], op0=ALU.mult, op1=ALU.add),
                              reads=[B_S32[q], B_PS[stb], B_GG[q]], writes=[B_S32[q]])
                    fw.op(act, lambda q=q: acopy(out=SBD[:, q, :], in_=S32[:, q, :]), reads=[B_S32[q]], writes=[B_SBD[q]])
                if not full:
                    continue
                Y3 = PS[2][:, :].rearrange("p (h i) -> p h i", h=8)
                sm, sv, sr = stat[:, 16:24], stat[:, 24:32], stat[:, 32:40]
                fw.op(dve, lambda: V.tensor_reduce(out=sm, in_=Y3, op=ALU.add, axis=AX.X), reads=[RG_["Y"]], writes=[B_stat])
                fw.op(dve, lambda: V.tensor_scalar(out=sm, in0=sm, scalar1=1.0 / 64, scalar2=None, op0=ALU.mult), reads=[B_stat], writes=[B_stat])
                fw.op(dve, lambda: V.tensor_tensor(out=YN[:].rearrange("p (h i) -> p h i", h=8), in0=Y3, in1=sm.unsqueeze(2).to_broadcast([128, 8, 64]),
                                                   op=ALU.subtract), reads=[RG_["Y"], B_stat], writes=[B_YN])
                fw.op(pool, lambda: G.tensor_tensor(out=beta[0][:, 0:512], in0=YN[:], in1=YN[:], op=ALU.mult), reads=[B_YN], writes=[B_beta[0]])
                fw.op(dve, lambda: V.tensor_reduce(out=sv, in_=beta[0][:, 0:512].rearrange("p (h i) -> p h i", h=8), op=ALU.add, axis=AX.X),
                      reads=[B_beta[0]], writes=[B_stat])
                fw.op(act, lambda: S.activation(out=sv, in_=sv, func=AF.Sqrt, scale=1.0 / 64, bias=LNX_EPS), reads=[B_stat], writes=[B_stat])
                fw.op(dve, lambda: V.reciprocal(out=sr, in_=sv), reads=[B_stat], writes=[B_stat])
                fw.op(dve, lambda: V.tensor_tensor(out=YNB[:].rearrange("p (h i) -> p h i", h=8), in0=YN[:].rearrange("p (h i) -> p h i", h=8),
                                                   in1=sr.unsqueeze(2).to_broadcast([128, 8, 64]), op=ALU.mult), reads=[B_YN, B_stat], writes=[B_YNB])
                for q in range(4):
                    fw.op(pe, lambda q=q: T.transpose(out=psb(0)[:, q * 128:(q + 1) * 128], in_=YNB[:, q * 128:(q + 1) * 128], identity=identb[:]),
                          reads=[B_YNB, B_identb], writes=[B_PS[0]])
                J_ = slice(j * 128, (j + 1) * 128)
                for q in range(4):
                    fw.op(dve, lambda q=q: V.tensor_scalar(out=tmpf[q][:, 0:128], in0=psb(0)[:, q * 128:(q + 1) * 128], scalar1=col(C_LW + q), scalar2=col(C_LB + q),
                                                           op0=ALU.mult, op1=ALU.add), reads=[B_PS[0], B_cols], writes=[B_tmpf[q]])
                for q in range(4):
                    fw.op(pool, lambda q=q, J_=J_: G.tensor_tensor(out=tmpf[q][:, 128:256], in0=tmpf[q][:, 0:128], in1=BONT[:, q, J_], op=ALU.add),
                          reads=[B_tmpf[q], B_BONT[q]], writes=[B_tmpf[q]])
                for q in range(4):
                    fw.op(pool, lambda q=q, J_=J_: G.tensor_tensor(out=mixT[:, 4 + q, J_], in0=tmpf[q][:, 128:256], in1=GT[:, q, J_], op=ALU.mult),
                          reads=[B_tmpf[q], B_GT[q]], writes=[B_mixT[4 + q]])

        def wout(hb, nt, strm):
            acc = [4, 5, 6, 7]
            slot = None
            for c in range(8):
                if c % 3 == 0:
                    if slot is not None:
                        strm.after_use()
                    slot = strm.get()
                off = (c % 3) * 1024
                for j in range(nt):
                    for half in range(2):
                        a = acc[2 * j + half]
                        fw.op(pe, lambda c=c, j=j, half=half, a=a, slot=slot, off=off: T.matmul(
                            PS[a][:, :], lhsT=mixT[:, c, j * 128:(j + 1) * 128], rhs=ring[slot][:, off + half * 512:off + (half + 1) * 512],
                            start=(c == 0), stop=(c == 7)), reads=[B_mixT[c], B_ring[slot]], writes=[B_PS[a]])
            strm.after_use()
            for j in range(nt):
                for half in range(2):
                    a = acc[2 * j + half]
                    fw.op(dve, lambda j=j, half=half, a=a: V.tensor_tensor(
                        out=hbuf[hb][:, j, half * 512:(half + 1) * 512], in0=PS[a][:, :], in1=hbuf[hb][:, j, half * 512:(half + 1) * 512], op=ALU.add),
                        reads=[B_PS[a], B_h[hb][j]], writes=[B_h[hb][j]])

        def final(hb, tiles, nt):
            for j, gi in enumerate(tiles):
                k = j % 2
                sc = stat[:, 40 + 2 * j:41 + 2 * j]
                sc2 = stat[:, 41 + 2 * j:42 + 2 * j]
                fw.op(act, lambda j=j, sc=sc, k=k: S.activation(out=ybuf[k][:], in_=hbuf[hb][:, j, :], func=AF.Square, accum_out=sc),
                      reads=[B_h[hb][j]], writes=[B_beta[0], B_beta[1], B_stat])
                fw.op(act, lambda sc=sc: S.activation(out=sc, in_=sc, func=AF.Sqrt, scale=1.0 / D, bias=RMS_EPS), reads=[B_stat], writes=[B_stat])
                fw.op(dve, lambda sc=sc, sc2=sc2: V.reciprocal(out=sc2, in_=sc), reads=[B_stat], writes=[B_stat])
                fw.op(dve, lambda j=j, k=k, sc2=sc2: V.scalar_tensor_tensor(out=ybuf[k][:], in0=hbuf[hb][:, j, :], scalar=sc2, in1=gfin[:],
                                                                            op0=ALU.mult, op1=ALU.mult), reads=[B_h[hb][j], B_stat, B_gfin], writes=[B_beta[0], B_beta[1]])
                fw.dma(sp, d_out[k], out_d[(gi - 1) * 128:gi * 128, :], ybuf[k][:], reads=[B_beta[0], B_beta[1]])

        sts = [[0]] + [[i, i + 1] for i in range(1, ntiles, 2) if i + 1 < ntiles]
        if ntiles % 2 == 0:
            sts.append([ntiles - 1])
        load_h(0, sts[0], 0)
        for sti, tiles in enumerate(sts):
            hb = sti % 2
            nt = len(tiles)
            full = sti > 0
            order = list(range(P_FFN[1], P_FFN[1] + NM)) + list(range(P_IN, P_IN + 9))
            if full:
                order += list(range(P_OUT, P_OUT + 3)) + list(range(P_FFN[2], P_FFN[2] + NM))
            strm = Stream(order)
            strm.ensure(NSLOT - 1)
            norm_T(hb, nt)
            ffn(1, hb, nt, strm)
            norm_T(hb, nt)
            proj(tiles, nt, strm)
            if full:
                attention(tiles, nt)
            rwt_bufs = [B_A[0], B_SG[0], B_CS[0], B_KKT[0], B_KTL[0], B_GI[0], B_GP[0]] + B_GG
            fw.handoff(B_actT, rwt_bufs)
            rwkv(tiles, nt, full)
            fw.handoff(rwt_bufs, B_actT)
            if full and debug:
                for c in range(8):
                    fw.dma(sp, d_dbg, dbg_d[c, :, tiles[0] * 128:tiles[0] * 128 + nt * 128], mixT[:, c, 0:nt * 128], reads=[B_mixT[c]])
            if full:
                wout(hb, nt, strm)
                norm_T(hb, nt)
                ffn(2, hb, nt, strm)
                final(hb, tiles, nt)
            if sti + 1 < len(sts):
                load_h(sti + 1, sts[sti + 1], 0)
        if debug:
            sp.prog.append(lambda: sp.eng.wait_ge(d_dbg.sem, d_dbg.count))
        for k in range(2):
            sp.prog.append(lambda k=k: sp.eng.wait_ge(d_out[k].sem, d_out[k].count))
        fw.run(block)
    return nc


_NC_CACHE = {}


def kernel(**inp):
    inp = {k: np.asarray(v) for k, v in inp.items()}
    x = inp["x"].astype(np.float32, copy=False)
    nb_ = x.shape[0]
    if "nc" not in _NC_CACHE:
        _NC_CACHE["nc"] = build()
    nc = _NC_CACHE["nc"]
    wp = host_pieces(inp)
    cols = host_cols(inp)
    consts = host_consts()
    gfin = np.ascontiguousarray(np.broadcast_to(inp["final_norm"].reshape(1, D), (128, D))).astype(np.float32)
    lora = np.zeros((96, 3, 512), np.float32)
    lora[0:32, 0] = inp["rwkv_w_up"][0]
    lora[0:32, 1] = inp["rwkv_a_up"][0]
    lora[0:96, 2] = inp["rwkv_g_up"][0]
    meta = np.ascontiguousarray(inp["meta_tokens"]).astype(np.float32)
    in_maps = [{"x": np.ascontiguousarray(x[b]), "meta": meta, "wp": wp, "cols": cols, "consts": consts, "gfin": gfin, "lora": lora}
               for b in range(nb_)]
    res = run_bass_kernel_spmd(nc, in_maps, core_ids=list(range(nb_)))
    return np.stack([np.asarray(r["out"]) for r in res.results], axis=0).astype(np.float32)
```

```python
import numpy as np
from contextlib import ExitStack
import concourse.bass as bass
import concourse.mybir as mybir
from concourse.bass_utils import run_bass_kernel_spmd
from concourse.ap import AP

F32 = mybir.dt.float32
BF16 = mybir.dt.bfloat16
AF = mybir.ActivationFunctionType
ALU = mybir.AluOpType
AX = mybir.AxisListType

D = 1024
DFF = 2816
NM = DFF // 128
SEQ = 4096
NTILES_FULL = 33
PE_ELEMS = 3072
NSLOT = 4
INV_DT = F32
DECAY_C = float(np.exp(-0.5))
LNX_EPS = 64e-5
RMS_EPS = 1e-6

C_MU_R, C_MU_K, C_MU_V = 0, 4, 8
C_MU_XW, C_MU_XA, C_MU_XG = 12, 13, 14
C_W0, C_A0, C_KK, C_KA, C_RK, C_LW, C_LB = 15, 19, 23, 27, 31, 35, 39
C_G1, C_GM, C_G2 = 43, 51, 59
NCOLS = 67
K_ID, K_BD, K_M1, K_M2, K_MT, K_MP, K_NMT, K_NMP = 0, 128, 256, 384, 512, 640, 768, 896
NCONST = 1024


class Buf:
    def __init__(self, name):
        self.name = name
        self.w = None
        self.r = {}


class E:
    def __init__(self, name, eng, sem):
        self.name, self.eng, self.sem = name, eng, sem
        self.count = 0
        self.waited = {}
        self.prog = []
        self.inc = [False]
        self.prefix = None

    def resolve(self, c):
        return self.prefix[c]


class DSem:
    def __init__(self, name, sem):
        self.name, self.sem = name, sem
        self.count = 0

    def resolve(self, c):
        return c


class FW:
    def __init__(self, nc, sems):
        self.nc = nc
        self.pe = E("pe", nc.tensor, sems["pe"])
        self.act = E("act", nc.scalar, sems["act"])
        self.dve = E("dve", nc.vector, sems["dve"])
        self.pool = E("pool", nc.gpsimd, sems["pool"])
        self.sp = E("sp", nc.sync, sems["sp"])
        self.nops = 0

    def _deps(self, reads, writes):
        deps = []
        for b in reads:
            if b.w is not None:
                deps.append(b.w)
        for b in writes:
            if b.w is not None:
                deps.append(b.w)
            deps.extend(b.r.values())
        return deps

    def _wait(self, e, deps):
        best = {}
        for (src, c) in deps:
            if src is e:
                if e.name == "pe":
                    continue
            if e.waited.get(src.name, 0) >= c:
                continue
            if best.get(src.name, (None, 0))[1] < c:
                best[src.name] = (src, c)
        for nm, (src, c) in best.items():
            if isinstance(src, E):
                src.inc[c] = True
            e.prog.append(lambda src=src, c=c, e=e: e.eng.wait_ge(src.sem, src.resolve(c)))
            e.waited[nm] = c

    def _mark(self, me, reads, writes):
        for b in writes:
            b.w = me
            b.r = {}
        for b in reads:
            if b.w is not None and b.w == me:
                continue
            old = b.r.get(me[0].name)
            if old is None or old[1] < me[1]:
                b.r[me[0].name] = me

    def op(self, e, fn, reads=(), writes=()):
        self._wait(e, self._deps(reads, writes))
        e.count += 1
        e.inc.append(False)
        idx = e.count

        def emit(fn=fn, e=e, idx=idx):
            inst = fn()
            if e.inc[idx]:
                inst.then_inc(e.sem, 1)
        e.prog.append(emit)
        self._mark((e, idx), reads, writes)
        self.nops += 1

    def dma(self, q, dsem, out, in_, reads=(), writes=()):
        self._wait(q, self._deps(reads, writes))
        q.prog.append(lambda q=q, out=out, in_=in_, dsem=dsem: q.eng.dma_start(out=out, in_=in_).then_inc(dsem.sem, 16))
        dsem.count += 16
        self._mark((dsem, dsem.count), reads, writes)

    def handoff(self, srcs, dsts):
        for d in dsts:
            for s in srcs:
                if s.w is not None:
                    o = d.r.get(s.w[0].name)
                    if o is None or o[1] < s.w[1]:
                        d.r[s.w[0].name] = s.w
                for nm, v in s.r.items():
                    o = d.r.get(nm)
                    if o is None or o[1] < v[1]:
                        d.r[nm] = v

    def run(self, block):
        for e in (self.pe, self.act, self.dve, self.pool, self.sp):
            acc, pre = 0, [0]
            for f in e.inc[1:]:
                acc += 1 if f else 0
                pre.append(acc)
            e.prefix = pre

        def mk(e):
            def body(eng):
                for f in e.prog:
                    f()
            return body
        block.tensor(mk(self.pe))
        block.scalar(mk(self.act))
        block.vector(mk(self.dve))
        block.gpsimd(mk(self.pool))
        block.sync(mk(self.sp))


def rev(ap_, n):
    pat = [list(x) for x in ap_.ap]
    step = pat[-1][0]
    pat[-1] = [-step, n]
    return AP(ap_.tensor, ap_.offset + (n - 1) * step, pat)


def piece_table():
    pieces = []
    for f in (1, 2):
        for m in range(NM):
            pieces.append([("g", f, m), ("u", f, m), ("d", f, m)])
    chunks = [("in", c) for c in range(27)]
    for i in range(0, 27, 3):
        pieces.append(chunks[i:i + 3])
    oc = [("o", c) for c in range(8)]
    for i in range(0, 8, 3):
        pieces.append(oc[i:i + 3])
    return pieces


PIECES = piece_table()
NPIECES = len(PIECES)
P_FFN = {1: 0, 2: NM}
P_IN = 2 * NM
P_OUT = 2 * NM + 9

IN_CHUNKS = []
for i in range(4):
    IN_CHUNKS.append((i * 128, 128))
for i in range(4):
    IN_CHUNKS.append((512 + i * 128, 128))
for i in range(4):
    IN_CHUNKS.append((1536 + i * 128, 128))
for i in range(4):
    IN_CHUNKS.append((2048 + i * 128, 128))
for i in range(4):
    IN_CHUNKS.append((2560 + i * 128, 128))
IN_CHUNKS.append((3072, 32))
IN_CHUNKS.append((3104, 32))
IN_CHUNKS.append((3136, 96))
for i in range(4):
    IN_CHUNKS.append((1024 + i * 128, 128))


def host_pieces(inp):
    wp = np.zeros((NPIECES, 128, PE_ELEMS), np.float32)

    def kc(wcols):
        c = wcols.shape[1]
        t = np.zeros((8, 128, 128), np.float32)
        t[:, :, :c] = wcols.reshape(8, 128, c)
        return t.transpose(1, 0, 2).reshape(128, 1024)

    for pi, piece in enumerate(PIECES):
        for si, sub in enumerate(piece):
            kind = sub[0]
            if kind == "g":
                blk = kc(inp[f"ffn{sub[1]}_w_gate"][0][:, sub[2] * 128:(sub[2] + 1) * 128])
            elif kind == "u":
                blk = kc(inp[f"ffn{sub[1]}_w_up"][0][:, sub[2] * 128:(sub[2] + 1) * 128])
            elif kind == "d":
                blk = inp[f"ffn{sub[1]}_w_down"][0][sub[2] * 128:(sub[2] + 1) * 128, :]
            elif kind == "in":
                s, w = IN_CHUNKS[sub[1]]
                blk = kc(inp["w_in"][0][:, s:s + w])
            else:
                blk = inp["w_out"][0][sub[1] * 128:(sub[1] + 1) * 128, :]
            wp[pi, :, si * 1024:(si + 1) * 1024] = blk
    return wp


def host_cols(inp):
    cols = np.zeros((128, NCOLS), np.float32)
    mu = inp["rwkv_mu"][0]

    def c4(v):
        return np.asarray(v).reshape(4, 128).T

    cols[:, C_MU_R:C_MU_R + 4] = c4(mu[0:512])
    cols[:, C_MU_K:C_MU_K + 4] = c4(mu[512:1024])
    cols[:, C_MU_V:C_MU_V + 4] = c4(mu[1024:1536])
    cols[0:32, C_MU_XW] = mu[1536:1568]
    cols[0:32, C_MU_XA] = mu[1568:1600]
    cols[0:96, C_MU_XG] = mu[1600:1696]
    cols[:, C_W0:C_W0 + 4] = c4(inp["rwkv_w0"][0])
    cols[:, C_A0:C_A0 + 4] = c4(inp["rwkv_a0"][0])
    cols[:, C_KK:C_KK + 4] = c4(inp["rwkv_k_k"][0])
    cols[:, C_KA:C_KA + 4] = c4(inp["rwkv_k_a"][0])
    cols[:, C_RK:C_RK + 4] = c4(inp["rwkv_r_k"][0].reshape(512))
    cols[:, C_LW:C_LW + 4] = c4(inp["rwkv_lnx_w"][0])
    cols[:, C_LB:C_LB + 4] = c4(inp["rwkv_lnx_b"][0])
    cols[:, C_G1:C_G1 + 8] = inp["ffn1_norm"][0].reshape(8, 128).T
    cols[:, C_GM:C_GM + 8] = inp["mix_norm"][0].reshape(8, 128).T
    cols[:, C_G2:C_G2 + 8] = inp["ffn2_norm"][0].reshape(8, 128).T
    return cols


def host_consts():
    k = np.zeros((128, NCONST), np.float32)
    i = np.arange(128)
    k[:, K_ID:K_ID + 128] = np.eye(128)
    k[:, K_BD:K_BD + 128] = (i[:, None] // 64 == i[None, :] // 64)
    k[:, K_M1:K_M1 + 128] = (i[:, None] < i[None, :])
    k[:, K_M2:K_M2 + 128] = (i[:, None] <= i[None, :])
    k[:, K_MT:K_MT + 128] = (i[None, :] < i[:, None])
    k[:, K_MP:K_MP + 128] = (i[None, :] >= 112) * np.ones((128, 1))
    k[:, K_NMT:K_NMT + 128] = 1.0 - k[:, K_MT:K_MT + 128]
    k[:, K_NMP:K_NMP + 128] = 1.0 - k[:, K_MP:K_MP + 128]
    return k


def build(ntiles=NTILES_FULL, debug=False):
    nc = bass.Bass("TRN2", target_bir_lowering=False)
    LP = ntiles * 128
    x_d = nc.dram_tensor("x", [SEQ, D], F32, kind="ExternalInput").ap()
    meta_d = nc.dram_tensor("meta", [16, D], F32, kind="ExternalInput").ap()
    wp_d = nc.dram_tensor("wp", [NPIECES, 128, PE_ELEMS], F32, kind="ExternalInput").ap()
    cols_d = nc.dram_tensor("cols", [128, NCOLS], F32, kind="ExternalInput").ap()
    consts_d = nc.dram_tensor("consts", [128, NCONST], F32, kind="ExternalInput").ap()
    gfin_d = nc.dram_tensor("gfin", [128, D], F32, kind="ExternalInput").ap()
    lora_d = nc.dram_tensor("lora", [96, 3, 512], F32, kind="ExternalInput").ap()
    out_d = nc.dram_tensor("out", [SEQ, D], F32, kind="ExternalOutput").ap()
    wbf_d = nc.dram_tensor("wbf", [NPIECES, 128, PE_ELEMS], BF16, kind="Internal").ap()
    if debug:
        dbg_d = nc.dram_tensor("dbg", [8, 128, NTILES_FULL * 128], BF16, kind="ExternalOutput").ap()

    with ExitStack() as st:
        def sb(name, shape, dt):
            return st.enter_context(nc.sbuf_tensor("s_" + name, shape, dt))

        def ps(name, shape, dt):
            return st.enter_context(nc.psum_tensor("p_" + name, shape, dt))

        sems = {k: st.enter_context(nc.semaphore("m_" + k)) for k in ["pe", "act", "dve", "pool", "sp"]}

        def dsem(name):
            return DSem(name, st.enter_context(nc.semaphore(name)))

        fw = FW(nc, sems)
        pe, act, dve, pool, sp = fw.pe, fw.act, fw.dve, fw.pool, fw.sp
        V, S, G, T = nc.vector, nc.scalar, nc.gpsimd, nc.tensor

        def acopy(out, in_, scale=1.0):
            return S.activation(out=out, in_=in_, func=AF.Copy, scale=scale)

        cols = sb("cols", [128, NCOLS], F32)
        consts = sb("consts", [128, NCONST], F32)
        gfin = sb("gfin", [128, D], F32)
        lora = sb("lora", [96, 3, 512], BF16)
        identb = sb("identb", [128, 128], BF16)
        bdones = sb("bdones", [128, 128], BF16)
        ones_f = sb("ones_f", [128, 128], F32)
        zeros_f = sb("zeros_f", [128, 512], BF16)
        KT = sb("KT", [128, 4, NTILES_FULL * 128], BF16)
        KTf = KT[:].rearrange("p a b -> p (a b)").bitcast(F32)
        VS = sb("VS", [128, NTILES_FULL, 512], BF16)
        VSf = VS[:].rearrange("p a b -> p (a b)").bitcast(F32)
        ring = [sb(f"ring{i}", [128, PE_ELEMS], BF16) for i in range(NSLOT)]
        stage = [VSf[:, i * PE_ELEMS:(i + 1) * PE_ELEMS] for i in range(2)] + [KTf[:, i * PE_ELEMS:(i + 1) * PE_ELEMS] for i in range(2)]

        class QV:
            def __init__(self, ap_):
                self.ap_ = ap_

            def __getitem__(self, idx):
                p_, q_, c_ = idx
                return self.ap_[p_, c_]
        B_cols, B_consts, B_gfin, B_lora = Buf("cols"), Buf("consts"), Buf("gfin"), Buf("lora")
        B_identb, B_bdones, B_ones, B_zeros = Buf("identb"), Buf("bdones"), Buf("ones"), Buf("zeros")
        B_KT, B_VS = Buf("KT"), Buf("VS")
        B_ring = [Buf(f"ring{i}") for i in range(NSLOT)]
        B_stage = [Buf(f"stage{i}") for i in range(4)]
        d_ring = [dsem(f"dring{i}") for i in range(NSLOT)]
        d_stage = [dsem(f"dstage{i}") for i in range(4)]
        d_cst = [dsem(f"dcst{i}") for i in range(4)]
        d_cvt = [dsem(f"dcvt{i}") for i in range(NSLOT)]
        d_x = [dsem("dx0"), dsem("dx1")]
        d_out = [dsem("dout0"), dsem("dout1")]
        d_dbg = dsem("ddbg")

        NT = 2
        TT = NT * 128
        hbuf = [sb("h0", [128, NT, D], F32)] * 2
        B_h = [[Buf(f"h_{j}") for j in range(NT)]] * 2
        nb = [sb("nb0", [128, D], BF16)] * 2
        B_nb = [Buf("nb0")] * 2
        nT = sb("nT", [128, 8, TT], BF16)
        B_nT = Buf("nT")
        actT = sb("actT", [128, NM, TT], BF16)
        B_actT = [Buf(f"actT{m}") for m in range(NM)]
        RWT = actT[:].rearrange("p m t -> p (m t)").bitcast(F32)
        sig_t = [sb("sig0", [128, TT], BF16)] * 2
        B_sig = [Buf("sig0")] * 2
        stat = sb("stat", [128, 64], F32)
        B_stat = Buf("stat")
        QT = sb("QT", [128, 4, TT], BF16)
        B_QT = Buf("QT")
        mixT = sb("mixT", [128, 8, TT], BF16)
        B_mixT = [Buf(f"mixT{c}") for c in range(8)]
        carry = sb("carry", [128, 15], F32)
        B_carry = Buf("carry")
        raw = [sb("raw0", [128, TT + 1], F32)] * 2
        B_raw = [Buf("raw0")] * 2
        tmpf = [sb(f"tmpf{i}", [128, TT], F32) for i in range(4)]
        B_tmpf = [Buf(f"tmpf{i}") for i in range(4)]
        MIX = sb("MIX", [128, 4, TT], F32)
        MIXR = sb("MIXR", [128, 4, TT], BF16)
        MIXV = sb("MIXV", [128, 4, TT], BF16)
        B_MIX = [Buf(f"MIX{i}") for i in range(12)]
        xw_t = sb("xw_t", [32, TT], BF16)
        xa_t = sb("xa_t", [32, TT], BF16)
        xg_t = sb("xg_t", [96, TT], BF16)
        B_xw, B_xa, B_xg = Buf("xw"), Buf("xa"), Buf("xg")
        A_T = QV(RWT[:, 0*256:1*256])
        B_A = [Buf("A")] * 4
        SG = QV(RWT[:, 1*256:2*256])
        B_SG = [Buf("SG")] * 4
        CS = QV(RWT[:, 2*256:3*256])
        B_CS = [Buf("CS")] * 4
        KKT = QV(RWT[:, 3*256:4*256])
        B_KKT = [Buf("KKT")] * 4
        KTL = QV(RWT[:, 4*256:5*256])
        B_KTL = [Buf("KTL")] * 4
        GG = RWT[:, 7*256:7*256+1024].rearrange("p (q t) -> p q t", q=4)
        GI = QV(RWT[:, 5*256:6*256])
        GP = QV(RWT[:, 6*256:7*256])
        B_GG = [Buf(f"GG{i}") for i in range(4)]
        B_GI = [Buf("GI")] * 4
        B_GP = [Buf("GP")] * 4
        UNI = sb("UNI", [128, 7680], F32)
        UNIb = UNI[:].bitcast(BF16)
        OPS = UNIb[:, 0:6144].rearrange("p (q j s t) -> p q j s t", q=4, j=NT, s=6)
        B_OPS = [[Buf(f"OPS{q}_{i}") for i in range(NT)] for q in range(4)]
        TOK = sb("TOK", [128, 2, 128], BF16)
        B_TOK = Buf("TOK")
        VTOK = sb("VTOK", [128, NT, 512], BF16)
        B_VTOK = [Buf(f"VTOK{i}") for i in range(NT)]
        BONT = sb("BONT", [128, 4, TT], BF16)
        B_BONT = [Buf(f"BONT{i}") for i in range(4)]
        GT = sb("GT", [128, 4, TT], BF16)
        B_GT = [Buf(f"GT{i}") for i in range(4)]
        AM = UNIb[:, 6144:10240].rearrange("p (a h b t) -> p a h b t", a=4, h=2, b=2)
        B_AM = [[Buf(f"AM{s_}_{h}") for h in range(2)] for s_ in range(4)]
        NCH = UNIb[:, 10240:14336].rearrange("p (s b h a t) -> p s b h a t", s=4, b=2, h=2, a=2)
        B_NCH = [[Buf(f"NCH{s_}_{b}") for b in range(2)] for s_ in range(4)]
        XCH = UNIb[:, 14336:15360].rearrange("p (s b t) -> p s b t", s=4, b=2)
        B_XCH = [[Buf(f"XCH{s_}_{b}") for b in range(2)] for s_ in range(4)]
        Pbf = sb("Pbf", [128, 4, 128], BF16)
        B_Pbf = [Buf(f"Pbf{i}") for i in range(4)]

        S32 = sb("S32", [128, 4, 128], F32)
        SBD = sb("SBD", [128, 4, 128], BF16)
        B_S32 = [Buf(f"S32_{q}") for q in range(4)]
        B_SBD = [Buf(f"SBD_{q}") for q in range(4)]
        YNB = sb("YNB", [128, 512], BF16)
        B_YNB = Buf("YNB")
        identf = consts[:, K_ID:K_ID + 128]
        bb_t = sb("bb", [128, 1024], F32)
        beta = [bb_t[:, 0:512], bb_t[:, 512:1024]]
        ybuf = [bb_t, bb_t]
        om = [sb(f"om{i}", [128, 512], F32)[:] for i in range(2)]
        RB = [sb(f"RB{i}", [128, 513], F32)[:] for i in range(2)]
        attn = [sb(f"attn{i}", [128, 512], BF16)[:] for i in range(2)]
        attnT = [sb(f"attnT{i}", [128, 4, 128], BF16)[:] for i in range(2)]
        NSETS = 6
        for k_ in range(NSETS - 2):
            base = k_ * 1538
            om.append(UNI[:, base:base + 512])
            RB.append(UNI[:, base + 512:base + 1025])
            attn.append(UNIb[:, 2 * (base + 1026):2 * (base + 1026) + 512])
            attnT.append(UNIb[:, 2 * (base + 1282):2 * (base + 1282) + 512].rearrange("p (b t) -> p b t", b=4))
        B_beta = [Buf(f"beta{i}") for i in range(5)]
        YN = beta[1]
        B_YN = B_beta[1]
        B_om = [Buf(f"om{i}") for i in range(6)]
        B_RB = [Buf(f"RB{i}") for i in range(6)]
        B_attn = [Buf(f"attn{i}") for i in range(6)]
        B_attnT = [Buf(f"attnT{i}") for i in range(6)]

        PS = [ps(f"ps{i}", [128, 512], F32) for i in range(8)]
        B_PS = [Buf(f"ps{i}") for i in range(8)]
        RG_ = {"mm1": B_PS[0], "mm3": B_PS[0], "mm2": B_PS[1], "state": B_PS[1], "Y": B_PS[2], "X0": B_PS[3], "X1": B_PS[3], "tok": B_PS[3],
               "ch0": B_PS[4], "ch1": B_PS[5], "sq0": B_PS[6], "sq1": B_PS[7]}

        def psb(i):
            return PS[i][:].bitcast(BF16)

        block = st.enter_context(nc.Block())

        fw.dma(sp, d_cst[0], cols[:], cols_d[:, :], writes=[B_cols])
        fw.dma(sp, d_cst[1], consts[:], consts_d[:, :], writes=[B_consts])
        fw.dma(sp, d_cst[2], gfin[:], gfin_d[:, :], writes=[B_gfin])
        fw.op(dve, lambda: V.tensor_copy(out=identb[:], in_=consts[:, K_ID:K_ID + 128]), reads=[B_consts], writes=[B_identb])
        fw.op(dve, lambda: V.tensor_copy(out=bdones[:], in_=consts[:, K_BD:K_BD + 128]), reads=[B_consts], writes=[B_bdones])
        for li in range(3):
            fw.dma(sp, d_cst[3], bb_t[0:96, 0:512], lora_d[:, li, :], writes=[B_beta[0]])
            fw.op(dve, lambda li=li: V.tensor_copy(out=lora[:, li, :], in_=bb_t[0:96, 0:512]), reads=[B_beta[0]], writes=[B_lora])
        fw.op(pool, lambda: G.memset(ones_f[:], 1.0), writes=[B_ones])
        fw.op(pool, lambda: G.memset(zeros_f[:], 0.0), writes=[B_zeros])
        fw.op(pool, lambda: G.memset(S32[:], 0.0), writes=B_S32)
        fw.op(pool, lambda: G.memset(SBD[:], 0.0), writes=B_SBD)
        fw.op(pool, lambda: G.memset(carry[:], 0.0), writes=[B_carry])

        cast_engs = [dve, act, pool]
        for pi, piece in enumerate(PIECES):
            sbuf_i = pi % 4
            fw.dma(sp, d_stage[sbuf_i], stage[sbuf_i], wp_d[pi, :, :], writes=[B_stage[sbuf_i]])
            slot = pi % NSLOT
            for si, sub in enumerate(piece):
                kind = sub[0]
                src = stage[sbuf_i][:, si * 1024:(si + 1) * 1024]
                dst = ring[slot][:, si * 1024:(si + 1) * 1024]
                gc = None
                if kind in ("g", "u"):
                    gc = C_G1 if sub[1] == 1 else C_G2
                elif kind == "in":
                    gc = C_GM
                if gc is not None:
                    fw.op(dve, lambda src=src, dst=dst, gc=gc: V.tensor_tensor(
                        out=dst.rearrange("p (k c) -> p k c", k=8), in0=src.rearrange("p (k c) -> p k c", k=8),
                        in1=cols[:, gc:gc + 8].unsqueeze(2).to_broadcast([128, 8, 128]), op=ALU.mult),
                        reads=[B_stage[sbuf_i], B_cols], writes=[B_ring[slot]])
                else:
                    e = act if (pi + si) % 2 == 0 else pool
                    if e is act:
                        fw.op(act, lambda src=src, dst=dst: acopy(out=dst, in_=src), reads=[B_stage[sbuf_i]], writes=[B_ring[slot]])
                    else:
                        fw.op(pool, lambda src=src, dst=dst: G.tensor_copy(out=dst, in_=src), reads=[B_stage[sbuf_i]], writes=[B_ring[slot]])
            ne = len(piece) * 1024
            fw.dma(sp, d_cvt[slot], wbf_d[pi, :, 0:ne], ring[slot][:, 0:ne], reads=[B_ring[slot]])
        for dc in d_cvt:
            sp.prog.append(lambda dc=dc, c=dc.count: sp.eng.wait_ge(dc.sem, c))
            sp.waited[dc.name] = dc.count
        fw.handoff(B_stage[0:2], [B_VS])
        fw.handoff(B_stage[2:4], [B_KT])

        stream_state = {"next": 0, "order": []}

        def prefetch(pi):
            slot = stream_state["next"] % NSLOT
            stream_state["next"] += 1
            ne = len(PIECES[pi]) * 1024
            fw.dma(sp, d_ring[slot], ring[slot][:, 0:ne], wbf_d[pi, :, 0:ne], writes=[B_ring[slot]])
            return slot

        class Stream:
            def __init__(self, order):
                self.order = order
                self.slots = {}
                self.issued = 0
                self.used = 0

            def ensure(self, upto):
                while self.issued < min(upto, len(self.order)):
                    self.slots[self.issued] = prefetch(self.order[self.issued])
                    self.issued += 1

            def get(self):
                i = self.used
                self.ensure(i + 1)
                slot = self.slots[i]
                self.used += 1
                return slot

            def after_use(self):
                self.ensure(self.used + NSLOT - 1)

        def load_h(sti, tiles, hb):
            for j, gi in enumerate(tiles):
                if gi == 0:
                    fw.op(pool, lambda hb=hb, j=j: G.memset(hbuf[hb][:, j, :], 0.0), writes=[B_h[hb][j]])
                    fw.dma(sp, d_x[j], hbuf[hb][112:128, j, :], meta_d[:, :], writes=[B_h[hb][j]])
                else:
                    fw.dma(sp, d_x[j], hbuf[hb][:, j, :], x_d[(gi - 1) * 128:gi * 128, :], writes=[B_h[hb][j]])

        def norm_T(hb, nt):
            for j in range(nt):
                k = j % 2
                sc = stat[:, 2 * j:2 * j + 1]
                sc2 = stat[:, 2 * j + 1:2 * j + 2]
                fw.op(act, lambda j=j, sc=sc: S.activation(out=beta[1][:].bitcast(BF16), in_=hbuf[hb][:, j, :], func=AF.Square, accum_out=sc),
                      reads=[B_h[hb][j]], writes=[B_beta[1], B_stat])
                fw.op(act, lambda sc=sc: S.activation(out=sc, in_=sc, func=AF.Sqrt, scale=1.0 / D, bias=RMS_EPS), reads=[B_stat], writes=[B_stat])
                fw.op(dve, lambda sc=sc, sc2=sc2: V.reciprocal(out=sc2, in_=sc), reads=[B_stat], writes=[B_stat])
                fw.op(act, lambda j=j, k=k, sc2=sc2: S.activation(out=nb[k][:], in_=hbuf[hb][:, j, :], func=AF.Copy, scale=sc2),
                      reads=[B_h[hb][j], B_stat], writes=[B_nb[k]])
                pb = 6 + k
                for c in range(8):
                    fw.op(pe, lambda c=c, k=k, pb=pb: T.transpose(out=psb(pb)[:, c * 128:(c + 1) * 128], in_=nb[k][:, c * 128:(c + 1) * 128],
                                                                   identity=identb[:]),
                          reads=[B_nb[k], B_identb], writes=[B_PS[pb]])
                fw.op(dve, lambda j=j, pb=pb: V.tensor_copy(out=nT[:, :, j * 128:(j + 1) * 128],
                                                            in_=psb(pb).rearrange("p (c t) -> p c t", c=8)),
                      reads=[B_PS[pb]], writes=[B_nT])

        def ffn(f, hb, nt, strm):
            W = nt * 128
            acc = [4, 5, 6, 7]
            pend = []

            def down(m, slot):
                for j in range(nt):
                    for half in range(2):
                        a = acc[2 * j + half]
                        fw.op(pe, lambda m=m, j=j, half=half, a=a, slot=slot: T.matmul(
                            PS[a][:, :], lhsT=actT[:, m, j * 128:(j + 1) * 128], rhs=ring[slot][:, 2048 + half * 512:2048 + (half + 1) * 512],
                            start=(m == 0), stop=(m == NM - 1)), reads=[B_actT[m], B_ring[slot]], writes=[B_PS[a]])

            for m in range(NM):
                slot = strm.get()
                gb, ub = (0, 1) if m % 2 == 0 else (2, 3)
                for k in range(8):
                    fw.op(pe, lambda k=k, slot=slot, gb=gb: T.matmul(PS[gb][:, 0:W], lhsT=ring[slot][:, k * 128:(k + 1) * 128], rhs=nT[:, k, 0:W],
                                                                     start=(k == 0), stop=(k == 7)),
                          reads=[B_ring[slot], B_nT], writes=[B_PS[gb]])
                for k in range(8):
                    fw.op(pe, lambda k=k, slot=slot, ub=ub: T.matmul(PS[ub][:, 0:W], lhsT=ring[slot][:, 1024 + k * 128:1024 + (k + 1) * 128],
                                                                     rhs=nT[:, k, 0:W], start=(k == 0), stop=(k == 7)),
                          reads=[B_ring[slot], B_nT], writes=[B_PS[ub]])
                sg = m % 2
                fw.op(act, lambda gb=gb, sg=sg: S.activation(out=sig_t[sg][:, 0:W], in_=PS[gb][:, 0:W], func=AF.Silu),
                      reads=[B_PS[gb]], writes=[B_sig[sg]])
                fw.op(dve, lambda m=m, ub=ub, sg=sg: V.tensor_tensor(out=actT[:, m, 0:W], in0=PS[ub][:, 0:W], in1=sig_t[sg][:, 0:W], op=ALU.mult),
                      reads=[B_PS[ub], B_sig[sg]], writes=[B_actT[m]])
                pend.append((m, slot))
                if len(pend) > 1:
                    down(*pend.pop(0))
                    strm.after_use()
            while pend:
                down(*pend.pop(0))
                strm.after_use()
            for j in range(nt):
                for half in range(2):
                    a = acc[2 * j + half]
                    fw.op(dve, lambda j=j, half=half, a=a: V.scalar_tensor_tensor(
                        out=hbuf[hb][:, j, half * 512:(half + 1) * 512], in0=PS[a][:, :], scalar=0.5,
                        in1=hbuf[hb][:, j, half * 512:(half + 1) * 512], op0=ALU.mult, op1=ALU.add),
                        reads=[B_PS[a], B_h[hb][j]], writes=[B_h[hb][j]])

        def col(c, n=128):
            return cols[0:n, c:c + 1]

        def proj(tiles, nt, strm):
            W = nt * 128
            t0 = tiles[0] * 128
            slot = None
            for ci in range(27):
                if ci % 3 == 0:
                    if slot is not None:
                        strm.after_use()
                    slot = strm.get()
                off = (ci % 3) * 1024
                pb = ci % 4
                if ci < 23:
                    for k in range(8):
                        fw.op(pe, lambda k=k, slot=slot, off=off, pb=pb: T.matmul(
                            PS[pb][:, 0:W], lhsT=ring[slot][:, off + k * 128:off + (k + 1) * 128], rhs=nT[:, k, 0:W], start=(k == 0), stop=(k == 7)),
                            reads=[B_ring[slot], B_nT], writes=[B_PS[pb]])
                    if ci < 4:
                        fw.op(act, lambda ci=ci, pb=pb: acopy(out=QT[:, ci, 0:W], in_=PS[pb][:, 0:W], scale=0.125), reads=[B_PS[pb]], writes=[B_QT])
                    elif ci < 8:
                        fw.op(act, lambda ci=ci, pb=pb: acopy(out=KT[:, ci - 4, t0:t0 + W], in_=PS[pb][:, 0:W]), reads=[B_PS[pb]], writes=[B_KT])
                    else:
                        ri = ci - 8
                        rb = ri % 2
                        if ri < 4:
                            npart, mu_c, dst, dbuf = 128, C_MU_R + ri, MIXR[:, ri, 0:W], B_MIX[ri]
                        elif ri < 8:
                            npart, mu_c, dst, dbuf = 128, C_MU_R + ri, MIX[:, ri - 4, 0:W], B_MIX[ri]
                        elif ri < 12:
                            npart, mu_c, dst, dbuf = 128, C_MU_R + ri, MIXV[:, ri - 8, 0:W], B_MIX[ri]
                        elif ri == 12:
                            npart, mu_c, dst, dbuf = 32, C_MU_XW, xw_t[:, 0:W], B_xw
                        elif ri == 13:
                            npart, mu_c, dst, dbuf = 32, C_MU_XA, xa_t[:, 0:W], B_xa
                        else:
                            npart, mu_c, dst, dbuf = 96, C_MU_XG, xg_t[:, 0:W], B_xg
                        P_ = slice(0, npart)
                        fw.op(act, lambda rb=rb, pb=pb, P_=P_: acopy(out=raw[rb][P_, 1:W + 1], in_=PS[pb][P_, 0:W]), reads=[B_PS[pb]], writes=[B_raw[rb]])
                        fw.op(pool, lambda rb=rb, ri=ri, P_=P_: G.tensor_copy(out=raw[rb][P_, 0:1], in_=carry[P_, ri:ri + 1]),
                              reads=[B_carry], writes=[B_raw[rb]])
                        fw.op(pool, lambda rb=rb, ri=ri, P_=P_: G.tensor_copy(out=carry[P_, ri:ri + 1], in_=raw[rb][P_, W:W + 1]),
                              reads=[B_raw[rb]], writes=[B_carry])
                        tb = ri % 4
                        fw.op(dve, lambda rb=rb, tb=tb, P_=P_: V.tensor_tensor(out=tmpf[tb][P_, 0:W], in0=raw[rb][P_, 0:W], in1=raw[rb][P_, 1:W + 1],
                                                                               op=ALU.subtract), reads=[B_raw[rb]], writes=[B_tmpf[tb]])
                        fw.op(dve, lambda rb=rb, tb=tb, P_=P_, mu_c=mu_c, dst=dst, npart=npart: V.scalar_tensor_tensor(
                            out=dst, in0=tmpf[tb][P_, 0:W], scalar=col(mu_c, npart), in1=raw[rb][P_, 1:W + 1], op0=ALU.mult, op1=ALU.add),
                            reads=[B_tmpf[tb], B_raw[rb], B_cols], writes=[dbuf])
                else:
                    vi = ci - 23
                    for j in range(nt):
                        pbv = (ci + j) % 4
                        for k in range(8):
                            fw.op(pe, lambda k=k, j=j, slot=slot, off=off, pbv=pbv: T.matmul(
                                PS[pbv][:, 0:128], lhsT=nT[:, k, j * 128:(j + 1) * 128], rhs=ring[slot][:, off + k * 128:off + (k + 1) * 128],
                                start=(k == 0), stop=(k == 7)), reads=[B_ring[slot], B_nT], writes=[B_PS[pbv]])
                        gi = tiles[j]
                        fw.op(act, lambda gi=gi, vi=vi, pbv=pbv: acopy(out=VS[:, gi, vi * 128:(vi + 1) * 128], in_=PS[pbv][:, 0:128]),
                              reads=[B_PS[pbv]], writes=[B_VS])
            strm.after_use()

        def attention(tiles, nt):
            jobs = []
            for j, gi in enumerate(tiles):
                for q in range(4):
                    for hp in range(2):
                        nchunks = (gi + 1 + 3) // 4
                        for c in range(nchunks):
                            jobs.append((j, gi, q, hp, c, nchunks))

            SKEW = NSETS - 2
            uni_rw = [b_ for l_ in B_OPS for b_ in l_] + [b_ for l_ in B_AM for b_ in l_] + [b_ for l_ in B_NCH for b_ in l_] + [b_ for l_ in B_XCH for b_ in l_]
            uni_at = []
            for k_ in range(2, NSETS):
                uni_at += [B_om[k_], B_RB[k_], B_attn[k_], B_attnT[k_]]
            fw.handoff(uni_rw, uni_at)

            def stage_a(idx):
                j, gi, q, hp, c, nchunks = jobs[idx]
                bb = idx % NSETS
                zb = [0, 1, 6, 7][idx % 4]
                R_ = slice(hp * 64, (hp + 1) * 64)
                hi = gi - 4 * c
                lo = max(0, hi - 3)
                w = (hi - lo + 1) * 128
                fw.op(pe, lambda: T.matmul(PS[zb][:, 0:w], lhsT=QT[R_, q, j * 128:(j + 1) * 128], rhs=KT[R_, q, lo * 128:lo * 128 + w], start=True, stop=True),
                      reads=[B_QT, B_KT], writes=[B_PS[zb]])
                fw.op(act, lambda: S.activation(out=om[bb][:, 0:w], in_=PS[zb][:, 0:w], func=AF.Sigmoid, scale=-1.0), reads=[B_PS[zb]], writes=[B_om[bb]])
                masks = []
                if c == 0:
                    masks.append((w - 128, K_NMT))
                if lo == 0:
                    masks.append((0, K_NMP))
                for (o_, mk) in masks:
                    fw.op(dve, lambda o_=o_, mk=mk: V.tensor_tensor(out=om[bb][:, o_:o_ + 128], in0=om[bb][:, o_:o_ + 128],
                                                                    in1=consts[:, mk:mk + 128], op=ALU.max),
                          reads=[B_om[bb], B_consts], writes=[B_om[bb]])
                if c == 0:
                    init = 1.0
                    rd = [B_om[bb], B_zeros]
                else:
                    pr = (idx - 1) % NSETS
                    init = RB[pr][:, 0:1]
                    rd = [B_om[bb], B_zeros, B_RB[pr]]
                fw.op(dve, lambda: V.tensor_tensor_scan(out=rev(RB[bb][:, 0:w], w), data0=rev(om[bb][:, 0:w], w), data1=rev(zeros_f[:, 0:w], w),
                                                        initial=init, op0=ALU.mult, op1=ALU.add), reads=rd, writes=[B_RB[bb]])
                fw.op(pool, lambda: G.tensor_tensor(out=attn[bb][:, 0:w - 1], in0=RB[bb][:, 1:w], in1=RB[bb][:, 0:w - 1], op=ALU.subtract),
                      reads=[B_RB[bb]], writes=[B_attn[bb]])
                if c == 0:
                    fw.op(pool, lambda: G.tensor_scalar(out=attn[bb][:, w - 1:w], in0=RB[bb][:, w - 1:w], scalar1=-1.0, scalar2=1.0, op0=ALU.mult, op1=ALU.add),
                          reads=[B_RB[bb]], writes=[B_attn[bb]])
                else:
                    fw.op(pool, lambda: G.tensor_tensor(out=attn[bb][:, w - 1:w], in0=RB[pr][:, 0:1], in1=RB[bb][:, w - 1:w], op=ALU.subtract),
                          reads=[B_RB[bb], B_RB[pr]], writes=[B_attn[bb]])

            def stage_b(idx):
                j, gi, q, hp, c, nchunks = jobs[idx]
                bb = idx % NSETS
                tb = 2 + (idx % 2)
                ob = 4 + (q % 2)
                R_ = slice(hp * 64, (hp + 1) * 64)
                hi = gi - 4 * c
                lo = max(0, hi - 3)
                nb_ = hi - lo + 1
                w = nb_ * 128
                for b_ in range(nb_):
                    fw.op(pe, lambda b_=b_: T.transpose(out=psb(tb)[:, b_ * 128:(b_ + 1) * 128], in_=attn[bb][:, b_ * 128:(b_ + 1) * 128], identity=identb[:]),
                          reads=[B_attn[bb], B_identb], writes=[B_PS[tb]])
                fw.op(act, lambda: acopy(out=attnT[bb].rearrange("p b t -> p (b t)")[:, 0:w], in_=psb(tb)[:, 0:w]),
                      reads=[B_PS[tb]], writes=[B_attnT[bb]])

            def stage_b2(idx):
                j, gi, q, hp, c, nchunks = jobs[idx]
                bb = idx % NSETS
                ob = 4 + (q % 2)
                R_ = slice(hp * 64, (hp + 1) * 64)
                hi = gi - 4 * c
                lo = max(0, hi - 3)
                nb_ = hi - lo + 1
                for b_ in range(nb_):
                    first = (c == 0) and (b_ == 0)
                    last = (c == nchunks - 1) and (b_ == nb_ - 1)
                    fw.op(pe, lambda b_=b_, first=first, last=last: T.matmul(
                        PS[ob][R_, 0:128], lhsT=VS[:, lo + b_, q * 128 + hp * 64:q * 128 + (hp + 1) * 64], rhs=attnT[bb][:, b_, :],
                        start=first, stop=last), reads=[B_VS, B_attnT[bb]], writes=[B_PS[ob]])
                if hp == 1 and c == nchunks - 1:
                    fw.op(act, lambda: acopy(out=mixT[:, q, j * 128:(j + 1) * 128], in_=PS[ob][:, 0:128]), reads=[B_PS[ob]], writes=[B_mixT[q]])

            for i in range(len(jobs) + SKEW + 1):
                if i < len(jobs):
                    stage_a(i)
                if SKEW <= i < len(jobs) + SKEW:
                    stage_b(i - SKEW)
                if i >= SKEW + 1:
                    stage_b2(i - SKEW - 1)
            fw.handoff(uni_at, uni_rw)


        def rwkv(tiles, nt, full):
            W = nt * 128
            lw_up, la_up, lg_up = lora[0:32, 0, :], lora[0:32, 1, :], lora[0:96, 2, :]
            fw.op(act, lambda: S.activation(out=xw_t[:, 0:W], in_=xw_t[:, 0:W], func=AF.Tanh), reads=[B_xw], writes=[B_xw])
            fw.op(act, lambda: S.activation(out=xg_t[:, 0:W], in_=xg_t[:, 0:W], func=AF.Sigmoid), reads=[B_xg], writes=[B_xg])
            def rr_emit(chains):
                while any(chains):
                    for c_ in chains:
                        if c_:
                            fw.op(*c_.pop(0))

            for q in range(4):
                Q_ = slice(q * 128, (q + 1) * 128)
                kmix = MIX[:, q, 0:W]
                rmix = MIXR[:, q, 0:W]
                c1, c2, c3, c4 = [], [], [], []
                c1.append((pe, lambda Q_=Q_: T.matmul(PS[0][:, 0:W], lhsT=lw_up[:, Q_], rhs=xw_t[:, 0:W], start=True, stop=True),
                           [B_lora, B_xw], [B_PS[0]]))
                c1.append((act, lambda q=q: S.activation(out=SG[:, q, 0:W], in_=PS[0][:, 0:W], func=AF.Sigmoid, bias=col(C_W0 + q)),
                           [B_PS[0], B_cols], [B_SG[q]]))
                for j in range(nt):
                    J_ = slice(j * 128, (j + 1) * 128)
                    c1.append((dve, lambda q=q, J_=J_: V.tensor_tensor_scan(out=CS[:, q, J_], data0=ones_f[:, 0:128], data1=SG[:, q, J_], initial=0.0,
                                                                             op0=ALU.mult, op1=ALU.add), [B_SG[q], B_ones], [B_CS[q]]))
                c1.append((act, lambda q=q: S.activation(out=GG[:, q, 0:W], in_=CS[:, q, 0:W], func=AF.Exp, scale=-DECAY_C), [B_CS[q]], [B_GG[q]]))
                c1.append((act, lambda q=q: S.activation(out=GI[:, q, 0:W], in_=CS[:, q, 0:W], func=AF.Exp, scale=DECAY_C), [B_CS[q]], [B_GI[q]]))
                c1.append((pool, lambda q=q: G.tensor_tensor(out=tmpf[0][:, 0:W], in0=CS[:, q, 0:W], in1=SG[:, q, 0:W], op=ALU.subtract),
                           [B_CS[q], B_SG[q]], [B_tmpf[0]]))
                c1.append((act, lambda q=q: S.activation(out=GP[:, q, 0:W], in_=tmpf[0][:, 0:W], func=AF.Exp, scale=-DECAY_C), [B_tmpf[0]], [B_GP[q]]))
                c2.append((pe, lambda Q_=Q_: T.matmul(PS[1][:, 0:W], lhsT=la_up[:, Q_], rhs=xa_t[:, 0:W], start=True, stop=True),
                           [B_lora, B_xa], [B_PS[1]]))
                c2.append((act, lambda q=q: S.activation(out=A_T[:, q, 0:W], in_=PS[1][:, 0:W], func=AF.Sigmoid, bias=col(C_A0 + q)),
                           [B_PS[1], B_cols], [B_A[q]]))
                c2.append((dve, lambda q=q: V.tensor_scalar(out=tmpf[3][:, 0:W], in0=A_T[:, q, 0:W], scalar1=-1.0, scalar2=col(C_KA + q),
                                                            op0=ALU.add, op1=ALU.mult), [B_A[q], B_cols], [B_tmpf[3]]))
                c2.append((dve, lambda q=q, kmix=kmix: V.scalar_tensor_tensor(out=KTL[:, q, 0:W], in0=tmpf[3][:, 0:W], scalar=1.0, in1=kmix,
                                                                              op0=ALU.add, op1=ALU.mult), [B_tmpf[3], B_MIX[4 + q]], [B_KTL[q]]))
                c2.append((dve, lambda q=q, rmix=rmix: V.scalar_tensor_tensor(out=nT[:, 1, 0:W], in0=rmix, scalar=col(C_RK + q), in1=KTL[:, q, 0:W],
                                                                              op0=ALU.mult, op1=ALU.mult), [B_MIX[q], B_KTL[q], B_cols], [B_nT]))
                c2.append((pe, lambda: T.matmul(PS[3][:, 256:256 + W], lhsT=bdones[:], rhs=nT[:, 1, 0:W], start=True, stop=True),
                           [B_bdones, B_nT], [B_PS[3]]))
                c2.append((dve, lambda q=q: V.tensor_tensor(out=BONT[:, q, 0:W], in0=PS[3][:, 256:256 + W], in1=MIXV[:, q, 0:W], op=ALU.mult),
                           [B_PS[3], B_MIX[8 + q]], [B_BONT[q]]))
                c3.append((pe, lambda Q_=Q_: T.matmul(PS[2][:, 0:W], lhsT=lg_up[:, Q_], rhs=xg_t[:, 0:W], start=True, stop=True),
                           [B_lora, B_xg], [B_PS[2]]))
                c3.append((act, lambda q=q: acopy(out=GT[:, q, 0:W], in_=PS[2][:, 0:W]), [B_PS[2]], [B_GT[q]]))
                c4.append((dve, lambda q=q, kmix=kmix: V.tensor_scalar(out=KKT[:, q, 0:W], in0=kmix, scalar1=col(C_KK + q), scalar2=None, op0=ALU.mult),
                           [B_MIX[4 + q], B_cols], [B_KKT[q]]))
                c4.append((pool, lambda q=q: G.tensor_tensor(out=nT[:, 0, 0:W], in0=KKT[:, q, 0:W], in1=KKT[:, q, 0:W], op=ALU.mult),
                           [B_KKT[q]], [B_nT]))
                c4.append((pe, lambda: T.matmul(PS[3][:, 0:W], lhsT=bdones[:], rhs=nT[:, 0, 0:W], start=True, stop=True),
                           [B_bdones, B_nT], [B_PS[3]]))
                c4.append((act, lambda: S.activation(out=tmpf[1][:, 0:W], in_=PS[3][:, 0:W], func=AF.Sqrt), [B_PS[3]], [B_tmpf[1]]))
                c4.append((dve, lambda: V.tensor_scalar(out=tmpf[1][:, 0:W], in0=tmpf[1][:, 0:W], scalar1=1e-12, scalar2=None, op0=ALU.max),
                           [B_tmpf[1]], [B_tmpf[1]]))
                c4.append((dve, lambda: V.reciprocal(out=tmpf[2][:, 0:W], in_=tmpf[1][:, 0:W]), [B_tmpf[1]], [B_tmpf[2]]))
                c4.append((dve, lambda q=q: V.tensor_tensor(out=KKT[:, q, 0:W], in0=KKT[:, q, 0:W], in1=tmpf[2][:, 0:W], op=ALU.mult),
                           [B_KKT[q], B_tmpf[2]], [B_KKT[q]]))
                rr_emit([c1, c4, c2, c3])
                oc = []
                for j in range(nt):
                    J_ = slice(j * 128, (j + 1) * 128)
                    ob_ = B_OPS[q][j]
                    tq = tmpf[1] if j == 0 else tmpf[2]
                    bq = B_tmpf[1] if j == 0 else B_tmpf[2]
                    c_ = []
                    c_.append((dve, lambda q=q, j=j, J_=J_: V.scalar_tensor_tensor(out=OPS[:, q, j, 0, :], in0=KKT[:, q, J_], scalar=-1.0, in1=GP[:, q, J_],
                                                                                    op0=ALU.mult, op1=ALU.mult), [B_KKT[q], B_GP[q]], [ob_]))
                    c_.append((pool, lambda q=q, j=j, J_=J_: G.tensor_tensor(out=OPS[:, q, j, 1, :], in0=MIXR[:, q, J_], in1=GG[:, q, J_], op=ALU.mult),
                               [B_MIX[q], B_GG[q]], [ob_]))
                    c_.append((pool, lambda q=q, j=j, J_=J_: G.tensor_tensor(out=OPS[:, q, j, 2, :], in0=KTL[:, q, J_], in1=GI[:, q, J_], op=ALU.mult),
                               [B_KTL[q], B_GI[q]], [ob_]))
                    c_.append((pool, lambda q=q, J_=J_, tq=tq: G.tensor_tensor(out=tq[:, 0:128], in0=KKT[:, q, J_], in1=A_T[:, q, J_], op=ALU.mult),
                               [B_KKT[q], B_A[q]], [bq]))
                    c_.append((pool, lambda q=q, j=j, J_=J_, tq=tq: G.tensor_tensor(out=OPS[:, q, j, 3, :], in0=tq[:, 0:128], in1=GI[:, q, J_], op=ALU.mult),
                               [bq, B_GI[q]], [ob_]))
                    gl = GG[:, q, j * 128 + 127:j * 128 + 128]
                    c_.append((dve, lambda q=q, j=j, gl=gl: V.tensor_scalar(out=OPS[:, q, j, 4:6, :], in0=OPS[:, q, j, 2:4, :], scalar1=gl, scalar2=None,
                                                                            op0=ALU.mult), [ob_, B_GG[q]], [ob_]))
                    oc.append(c_)
                rr_emit(oc)
            for j in range(nt):
                for q in range(4):
                    fw.op(pe, lambda q=q, j=j: T.transpose(out=psb(0)[:, q * 128:(q + 1) * 128], in_=MIXV[:, q, j * 128:(j + 1) * 128], identity=identb[:]),
                          reads=[B_MIX[8 + q], B_identb], writes=[B_PS[0]])
                fw.op(act, lambda j=j: acopy(out=VTOK[:, j, :], in_=psb(0)[:, 0:512]), reads=[B_PS[0]], writes=[B_VTOK[j]])
            for j, gi in enumerate(tiles):
                for q in range(4):
                    sl = q
                    ob_ = B_OPS[q][j]
                    for hp in range(2):
                        R_ = slice(hp * 64, (hp + 1) * 64)
                        fw.op(pe, lambda R_=R_, q=q, j=j: T.matmul(PS[0][:, 0:256], lhsT=OPS[R_, q, j, 2, :], rhs=OPS[R_, q, j, 0:2, :].rearrange("p a t -> p (a t)"),
                                                                   start=True, stop=True), reads=[ob_], writes=[B_PS[0]])
                        fw.op(pe, lambda R_=R_, q=q, j=j: T.matmul(PS[1][:, 0:256], lhsT=OPS[R_, q, j, 3, :], rhs=OPS[R_, q, j, 0:2, :].rearrange("p a t -> p (a t)"),
                                                                   start=True, stop=True), reads=[ob_], writes=[B_PS[1]])
                        fw.op(pe, lambda R_=R_, q=q, j=j: T.matmul(PS[0][:, 256:384], lhsT=OPS[R_, q, j, 0, :], rhs=OPS[R_, q, j, 3, :],
                                                                   start=True, stop=True), reads=[ob_], writes=[B_PS[0]])
                        fw.op(dve, lambda hp=hp, sl=sl: V.tensor_tensor(out=AM[:, sl, hp, 0, :], in0=PS[0][:, 0:256], in1=consts[:, K_M1:K_M1 + 256], op=ALU.mult),
                              reads=[B_PS[0], B_consts], writes=[B_AM[sl][hp]])
                        fw.op(dve, lambda hp=hp, sl=sl: V.tensor_tensor(out=AM[:, sl, hp, 1, :], in0=PS[1][:, 0:256], in1=consts[:, K_M1:K_M1 + 256], op=ALU.mult),
                              reads=[B_PS[1], B_consts], writes=[B_AM[sl][hp]])
                        fw.op(act, lambda hp=hp, sl=sl: acopy(out=NCH[:, sl, 0, hp, 0, :], in_=AM[:, sl, hp, 1, 0:128]),
                              reads=[B_AM[sl][hp]], writes=[B_NCH[sl][0]])
                        fw.op(dve, lambda hp=hp, sl=sl: V.tensor_tensor(out=NCH[:, sl, 0, hp, 1, :], in0=PS[0][:, 256:384], in1=consts[:, K_MT:K_MT + 128], op=ALU.mult),
                              reads=[B_PS[0], B_consts], writes=[B_NCH[sl][0]])
                    XR = slice(sl * 128, (sl + 1) * 128)
                    fw.op(pe, lambda q=q, j=j, XR=XR: T.matmul(PS[3][:, XR], lhsT=OPS[:, q, j, 0, :], rhs=SBD[:, q, :], start=True, stop=False),
                          reads=[ob_, B_SBD[q]], writes=[B_PS[3]])
                    for hp in range(2):
                        HC = slice(sl * 128 + hp * 64, sl * 128 + (hp + 1) * 64)
                        fw.op(pe, lambda hp=hp, HC=HC, q=q, j=j, sl=sl: T.matmul(PS[3][:, HC], lhsT=AM[:, sl, hp, 0, 0:128], rhs=VTOK[:, j, q * 128 + hp * 64:q * 128 + (hp + 1) * 64],
                                                                                 start=False, stop=(hp == 1)), reads=[B_AM[sl][hp], B_VTOK[j]], writes=[B_PS[3]])
                    fw.op(act, lambda sl=sl, XR=XR: acopy(out=XCH[:, sl, 0, :], in_=PS[3][:, XR]), reads=[B_PS[3]], writes=[B_XCH[sl][0]])
                for lvl in range(7):
                    cb, nb2 = lvl % 2, (lvl + 1) % 2
                    for sl in range(4):
                        cbk = sl // 2
                        for hp in range(2):
                            H_ = slice(hp * 64, (hp + 1) * 64)
                            CO = slice((sl % 2) * 128 + hp * 64, (sl % 2) * 128 + (hp + 1) * 64)
                            fw.op(pe, lambda H_=H_, CO=CO, cb=cb, cbk=cbk, sl=sl: T.matmul(PS[cbk][:, CO], lhsT=identb[:], rhs=XCH[:, sl, cb, H_], start=True, stop=False),
                                  reads=[B_identb, B_XCH[sl][cb]], writes=[B_PS[cbk]])
                            fw.op(pe, lambda H_=H_, CO=CO, cb=cb, cbk=cbk, sl=sl, hp=hp: T.matmul(PS[cbk][:, CO], lhsT=NCH[:, sl, cb, hp, 0, :], rhs=XCH[:, sl, cb, H_],
                                                                                                   start=False, stop=True),
                                  reads=[B_NCH[sl][cb], B_XCH[sl][cb]], writes=[B_PS[cbk]])
                        if sl % 2 == 1:
                            for s2 in (sl - 1, sl):
                                CX = slice((s2 % 2) * 128, (s2 % 2) * 128 + 128)
                                if lvl < 6:
                                    fw.op(act, lambda s2=s2, nb2=nb2, cbk=cbk, CX=CX: acopy(out=XCH[:, s2, nb2, :], in_=PS[cbk][:, CX]),
                                          reads=[B_PS[cbk]], writes=[B_XCH[s2][nb2]])
                                else:
                                    fw.op(act, lambda s2=s2, cbk=cbk, CX=CX: acopy(out=Pbf[:, s2, :], in_=PS[cbk][:, CX]), reads=[B_PS[cbk]], writes=[B_Pbf[s2]])
                    if lvl < 6:
                        for sl in range(4):
                            sbk = 4 + sl
                            for hp in range(2):
                                o_ = hp * 256
                                fw.op(pe, lambda cb=cb, hp=hp, sbk=sbk, sl=sl, o_=o_: T.matmul(PS[sbk][:, o_:o_ + 128], lhsT=NCH[:, sl, cb, hp, 1, :], rhs=NCH[:, sl, cb, hp, 0, :],
                                                                                               start=True, stop=True), reads=[B_NCH[sl][cb]], writes=[B_PS[sbk]])
                                fw.op(pe, lambda cb=cb, hp=hp, sbk=sbk, sl=sl, o_=o_: T.matmul(PS[sbk][:, o_ + 128:o_ + 256], lhsT=NCH[:, sl, cb, hp, 0, :], rhs=NCH[:, sl, cb, hp, 1, :],
                                                                                               start=True, stop=True), reads=[B_NCH[sl][cb]], writes=[B_PS[sbk]])
                            if sl % 2 == 0:
                                fw.op(dve, lambda nb2=nb2, sbk=sbk, sl=sl: V.tensor_copy(out=NCH[:, sl, nb2, :, :, :].rearrange("p h a t -> p (h a t)"), in_=PS[sbk][:, 0:512]),
                                      reads=[B_PS[sbk]], writes=[B_NCH[sl][nb2]])
                            else:
                                fw.op(act, lambda nb2=nb2, sbk=sbk, sl=sl: acopy(out=NCH[:, sl, nb2, :, :, :].rearrange("p h a t -> p (h a t)"), in_=PS[sbk][:, 0:512]),
                                      reads=[B_PS[sbk]], writes=[B_NCH[sl][nb2]])
                for q in range(4):
                    sl = q
                    ob_ = B_OPS[q][j]
                    if full:
                        YC = slice(q * 128, (q + 1) * 128)
                        fw.op(pe, lambda q=q, j=j, YC=YC: T.matmul(PS[2][:, YC], lhsT=OPS[:, q, j, 1, :], rhs=SBD[:, q, :], start=True, stop=False),
                              reads=[ob_, B_SBD[q]], writes=[B_PS[2]])
                        for hp in range(2):
                            HC = slice(q * 128 + hp * 64, q * 128 + (hp + 1) * 64)
                            H_ = slice(hp * 64, (hp + 1) * 64)
                            fw.op(pe, lambda hp=hp, HC=HC, H_=H_, sl=sl: T.matmul(PS[2][:, HC], lhsT=AM[:, sl, hp, 1, 128:256], rhs=Pbf[:, sl, H_], start=False, stop=False),
                                  reads=[B_AM[sl][hp], B_Pbf[sl]], writes=[B_PS[2]])
                            fw.op(pe, lambda hp=hp, HC=HC, j=j, sl=sl: T.matmul(PS[2][:, HC], lhsT=AM[:, sl, hp, 0, 128:256], rhs=VTOK[:, j, HC], start=False, stop=(hp == 1)),
                                  reads=[B_AM[sl][hp], B_VTOK[j]], writes=[B_PS[2]])
                    tkb = 3
                    for s_ in range(2):
                        fw.op(pe, lambda s_=s_, q=q, j=j: T.transpose(out=psb(tkb)[:, s_ * 128:(s_ + 1) * 128], in_=OPS[:, q, j, 4 + s_, :], identity=identb[:]),
                              reads=[ob_, B_identb], writes=[B_PS[tkb]])
                    fw.op(dve, lambda: V.tensor_copy(out=TOK[:].rearrange("p a t -> p (a t)"), in_=psb(tkb)[:, 0:256]), reads=[B_PS[tkb]], writes=[B_TOK])
                    stb = 0 + (q % 2)
                    fw.op(pe, lambda sl=sl, stb=stb: T.matmul(PS[stb][:, 0:128], lhsT=TOK[:, 1, :], rhs=Pbf[:, sl, :], start=True, stop=False),
                          reads=[B_TOK, B_Pbf[sl]], writes=[B_PS[stb]])
                    fw.op(pe, lambda q=q, j=j, stb=stb: T.matmul(PS[stb][:, 0:128], lhsT=TOK[:, 0, :], rhs=VTOK[:, j, q * 128:(q + 1) * 128], start=False, stop=True),
                          reads=[B_TOK, B_VTOK[j]], writes=[B_PS[stb]])
                    gl = GG[:, q, j * 128 + 127:j * 128 + 128]
                    for hp in range(2):
                        H_ = slice(hp * 64, (hp + 1) * 64)
                        fw.op(dve, lambda q=q, H_=H_, gl=gl, stb=stb: V.scalar_tensor_tensor(out=S32[H_, q, H_], in0=S32[H_, q, H_], scalar=gl[H_, :],
                                                                                             in1=PS[stb][H_, H_.start:H_.stop], op0=ALU.mult, op1=ALU.add),
                              reads=[B_S32[q], B_PS[stb], B_GG[q]], writes=[B_S32[q]])
                    fw.op(act, lambda q=q: acopy(out=SBD[:, q, :], in_=S32[:, q, :]), reads=[B_S32[q]], writes=[B_SBD[q]])
                if not full:
                    continue
                Y3 = PS[2][:, :].rearrange("p (h i) -> p h i", h=8)
                sm, sv, sr = stat[:, 16:24], stat[:, 24:32], stat[:, 32:40]
                fw.op(dve, lambda: V.tensor_reduce(out=sm, in_=Y3, op=ALU.add, axis=AX.X), reads=[RG_["Y"]], writes=[B_stat])
                fw.op(dve, lambda: V.tensor_scalar(out=sm, in0=sm, scalar1=1.0 / 64, scalar2=None, op0=ALU.mult), reads=[B_stat], writes=[B_stat])
                fw.op(dve, lambda: V.tensor_tensor(out=YN[:].rearrange("p (h i) -> p h i", h=8), in0=Y3, in1=sm.unsqueeze(2).to_broadcast([128, 8, 64]),
                                                   op=ALU.subtract), reads=[RG_["Y"], B_stat], writes=[B_YN])
                fw.op(pool, lambda: G.tensor_tensor(out=beta[0][:, 0:512], in0=YN[:], in1=YN[:], op=ALU.mult), reads=[B_YN], writes=[B_beta[0]])
                fw.op(dve, lambda: V.tensor_reduce(out=sv, in_=beta[0][:, 0:512].rearrange("p (h i) -> p h i", h=8), op=ALU.add, axis=AX.X),
                      reads=[B_beta[0]], writes=[B_stat])
                fw.op(act, lambda: S.activation(out=sv, in_=sv, func=AF.Sqrt, scale=1.0 / 64, bias=LNX_EPS), reads=[B_stat], writes=[B_stat])
                fw.op(dve, lambda: V.reciprocal(out=sr, in_=sv), reads=[B_stat], writes=[B_stat])
                fw.op(dve, lambda: V.tensor_tensor(out=YNB[:].rearrange("p (h i) -> p h i", h=8), in0=YN[:].rearrange("p (h i) -> p h i", h=8),
                                                   in1=sr.unsqueeze(2).to_broadcast([128, 8, 64]), op=ALU.mult), reads=[B_YN, B_stat], writes=[B_YNB])
                for q in range(4):
                    fw.op(pe, lambda q=q: T.transpose(out=psb(0)[:, q * 128:(q + 1) * 128], in_=YNB[:, q * 128:(q + 1) * 128], identity=identb[:]),
                          reads=[B_YNB, B_identb], writes=[B_PS[0]])
                for q in range(4):
                    J_ = slice(j * 128, (j + 1) * 128)
                    fw.op(dve, lambda q=q: V.tensor_scalar(out=tmpf[2][:, 0:128], in0=psb(0)[:, q * 128:(q + 1) * 128], scalar1=col(C_LW + q), scalar2=col(C_LB + q),
                                                           op0=ALU.mult, op1=ALU.add), reads=[B_PS[0], B_cols], writes=[B_tmpf[2]])
                    fw.op(pool, lambda q=q, J_=J_: G.tensor_tensor(out=tmpf[3][:, 0:128], in0=tmpf[2][:, 0:128], in1=BONT[:, q, J_], op=ALU.add),
                          reads=[B_tmpf[2], B_BONT[q]], writes=[B_tmpf[3]])
                    fw.op(pool, lambda q=q, J_=J_: G.tensor_tensor(out=mixT[:, 4 + q, J_], in0=tmpf[3][:, 0:128], in1=GT[:, q, J_], op=ALU.mult),
                          reads=[B_tmpf[3], B_GT[q]], writes=[B_mixT[4 + q]])

        def wout(hb, nt, strm):
            acc = [4, 5, 6, 7]
            slot = None
            for c in range(8):
                if c % 3 == 0:
                    if slot is not None:
                        strm.after_use()
                    slot = strm.get()
                off = (c % 3) * 1024
                for j in range(nt):
                    for half in range(2):
                        a = acc[2 * j + half]
                        fw.op(pe, lambda c=c, j=j, half=half, a=a, slot=slot, off=off: T.matmul(
                            PS[a][:, :], lhsT=mixT[:, c, j * 128:(j + 1) * 128], rhs=ring[slot][:, off + half * 512:off + (half + 1) * 512],
                            start=(c == 0), stop=(c == 7)), reads=[B_mixT[c], B_ring[slot]], writes=[B_PS[a]])
            strm.after_use()
            for j in range(nt):
                for half in range(2):
                    a = acc[2 * j + half]
                    fw.op(dve, lambda j=j, half=half, a=a: V.tensor_tensor(
                        out=hbuf[hb][:, j, half * 512:(half + 1) * 512], in0=PS[a][:, :], in1=hbuf[hb][:, j, half * 512:(half + 1) * 512], op=ALU.add),
                        reads=[B_PS[a], B_h[hb][j]], writes=[B_h[hb][j]])

        def final(hb, tiles, nt):
            for j, gi in enumerate(tiles):
                k = j % 2
                sc = stat[:, 40 + 2 * j:41 + 2 * j]
                sc2 = stat[:, 41 + 2 * j:42 + 2 * j]
                fw.op(act, lambda j=j, sc=sc, k=k: S.activation(out=ybuf[k][:], in_=hbuf[hb][:, j, :], func=AF.Square, accum_out=sc),
                      reads=[B_h[hb][j]], writes=[B_beta[0], B_beta[1], B_stat])
                fw.op(act, lambda sc=sc: S.activation(out=sc, in_=sc, func=AF.Sqrt, scale=1.0 / D, bias=RMS_EPS), reads=[B_stat], writes=[B_stat])
                fw.op(dve, lambda sc=sc, sc2=sc2: V.reciprocal(out=sc2, in_=sc), reads=[B_stat], writes=[B_stat])
                fw.op(dve, lambda j=j, k=k, sc2=sc2: V.scalar_tensor_tensor(out=ybuf[k][:], in0=hbuf[hb][:, j, :], scalar=sc2, in1=gfin[:],
                                                                            op0=ALU.mult, op1=ALU.mult), reads=[B_h[hb][j], B_stat, B_gfin], writes=[B_beta[0], B_beta[1]])
                fw.dma(sp, d_out[k], out_d[(gi - 1) * 128:gi * 128, :], ybuf[k][:], reads=[B_beta[0], B_beta[1]])

        sts = [[0]] + [[i, i + 1] for i in range(1, ntiles, 2) if i + 1 < ntiles]
        if ntiles % 2 == 0:
            sts.append([ntiles - 1])
        load_h(0, sts[0], 0)
        for sti, tiles in enumerate(sts):
            hb = sti % 2
            nt = len(tiles)
            full = sti > 0
            order = list(range(P_FFN[1], P_FFN[1] + NM)) + list(range(P_IN, P_IN + 9))
            if full:
                order += list(range(P_OUT, P_OUT + 3)) + list(range(P_FFN[2], P_FFN[2] + NM))
            strm = Stream(order)
            strm.ensure(NSLOT - 1)
            norm_T(hb, nt)
            ffn(1, hb, nt, strm)
            norm_T(hb, nt)
            proj(tiles, nt, strm)
            if full:
                attention(tiles, nt)
            rwt_bufs = [B_A[0], B_SG[0], B_CS[0], B_KKT[0], B_KTL[0], B_GI[0], B_GP[0]] + B_GG
            fw.handoff(B_actT, rwt_bufs)
            rwkv(tiles, nt, full)
            fw.handoff(rwt_bufs, B_actT)
            if full and debug:
                for c in range(8):
                    fw.dma(sp, d_dbg, dbg_d[c, :, tiles[0] * 128:tiles[0] * 128 + nt * 128], mixT[:, c, 0:nt * 128], reads=[B_mixT[c]])
            if full:
                wout(hb, nt, strm)
                norm_T(hb, nt)
                ffn(2, hb, nt, strm)
                final(hb, tiles, nt)
            if sti + 1 < len(sts):
                load_h(sti + 1, sts[sti + 1], 0)
        if debug:
            sp.prog.append(lambda: sp.eng.wait_ge(d_dbg.sem, d_dbg.count))
        for k in range(2):
            sp.prog.append(lambda k=k: sp.eng.wait_ge(d_out[k].sem, d_out[k].count))
        fw.run(block)
    return nc


_NC_CACHE = {}


def kernel(**inp):
    inp = {k: np.asarray(v) for k, v in inp.items()}
    x = inp["x"].astype(np.float32, copy=False)
    nb_ = x.shape[0]
    if "nc" not in _NC_CACHE:
        _NC_CACHE["nc"] = build()
    nc = _NC_CACHE["nc"]
    wp = host_pieces(inp)
    cols = host_cols(inp)
    consts = host_consts()
    gfin = np.ascontiguousarray(np.broadcast_to(inp["final_norm"].reshape(1, D), (128, D))).astype(np.float32)
    lora = np.zeros((96, 3, 512), np.float32)
    lora[0:32, 0] = inp["rwkv_w_up"][0]
    lora[0:32, 1] = inp["rwkv_a_up"][0]
    lora[0:96, 2] = inp["rwkv_g_up"][0]
    meta = np.ascontiguousarray(inp["meta_tokens"]).astype(np.float32)
    in_maps = [{"x": np.ascontiguousarray(x[b]), "meta": meta, "wp": wp, "cols": cols, "consts": consts, "gfin": gfin, "lora": lora}
               for b in range(nb_)]
    res = run_bass_kernel_spmd(nc, in_maps, core_ids=list(range(nb_)))
    return np.stack([np.asarray(r["out"]) for r in res.results], axis=0).astype(np.float32)
```
